# Optimizing a Trainium2 kernel written in Bass

```python
import math
import jax, jax.numpy as jnp
from jax import lax
import numpy as np

D_MODEL = 1024
BATCH = 8
SEQ = 4096
DEPTH = 1

CHUNK = 64
HEAD_DIM = 64
MIX_WIDTH = D_MODEL
RWKV_WIDTH = MIX_WIDTH // 2
RET_WIDTH = MIX_WIDTH - RWKV_WIDTH
RWKV_HEADS = RWKV_WIDTH // HEAD_DIM
RET_HEADS = RET_WIDTH // HEAD_DIM
DECAY_LORA = 64
AAA_LORA = 64
GATE_LORA = 128
RWKV_COLS = 3 * RWKV_WIDTH + DECAY_LORA + AAA_LORA + GATE_LORA
RET_COLS = 4 * RET_WIDTH
IN_COLS = RWKV_COLS + RET_COLS
RWKV_SPLITS = (RWKV_WIDTH, 2 * RWKV_WIDTH, 3 * RWKV_WIDTH,
               3 * RWKV_WIDTH + DECAY_LORA, 3 * RWKV_WIDTH + DECAY_LORA + AAA_LORA)
D_FF = 2816
CONV_WIDTH = 3
ROPE_BASE = 10000.0
NORM_EPS = 1e-6
RWKV_LN_EPS = 64e-5
RET_GN_EPS = 1e-6
W_DECAY_SCALE = math.exp(-0.5)
N_MOD = 6

kernel_name = "hymba_rwkv7_retention_convffn_adaln"


def rms_norm(x, g):
    x32 = x.astype(jnp.float32)
    y = x32 * lax.rsqrt(jnp.mean(x32 * x32, axis=-1, keepdims=True) + NORM_EPS)
    return y.astype(x.dtype) * g


def modulate(h, shift, scale):
    return h * (1 + scale[:, None, :]) + shift[:, None, :]


def head_norm(y, eps):
    y32 = y.astype(jnp.float32)
    mu = jnp.mean(y32, axis=-1, keepdims=True)
    var = jnp.mean(jnp.square(y32 - mu), axis=-1, keepdims=True)
    return (y32 - mu) * lax.rsqrt(var + eps)


def rotary(t, cos, sin):
    t1, t2 = jnp.split(t, 2, axis=-1)
    c = cos[None, :, None, :]
    s = sin[None, :, None, :]
    return jnp.concatenate([t1 * c - t2 * s, t1 * s + t2 * c], axis=-1)


def _rwkv7_step(state, inp):
    r, w, k, v, kk, a = inp
    sa = jnp.einsum('bhvk,bhk->bhv', state, -kk)
    state = (state * w[:, :, None, :]
             + sa[..., None] * (kk * a)[:, :, None, :]
             + v[..., None] * k[:, :, None, :])
    y = jnp.einsum('bhvk,bhk->bhv', state, r)
    return state, y


def rwkv7_mixer(z, mu, w0, w2, a0, a2, g2, k_k, k_a, r_k, ln_g, ln_b):
    B, S, _ = z.shape
    z = z.astype(jnp.float32)
    z_prev = jnp.pad(z, ((0, 0), (1, 0), (0, 0)))[:, :S]
    z = z + mu * (z_prev - z)
    r, k, v, wd, ad, gd = jnp.split(z, RWKV_SPLITS, axis=-1)
    decay = jnp.exp(-W_DECAY_SCALE * jax.nn.sigmoid(w0 + jnp.tanh(wd) @ w2))
    a = jax.nn.sigmoid(a0 + ad @ a2)
    g = jax.nn.sigmoid(gd) @ g2
    heads = lambda t: t.reshape(B, S, RWKV_HEADS, HEAD_DIM)
    kk = heads(k * k_k)
    kk = kk / jnp.maximum(jnp.sqrt(jnp.sum(kk * kk, axis=-1, keepdims=True)), 1e-12)
    k = k * (1 + (a - 1) * k_a)
    r, k, v, decay, a = heads(r), heads(k), heads(v), heads(decay), heads(a)
    xs = tuple(jnp.moveaxis(t, 1, 0) for t in (r, decay, k, v, kk, a))
    state0 = jnp.zeros((B, RWKV_HEADS, HEAD_DIM, HEAD_DIM), jnp.float32)
    _, y = lax.scan(_rwkv7_step, state0, xs)
    y = jnp.moveaxis(y, 0, 1)
    y = head_norm(y, RWKV_LN_EPS).reshape(B, S, RWKV_WIDTH) * ln_g + ln_b
    bonus = jnp.sum(r * k * r_k, axis=-1, keepdims=True) * v
    return (y + bonus.reshape(B, S, RWKV_WIDTH)) * g


def retention_mixer(z, cos, sin, gn_g):
    B, S, _ = z.shape
    nc = S // CHUNK
    q, k, v, g = jnp.split(z.astype(jnp.float32), 4, axis=-1)
    heads = lambda t: t.reshape(B, S, RET_HEADS, HEAD_DIM)
    q = rotary(heads(q), cos, sin) * HEAD_DIM ** -0.5
    k = rotary(heads(k), cos, sin)
    v = heads(v)
    chunks = lambda t: t.reshape(B, nc, CHUNK, RET_HEADS, HEAD_DIM).transpose(0, 3, 1, 2, 4)
    q, k, v = chunks(q), chunks(k), chunks(v)
    log_gamma = jnp.log1p(-(2.0 ** (-5.0 - jnp.arange(RET_HEADS, dtype=jnp.float32))))
    idx = jnp.arange(CHUNK, dtype=jnp.float32)
    d_intra = jnp.exp(log_gamma[:, None, None] * jnp.abs(idx[:, None] - idx[None, :]))
    q_dec = jnp.exp(log_gamma[:, None] * (idx + 1.0))
    k_dec = jnp.exp(log_gamma[:, None] * (CHUNK - 1.0 - idx))
    gamma_chunk = jnp.exp(log_gamma * CHUNK)
    scores = jnp.einsum('bhcnd,bhcmd->bhcnm', q, k) * d_intra[None, :, None]
    y = jnp.einsum('bhcnm,bhcme->bhcne', scores, v)
    kv = jnp.einsum('bhcjd,bhcje->cbhde', k * k_dec[None, :, None, :, None], v)

    def advance(state, kv_c):
        return state * gamma_chunk[None, :, None, None] + kv_c, state

    state0 = jnp.zeros((B, RET_HEADS, HEAD_DIM, HEAD_DIM), jnp.float32)
    _, s_prev = lax.scan(advance, state0, kv)
    y = y + jnp.einsum('bhcnd,cbhde->bhcne', q * q_dec[None, :, None, :, None], s_prev)
    y = y.transpose(0, 2, 3, 1, 4).reshape(B, S, RET_HEADS, HEAD_DIM)
    y = head_norm(y, RET_GN_EPS).reshape(B, S, RET_WIDTH) * gn_g
    return jax.nn.silu(g) * y


def conv_glu_ffn(h, w_up, conv_w, conv_b, w_down):
    S = h.shape[1]
    u = h @ w_up
    u_pad = jnp.pad(u, ((0, 0), (CONV_WIDTH - 1, 0), (0, 0)))
    u = conv_b + sum(u_pad[:, j:j + S] * conv_w[j] for j in range(CONV_WIDTH))
    val, gate = jnp.split(u, 2, axis=-1)
    return (jax.nn.silu(gate) * val) @ w_down


def setup_inputs(seed: int = 0) -> dict:
    key = jax.random.key(seed)
    ks = jax.random.split(key, 32)
    f32 = jnp.float32
    nrm = lambda k, shape, s: jax.random.normal(k, shape, f32) * s
    L = DEPTH
    return {
        "x": nrm(ks[0], (BATCH, SEQ, D_MODEL), 1.0),
        "c": nrm(ks[1], (BATCH, D_MODEL), 1.0),
        "w_ada": nrm(ks[2], (L, D_MODEL, N_MOD * D_MODEL), 0.5 * D_MODEL ** -0.5),
        "b_ada": nrm(ks[3], (L, N_MOD * D_MODEL), 0.02),
        "attn_norm_g": 1.0 + nrm(ks[4], (L, D_MODEL), 0.02),
        "w_in": nrm(ks[5], (L, D_MODEL, IN_COLS), D_MODEL ** -0.5),
        "rwkv_mu": jax.random.uniform(ks[6], (L, RWKV_COLS), f32),
        "rwkv_w0": nrm(ks[7], (L, RWKV_WIDTH), 1.5),
        "rwkv_w2": nrm(ks[8], (L, DECAY_LORA, RWKV_WIDTH), DECAY_LORA ** -0.5),
        "rwkv_a0": nrm(ks[9], (L, RWKV_WIDTH), 0.5),
        "rwkv_a2": nrm(ks[10], (L, AAA_LORA, RWKV_WIDTH), 0.5 * AAA_LORA ** -0.5),
        "rwkv_g2": nrm(ks[11], (L, GATE_LORA, RWKV_WIDTH), GATE_LORA ** -0.5),
        "rwkv_k_k": 0.85 + nrm(ks[12], (L, RWKV_WIDTH), 0.05),
        "rwkv_k_a": 1.0 + nrm(ks[13], (L, RWKV_WIDTH), 0.05),
        "rwkv_r_k": nrm(ks[14], (L, RWKV_HEADS, HEAD_DIM), 0.1),
        "rwkv_ln_g": 1.0 + nrm(ks[15], (L, RWKV_WIDTH), 0.02),
        "rwkv_ln_b": nrm(ks[16], (L, RWKV_WIDTH), 0.02),
        "ret_gn_g": 1.0 + nrm(ks[17], (L, RET_WIDTH), 0.02),
        "w_out": nrm(ks[18], (L, MIX_WIDTH, D_MODEL), MIX_WIDTH ** -0.5),
        "ffn_norm_g": 1.0 + nrm(ks[19], (L, D_MODEL), 0.02),
        "ffn_w_up": nrm(ks[20], (L, D_MODEL, 2 * D_FF), D_MODEL ** -0.5),
        "ffn_conv_w": nrm(ks[21], (L, CONV_WIDTH, 2 * D_FF), CONV_WIDTH ** -0.5),
        "ffn_conv_b": nrm(ks[22], (L, 2 * D_FF), 0.02),
        "ffn_w_down": nrm(ks[23], (L, D_FF, D_MODEL), D_FF ** -0.5),
        "final_norm_g": 1.0 + nrm(ks[24], (D_MODEL,), 0.02),
    }


def reference(x, c, w_ada, b_ada, attn_norm_g, w_in, rwkv_mu, rwkv_w0, rwkv_w2, rwkv_a0,
              rwkv_a2, rwkv_g2, rwkv_k_k, rwkv_k_a, rwkv_r_k, rwkv_ln_g, rwkv_ln_b, ret_gn_g,
              w_out, ffn_norm_g, ffn_w_up, ffn_conv_w, ffn_conv_b, ffn_w_down, final_norm_g):
    dt = x.dtype
    S = x.shape[1]
    pos = jnp.arange(S, dtype=jnp.float32)
    inv_freq = ROPE_BASE ** (-jnp.arange(0, HEAD_DIM, 2, dtype=jnp.float32) / HEAD_DIM)
    ang = pos[:, None] * inv_freq[None, :]
    cos, sin = jnp.cos(ang), jnp.sin(ang)
    for l in range(DEPTH):
        mod = jax.nn.silu(c) @ w_ada[l] + b_ada[l]
        sh_a, sc_a, gt_a, sh_f, sc_f, gt_f = jnp.split(mod, N_MOD, axis=-1)
        h = modulate(rms_norm(x, attn_norm_g[l]), sh_a, sc_a)
        z = h @ w_in[l]
        y_rwkv = rwkv7_mixer(z[..., :RWKV_COLS], rwkv_mu[l], rwkv_w0[l], rwkv_w2[l],
                             rwkv_a0[l], rwkv_a2[l], rwkv_g2[l], rwkv_k_k[l], rwkv_k_a[l],
                             rwkv_r_k[l], rwkv_ln_g[l], rwkv_ln_b[l])
        y_ret = retention_mixer(z[..., RWKV_COLS:], cos, sin, ret_gn_g[l])
        y_mix = jnp.concatenate([y_rwkv, y_ret], axis=-1).astype(dt) @ w_out[l]
        x = (x + gt_a[:, None, :] * y_mix).astype(dt)
        h = modulate(rms_norm(x, ffn_norm_g[l]), sh_f, sc_f)
        y_ffn = conv_glu_ffn(h, ffn_w_up[l], ffn_conv_w[l], ffn_conv_b[l], ffn_w_down[l])
        x = (x + gt_f[:, None, :] * y_ffn).astype(dt)
    return rms_norm(x, final_norm_g).astype(dt)
```

```python
import contextlib
import math
import numpy as np
import concourse.bass as bass
import concourse.mybir as mybir
from concourse.bass_utils import run_bass_kernel_spmd

F32 = mybir.dt.float32
BF16 = mybir.dt.bfloat16
AF = mybir.ActivationFunctionType
ALU = mybir.AluOpType
AX = mybir.AxisListType

NCORES = 8
S_LEN = 4096
D = 1024
NT = S_LEN // 128
KT = D // 128
DFF = 2816
NJ = DFF // 128
RW = 512
EPS = 1e-6
LN_EPS = 64e-5
WSC = math.exp(-0.5)


class View:
    __slots__ = ("buf", "ap")

    def __init__(self, buf, ap):
        self.buf = buf
        self.ap = ap

    def __getitem__(self, idx):
        return View(self.buf, self.ap[idx])

    def rr(self, pat, **kw):
        return View(self.buf, self.ap.rearrange(pat, **kw))

    def bc(self, shape):
        return View(self.buf, self.ap.broadcast_to(list(shape)))

    def cast(self, dt):
        return View(self.buf, self.ap.bitcast(dt))


class Buf:
    __slots__ = ("name", "t", "writer", "readers", "dsem", "dcnt", "dram")

    def __init__(self, name, t, dram=False):
        self.name = name
        self.t = t
        self.dram = dram
        self.writer = None
        self.readers = []
        self.dsem = None
        self.dcnt = 0

    def __getitem__(self, idx):
        return View(self, self.t[idx])

    @property
    def v(self):
        return View(self, self.t[:])


class Sched:
    ENGS = ("pe", "act", "dve", "pool", "sp")
    WKEYS = ("out", "accum_out", "ap")

    def __init__(self, nc, same_engine_raw=True):
        self.nc = nc
        self.sem = {}
        self.cnt = {e: 0 for e in self.ENGS}
        self.seen = {e: {} for e in self.ENGS}
        self.same_engine_raw = same_engine_raw
        self.q = {e: [] for e in self.ENGS}
        self.dbufs = []
        self.ninst = 0
        self.nwaits = 0

    def open(self, stack):
        self.stack = stack
        for e in self.ENGS:
            self.sem[e] = stack.enter_context(self.nc.semaphore("s_" + e))

    def _emit_waits(self, e, waits):
        seen = self.seen[e]
        for sem, val in waits:
            k = id(sem)
            if seen.get(k, 0) >= val:
                continue
            seen[k] = val
            self.q[e].append(("wait", sem, val))
            self.nwaits += 1

    def op(self, e, fn, reads, writes):
        waits = []
        for b in reads:
            w = b.writer
            if w is not None and (w[2] != e or self.same_engine_raw):
                waits.append(w[:2])
        for b in writes:
            w = b.writer
            if w is not None and w[2] != e:
                waits.append(w[:2])
            for rd in b.readers:
                if rd[2] != e:
                    waits.append(rd[:2])
        self._emit_waits(e, waits)
        self.cnt[e] += 1
        self.q[e].append(("op", fn, self.sem[e], 1))
        self.ninst += 1
        tok = (self.sem[e], self.cnt[e], e)
        for b in writes:
            b.writer = tok
            b.readers = []
        for b in reads:
            if b in writes:
                continue
            b.readers = [rd for rd in b.readers if rd[2] != e] + [tok]

    def do(self, e, method, **kw):
        reads, writes, real = [], [], {}
        for k, v in kw.items():
            if isinstance(v, View):
                (writes if k in self.WKEYS else reads).append(v.buf)
                real[k] = v.ap
            else:
                real[k] = v
        self.op(e, lambda eng: getattr(eng, method)(**real), reads, writes)

    def dma(self, e, out, in_, **kw):
        reads, writes = [], []
        owner = None
        if isinstance(in_, View):
            reads.append(in_.buf)
            if not in_.buf.dram:
                owner = in_.buf
            in_ = in_.ap
        if isinstance(out, View):
            writes.append(out.buf)
            if not out.buf.dram:
                owner = out.buf
            out = out.ap
        waits = []
        for b in reads:
            if b.writer is not None:
                waits.append(b.writer[:2])
        for b in writes:
            if b.writer is not None:
                waits.append(b.writer[:2])
            for rd in b.readers:
                waits.append(rd[:2])
        self._emit_waits(e, waits)
        if owner.dsem is None:
            owner.dsem = self.stack.enter_context(self.nc.semaphore("d%d_%s" % (len(self.dbufs), owner.name)))
            self.dbufs.append(owner)
        owner.dcnt += 16
        self.q[e].append(("op", (lambda eng, o=out, i=in_, kw=kw: eng.dma_start(out=o, in_=i, **kw)),
                          owner.dsem, 16))
        self.ninst += 1
        tok = (owner.dsem, owner.dcnt, "dma")
        for b in writes:
            b.writer = tok
            b.readers = []
        for b in reads:
            b.readers = b.readers + [tok]

    def barrier(self):
        for e in self.ENGS:
            waits = [(self.sem[o], self.cnt[o]) for o in self.ENGS if o != e and self.cnt[o] > 0]
            waits += [(b.dsem, b.dcnt) for b in self.dbufs]
            self._emit_waits(e, waits)

    def emit(self):
        def replay(q, eng):
            for it in q:
                if it[0] == "wait":
                    eng.wait_ge(it[1], it[2])
                else:
                    it[1](eng).then_inc(it[2], it[3])
        q = self.q
        self.q = {e: [] for e in self.ENGS}
        with self.nc.Block() as block:
            @block.tensor
            def _(eng):
                replay(q["pe"], eng)

            @block.scalar
            def _(eng):
                replay(q["act"], eng)

            @block.vector
            def _(eng):
                replay(q["dve"], eng)

            @block.gpsimd
            def _(eng):
                replay(q["pool"], eng)

            @block.sync
            def _(eng):
                replay(q["sp"], eng)


def _consts():
    idx = np.arange(128)
    ch = idx // 64
    same = ch[:, None] == ch[None, :]
    UI = (same & (idx[:, None] <= idx[None, :])).astype(np.float32)
    US = (same & (idx[:, None] < idx[None, :])).astype(np.float32)
    LS = (same & (idx[:, None] > idx[None, :])).astype(np.float32)
    sel2 = np.stack([(ch == 0), (ch == 1)], axis=1).astype(np.float32)
    maskA = np.concatenate([US, UI, US, UI], axis=1)
    LS4 = np.concatenate([LS] * 4, axis=1)
    ident = np.eye(128, dtype=np.float32)
    H = 8
    lg = np.log1p(-(2.0 ** (-5.0 - np.arange(H, dtype=np.float32)))).astype(np.float32)
    li = (idx % 64).astype(np.float32)
    dist = np.abs(li[:, None] - li[None, :])
    DT = np.zeros((128, H, 128), np.float32)
    for h in range(H):
        DT[:, (h % 2) * 4 + h // 2, :] = np.where(same, np.exp(lg[h] * dist), 0.0)
    qdec = np.exp(lg[None, :] * (li[:, None] + 1.0)).astype(np.float32)
    kdec = np.exp(lg[None, :] * (63.0 - li[:, None])).astype(np.float32)
    g64 = np.exp(lg * 64.0).astype(np.float32)
    gam = np.zeros((128, 4), np.float32)
    for i in range(4):
        gam[0:64, i] = g64[2 * i]
        gam[64:128, i] = g64[2 * i + 1]
    pos = np.arange(S_LEN, dtype=np.float32)
    inv = (10000.0 ** (-np.arange(0, 64, 2, dtype=np.float32) / 64)).astype(np.float32)
    ang = (pos[:, None] * inv[None, :]).astype(np.float32)
    c, s = np.cos(ang).astype(np.float32), np.sin(ang).astype(np.float32)
    CC = np.concatenate([c, c], axis=1)
    SS = np.concatenate([-s, s], axis=1)
    rot = np.stack([CC * 0.125, SS * 0.125, CC, SS], axis=1).astype(np.float32)
    rot = rot.reshape(NT, 128, 256)
    small = np.concatenate([UI, US, LS, ident, maskA, LS4, sel2, qdec, kdec, gam], axis=1)
    return dict(small=np.ascontiguousarray(small), DT=np.ascontiguousarray(DT.reshape(128, 1024)),
                rot=np.ascontiguousarray(rot))


SM_OFF = {}
_o = 0
for _n, _w in (("UI", 128), ("US", 128), ("LS", 128), ("ident", 128), ("maskA", 512), ("LS4", 512),
               ("sel2", 2), ("qdec", 8), ("kdec", 8), ("gam", 4)):
    SM_OFF[_n] = (_o, _o + _w)
    _o += _w
SM_W = _o


def build(debug=False, ntiles=NT, ngroups=NT // 4, stage=99):
    nc = bass.Bass("TRN2", target_bir_lowering=False)

    def din(name, shape):
        return nc.dram_tensor(name, list(shape), F32, kind="ExternalInput").ap()

    x_d = din("x", [S_LEN, D])
    cT_d = din("cT", [128, KT])
    wada_d = din("w_ada", [D, 6 * D])
    badaT_d = din("b_adaT", [128, 48])
    bada_d = din("b_ada", [1, 6 * D])
    gTa_d = din("gTa", [128, KT])
    gTf_d = din("gTf", [128, KT])
    fg_d = din("fg", [1, D])
    win_d = din("w_in", [D, 3840])
    mu_d = din("mu", [1, 1792])
    w2x_d = din("w2x", [65, RW])
    a2x_d = din("a2x", [65, RW])
    g2_d = din("g2", [128, RW])
    pvec_d = din("pvec", [6, RW])
    wout_d = din("w_out", [D, D])
    wup_d = din("w_up", [D, 2 * DFF])
    cwT_d = din("cwT", [128, 2 * NJ * 3])
    cbT_d = din("cbT", [128, 2 * NJ])
    wdn_d = din("w_down", [DFF, D])
    small_d = din("small", [128, SM_W])
    DT_d = din("DT", [128, 1024])
    rot_d = din("rot", [NT, 128, 256])
    out_d = nc.dram_tensor("out", [S_LEN, D], F32, kind="ExternalOutput").ap()
    x1s_t = nc.dram_tensor("x1s", [S_LEN, D], F32, kind="Internal")
    dbg_outs = {}

    with contextlib.ExitStack() as st0:
        S = Sched(nc)
        S.open(st0)

        name_ctr = [0]

        def sb(stack, name, shape, dt=F32):
            name_ctr[0] += 1
            return Buf(name, stack.enter_context(nc.sbuf_tensor("sb%d_%s" % (name_ctr[0], name), list(shape), dt)))

        def dbg(name, view, shape):
            if not debug or name in dbg_outs:
                return
            t = nc.dram_tensor("dbg_" + name, list(shape), F32, kind="ExternalOutput").ap()
            dbg_outs[name] = t
            S.dma("pool", t, view)

        PS = [Buf("ps%d" % i, st0.enter_context(nc.psum_tensor("ps%d" % i, [128, 512], F32))) for i in range(8)]
        ps_rr = [0]

        def psum():
            b = PS[ps_rr[0] % 8]
            ps_rr[0] += 1
            return b

        x1s = Buf("x1s", x1s_t.ap(), dram=True)

        small = sb(st0, "small", [128, SM_W])
        S.dma("sp", small.v, small_d)

        def sm(name):
            a, b = SM_OFF[name]
            return small[:, a:b]

        identb = sb(st0, "identb", [128, 128], BF16)
        S.do("dve", "tensor_copy", out=identb.v, in_=sm("ident"))
        modT = sb(st0, "modT", [128, 48])
        gscA = sb(st0, "gscA", [128, KT])
        gscF = sb(st0, "gscF", [128, KT])
        gtA = sb(st0, "gtA", [128, D])
        mhalf = sb(st0, "mhalf", [128, 8])
        S.do("pool", "memset", ap=mhalf.v, constant=-0.5)

        def rsqrt_small(dst, src):
            n = src.ap.shape[-1]
            S.do("pool", "tensor_tensor", out=dst, in0=src, in1=mhalf[:, 0:n], op=ALU.pow)

        hview = lambda v: v.rr("p (h d) -> p h d", h=8)
        wada_v = wada_d.rearrange("(k p) c -> p k c", p=128)
        win_v = win_d.rearrange("(k p) c -> p k c", p=128)
        wout_v = wout_d.rearrange("(k p) c -> p k c", p=128)

        def silu_c(stack):
            cT = sb(stack, "cT", [128, KT])
            sc = sb(stack, "sc", [128, KT])
            scB = sb(stack, "scB", [128, KT, 128])
            S.dma("sp", cT.v, cT_d)
            S.do("act", "activation", out=sc.v, in_=cT.v, func=AF.Tanh, scale=0.5)
            S.do("dve", "scalar_tensor_tensor", out=sc.v, in0=sc.v, scalar=1.0, in1=cT.v,
                 op0=ALU.add, op1=ALU.mult)
            S.do("dve", "tensor_scalar", out=sc.v, in0=sc.v, scalar1=0.5, scalar2=None, op0=ALU.mult)
            S.do("dve", "tensor_copy", out=scB.v, in_=sc[:, :, None].bc([128, KT, 128]))
            return sc, scB

        def mod_chunk(ci, wa, bbc, sc, scB, badaT, gt_dst):
            w_ = wa[ci % 2]
            S.dma("sp", w_.v, wada_v[:, :, ci * 256:(ci + 1) * 256])
            P = psum()
            if gt_dst is not None:
                b_ = bbc[ci % 2]
                S.dma("sp", b_.v, bada_d[:, ci * 256:(ci + 1) * 256].partition_broadcast(128))
                for k in range(KT):
                    S.do("pe", "matmul", out=P[:, 0:256], lhsT=scB[:, k, :], rhs=w_[:, k, :],
                         start=(k == 0), stop=(k == KT - 1))
                c0 = (ci % 4) * 256
                S.do("dve", "tensor_tensor", out=gt_dst[:, c0:c0 + 256], in0=P[:, 0:256], in1=b_.v, op=ALU.add)
            else:
                for nt_ in range(2):
                    for k in range(KT):
                        S.do("pe", "matmul", out=P[:, nt_:nt_ + 1],
                             lhsT=w_[:, k, nt_ * 128:(nt_ + 1) * 128], rhs=sc[:, k:k + 1],
                             start=(k == 0), stop=(k == KT - 1))
                S.do("dve", "tensor_tensor", out=modT[:, ci * 2:ci * 2 + 2], in0=P[:, 0:2],
                     in1=badaT[:, ci * 2:ci * 2 + 2], op=ALU.add)

        class NormT:
            def __init__(self, stack, tag, gsc, sh_col0):
                self.xt = [sb(stack, tag + "xt%d" % i, [128, D]) for i in range(2)]
                self.xn = sb(stack, tag + "xn", [128, D])
                self.junk = sb(stack, tag + "junk", [128, D], BF16)
                self.hT = [sb(stack, tag + "hT%d" % i, [128, KT, 129], BF16) for i in range(2)]
                self.st = [sb(stack, tag + "nst%d" % i, [128, 2]) for i in range(2)]
                self.gsc, self.sh0 = gsc, sh_col0
                S.do("pool", "memset", ap=self.hT[1][:, :, 128:129], constant=0.0)

            def run(self, t, src_rows):
                xb, hc, hp = self.xt[t % 2], self.hT[t % 2], self.hT[(t + 1) % 2]
                st_ = self.st[t % 2]
                S.dma("sp", xb.v, src_rows)
                S.do("act", "activation", out=self.junk.v, in_=xb.v, func=AF.Square, accum_out=st_[:, 0:1])
                S.do("dve", "tensor_scalar", out=st_[:, 0:1], in0=st_[:, 0:1], scalar1=1.0 / D, scalar2=EPS,
                     op0=ALU.mult, op1=ALU.add)
                rsqrt_small(st_[:, 1:2], st_[:, 0:1])
                S.do("act", "activation", out=self.xn.v, in_=xb.v, func=AF.Copy, scale=st_[:, 1:2])
                S.do("pool", "tensor_copy", out=hc[:, :, 0:1], in_=hp[:, :, 128:129])
                for half in range(2):
                    P = psum()
                    for kk_ in range(4):
                        k = half * 4 + kk_
                        S.do("pe", "transpose", out=P[:, kk_ * 128:(kk_ + 1) * 128],
                             in_=self.xn[:, k * 128:(k + 1) * 128], identity=sm("ident"))
                    for kk_ in range(4):
                        k = half * 4 + kk_
                        S.do("act", "activation", out=hc[:, k, 1:129], in_=P[:, kk_ * 128:(kk_ + 1) * 128],
                             func=AF.Identity, scale=self.gsc[:, k:k + 1],
                             bias=modT[:, self.sh0 + k:self.sh0 + k + 1])
                return xb, hc

        st1 = [sb(st0, "st1_%d" % i, [128, 8]) for i in range(12)]
        st_i = [0]

        def stat():
            b = st1[st_i[0] % len(st1)]
            st_i[0] += 1
            return b

        def head_norm(dst, src, eps, sq):
            s1, s2, mean, var = stat(), stat(), stat(), stat()
            S.do("dve", "tensor_reduce", out=s1.v, in_=hview(src), axis=AX.X, op=ALU.add)
            S.do("act", "activation", out=sq, in_=src, func=AF.Square)
            S.do("dve", "tensor_reduce", out=s2.v, in_=hview(sq), axis=AX.X, op=ALU.add)
            S.do("dve", "tensor_scalar", out=mean.v, in0=s1.v, scalar1=1.0 / 64, scalar2=None, op0=ALU.mult)
            S.do("dve", "tensor_tensor", out=var.v, in0=mean.v, in1=mean.v, op=ALU.mult)
            S.do("dve", "scalar_tensor_tensor", out=var.v, in0=s2.v, scalar=1.0 / 64, in1=var.v,
                 op0=ALU.mult, op1=ALU.subtract)
            S.do("dve", "tensor_scalar", out=var.v, in0=var.v, scalar1=eps, scalar2=None, op0=ALU.add)
            rs = stat()
            rsqrt_small(rs.v, var.v)
            S.do("dve", "tensor_tensor", out=hview(dst), in0=hview(src),
                 in1=mean[:, :, None].bc([128, 8, 64]), op=ALU.subtract)
            S.do("dve", "tensor_tensor", out=hview(dst), in0=hview(dst),
                 in1=rs[:, :, None].bc([128, 8, 64]), op=ALU.mult)

        def proj_T(dst, hc, W, c0, W2=None):
            P = psum()
            n = 2 * KT if W2 is not None else KT
            i = 0
            for k in range(KT):
                S.do("pe", "matmul", out=P.v, lhsT=hc[:, k, 1:129], rhs=W[:, k, c0:c0 + 512],
                     start=(i == 0), stop=(i == n - 1))
                i += 1
            if W2 is not None:
                for k in range(KT):
                    S.do("pe", "matmul", out=P.v, lhsT=hc[:, k, 0:128], rhs=W2[:, k, c0:c0 + 512],
                         start=False, stop=(i == n - 1))
                    i += 1
            S.do("act", "activation", out=dst, in_=P.v, func=AF.Copy)

        yrs = Buf("yrs", nc.dram_tensor("yrs", [S_LEN, RW], BF16, kind="Internal").ap(), dram=True)

        with contextlib.ExitStack() as stA:
            W1 = sb(stA, "W1", [128, KT, 1792], BF16)
            W2 = sb(stA, "W2", [128, KT, 1792], BF16)
            w2x = sb(stA, "w2x", [65, RW])
            a2x = sb(stA, "a2x", [65, RW])
            g2 = sb(stA, "g2", [128, RW])
            pv = [sb(stA, "pv%d" % i, [128, RW]) for i in range(5)]
            S.dma("sp", w2x.v, w2x_d)
            S.dma("sp", a2x.v, a2x_d)
            S.dma("sp", g2.v, g2_d)
            for i in range(5):
                S.dma("sp", pv[i].v, pvec_d[i:i + 1, :].partition_broadcast(128))

            with contextlib.ExitStack() as st00:
                mu_bc = sb(st00, "mu_bc", [128, 1792])
                omm = sb(st00, "omm", [128, 1792])
                S.dma("sp", mu_bc.v, mu_d.partition_broadcast(128))
                S.do("dve", "tensor_scalar", out=omm.v, in0=mu_bc.v, scalar1=-1.0, scalar2=1.0,
                     op0=ALU.mult, op1=ALU.add)
                stg = [sb(st00, "stg%d" % i, [128, 1792]) for i in range(2)]
                for k in range(KT):
                    sg = stg[k % 2]
                    S.dma("sp", sg.v, win_v[:, k, 0:1792])
                    S.do("dve", "tensor_tensor", out=W1[:, k, :], in0=sg.v, in1=omm.v, op=ALU.mult)
                    S.do("pool", "tensor_tensor", out=W2[:, k, :], in0=sg.v, in1=mu_bc.v, op=ALU.mult)
                sc, scB = silu_c(st00)
                badaT = sb(st00, "badaT", [128, 48])
                gTa = sb(st00, "gTa", [128, KT])
                gTf = sb(st00, "gTf", [128, KT])
                S.dma("sp", badaT.v, badaT_d)
                S.dma("sp", gTa.v, gTa_d)
                S.dma("sp", gTf.v, gTf_d)
                wa = [sb(st00, "wa%d" % i, [128, KT, 256]) for i in range(2)]
                bbc = [sb(st00, "bbc%d" % i, [128, 256]) for i in range(2)]
                for ci in range(24):
                    if 8 <= ci < 12:
                        mod_chunk(ci, wa, bbc, sc, scB, badaT, gtA)
                    elif ci >= 20:
                        continue
                    else:
                        mod_chunk(ci, wa, bbc, sc, scB, badaT, None)
                S.do("dve", "scalar_tensor_tensor", out=gscA.v, in0=modT[:, 8:16], scalar=1.0, in1=gTa.v,
                     op0=ALU.add, op1=ALU.mult)
                S.do("dve", "scalar_tensor_tensor", out=gscF.v, in0=modT[:, 32:40], scalar=1.0, in1=gTf.v,
                     op0=ALU.add, op1=ALU.mult)
                dbg("modT", modT.v, [128, 48])
                dbg("gtA", gtA.v, [128, D])
                S.barrier()
                S.emit()

            NA = NormT(stA, "a1", gscA, 0)
            Z = [sb(stA, "Z%d" % i, [128, RW]) for i in range(3)]
            TA = [sb(stA, "TA%d" % i, [128, RW]) for i in range(10)]
            TB = [sb(stA, "TB%d" % i, [128, RW], BF16) for i in range(9)]
            lo_w = sb(stA, "lo_w", [65, 128])
            lo_a = sb(stA, "lo_a", [65, 128])
            lo_g = sb(stA, "lo_g", [128, 128])
            S.do("pool", "memset", ap=lo_w.v, constant=1.0)
            S.do("pool", "memset", ap=lo_a.v, constant=1.0)
            WC = sb(stA, "WC", [128, 4, 2])
            Fh = sb(stA, "Fh", [128, 4, 512], BF16)
            AM = sb(stA, "AM", [128, 8, 512], BF16)
            Lp = [sb(stA, "Lp%d" % i, [128, 8, 128], BF16) for i in range(2)]
            Np = [sb(stA, "Np%d" % i, [128, 8, 128], BF16) for i in range(2)]
            X = [sb(stA, "X%d" % i, [128, 8, 128], BF16) for i in range(2)]
            RhT = sb(stA, "RhT", [128, 4, 128], BF16)
            MpT = sb(stA, "MpT", [128, 4, 2, 128], BF16)
            H32 = sb(stA, "H32", [128, 4, 64])
            Hc = [sb(stA, "Hc%d" % i, [128, 4, 64], BF16) for i in range(3)]
            Hbd = [sb(stA, "Hbd%d" % i, [128, 4, 128], BF16) for i in range(3)]
            slot = lambda h: (h % 2) * 4 + h // 2
            yrw = [sb(stA, "yrw%d" % i, [128, RW], BF16) for i in range(2)]
            S.do("pool", "memset", ap=H32.v, constant=0.0)
            S.do("pool", "memset", ap=Hc[0].v, constant=0.0)
            S.do("pool", "memset", ap=MpT.v, constant=0.0)
            for b_ in Hbd:
                S.do("pool", "memset", ap=b_.v, constant=0.0)

            def proj_F(hc, c0, m):
                P = psum()
                i = 0
                for W_, sl in ((W1, slice(1, 129)), (W2, slice(0, 128))):
                    for k in range(KT):
                        S.do("pe", "matmul", out=P[0:m, 0:128], lhsT=W_[:, k, c0:c0 + m], rhs=hc[:, k, sl],
                             start=(i == 0), stop=(i == 2 * KT - 1))
                        i += 1
                return P[0:m, 0:128]

            for t in range(ntiles):
                xb, hc = NA.run(t, x_d[t * 128:(t + 1) * 128, :])
                if t == 0:
                    dbg("hT", hc[:, :, 1:129], [128, KT, 128])
                if stage < 2:
                    continue
                zr, zk, zv = Z[0], Z[1], Z[2]
                proj_T(zr.v, hc, W1, 0, W2)
                proj_T(zk.v, hc, W1, 512, W2)
                proj_T(zv.v, hc, W1, 1024, W2)
                Pw = proj_F(hc, 1536, 64)
                S.do("act", "activation", out=lo_w[0:64, :], in_=Pw, func=AF.Tanh)
                Pa_ = proj_F(hc, 1600, 64)
                S.do("act", "activation", out=lo_a[0:64, :], in_=Pa_, func=AF.Copy)
                Pg = proj_F(hc, 1664, 128)
                S.do("act", "activation", out=lo_g.v, in_=Pg, func=AF.Tanh, scale=0.5)
                S.do("dve", "tensor_scalar", out=lo_g.v, in0=lo_g.v, scalar1=0.5, scalar2=0.5,
                     op0=ALU.mult, op1=ALU.add)
                if t == 0:
                    dbg("zr", zr.v, [128, RW])
                    dbg("zv", zv.v, [128, RW])
                if stage < 3:
                    continue
                lw, am1, gsb = TA[0], TA[1], TA[2]
                P = psum()
                S.do("pe", "matmul", out=P.v, lhsT=lo_w.v, rhs=w2x.v, start=True, stop=True)
                S.do("act", "activation", out=lw.v, in_=P.v, func=AF.Tanh, scale=0.5)
                S.do("dve", "tensor_scalar", out=lw.v, in0=lw.v, scalar1=1.0, scalar2=-0.5 * WSC,
                     op0=ALU.add, op1=ALU.mult)
                P = psum()
                S.do("pe", "matmul", out=P.v, lhsT=lo_a.v, rhs=a2x.v, start=True, stop=True)
                S.do("act", "activation", out=am1.v, in_=P.v, func=AF.Tanh, scale=0.5)
                S.do("dve", "tensor_scalar", out=am1.v, in0=am1.v, scalar1=0.5, scalar2=-0.5,
                     op0=ALU.mult, op1=ALU.add)
                P = psum()
                S.do("pe", "matmul", out=P.v, lhsT=lo_g.v, rhs=g2.v, start=True, stop=True)
                S.do("act", "activation", out=gsb.v, in_=P.v, func=AF.Copy)
                if t == 0:
                    dbg("lw", lw.v, [128, RW])
                    dbg("am1", am1.v, [128, RW])
                    dbg("gsb", gsb.v, [128, RW])
                if stage < 4:
                    continue
                Wc, Winv, Wprev, Ee = TA[3], TA[4], TA[5], TA[6]
                P = psum()
                S.do("pe", "matmul", out=P.v, lhsT=sm("UI"), rhs=lw.v, start=True, stop=True)
                S.do("act", "activation", out=Wc.v, in_=P.v, func=AF.Exp)
                S.do("act", "activation", out=Winv.v, in_=P.v, func=AF.Exp, scale=-1.0)
                P = psum()
                S.do("pe", "matmul", out=P.v, lhsT=sm("US"), rhs=lw.v, start=True, stop=True)
                S.do("act", "activation", out=Wprev.v, in_=P.v, func=AF.Exp)
                P = psum()
                S.do("pe", "matmul", out=P.v, lhsT=sm("LS"), rhs=lw.v, start=True, stop=True)
                S.do("act", "activation", out=Ee.v, in_=P.v, func=AF.Exp)
                P = psum()
                for i in range(4):
                    S.do("pe", "matmul", out=P[:, 2 * i:2 * i + 2], lhsT=lw[:, i * 128:(i + 1) * 128],
                         rhs=sm("sel2"), start=True, stop=True)
                S.do("act", "activation", out=WC.v, in_=P[:, 0:8].rr("p (i c) -> p i c", i=4), func=AF.Exp)
                if stage < 5:
                    continue
                kk0, k2, kka = TA[7], TA[8], TA[9]
                n2, rn = stat(), stat()
                S.do("dve", "tensor_tensor", out=kk0.v, in0=zk.v, in1=pv[0].v, op=ALU.mult)
                S.do("pool", "tensor_tensor", out=k2.v, in0=kk0.v, in1=kk0.v, op=ALU.mult)
                S.do("dve", "tensor_reduce", out=n2.v, in_=hview(k2.v), axis=AX.X, op=ALU.add)
                if stage < 5.1:
                    continue
                S.do("dve", "tensor_scalar", out=n2.v, in0=n2.v, scalar1=1e-24, scalar2=None, op0=ALU.max)
                rsqrt_small(rn.v, n2.v)
                if stage < 5.2:
                    continue
                S.do("dve", "tensor_tensor", out=hview(kk0.v), in0=hview(kk0.v),
                     in1=rn[:, :, None].bc([128, 8, 64]), op=ALU.mult)
                if stage < 5.3:
                    continue
                S.do("pool", "tensor_tensor", out=k2.v, in0=am1.v, in1=pv[1].v, op=ALU.mult)
                S.do("dve", "scalar_tensor_tensor", out=k2.v, in0=k2.v, scalar=1.0, in1=zk.v,
                     op0=ALU.add, op1=ALU.mult)
                S.do("dve", "scalar_tensor_tensor", out=kka.v, in0=am1.v, scalar=1.0, in1=kk0.v,
                     op0=ALU.add, op1=ALU.mult)
                if stage < 5.4:
                    continue
                at, bt, kt, rtl, Vb, Bh0, Bh1, Kh0, Kh1 = TB
                Bhc, Khc = (Bh0, Bh1), (Kh0, Kh1)
                S.do("dve", "scalar_tensor_tensor", out=at.v, in0=kk0.v, scalar=-1.0, in1=Wprev.v,
                     op0=ALU.mult, op1=ALU.mult)
                S.do("pool", "tensor_tensor", out=bt.v, in0=kka.v, in1=Winv.v, op=ALU.mult)
                S.do("dve", "tensor_tensor", out=kt.v, in0=k2.v, in1=Winv.v, op=ALU.mult)
                S.do("pool", "tensor_tensor", out=rtl.v, in0=zr.v, in1=Wc.v, op=ALU.mult)
                for c in range(2):
                    S.do("dve", "scalar_tensor_tensor", out=Bhc[c].v, in0=kka.v, scalar=sm("sel2")[:, c:c + 1], in1=Ee.v,
                         op0=ALU.mult, op1=ALU.mult)
                    S.do("dve", "scalar_tensor_tensor", out=Khc[c].v, in0=k2.v, scalar=sm("sel2")[:, c:c + 1], in1=Ee.v,
                         op0=ALU.mult, op1=ALU.mult)
                S.do("act", "activation", out=Vb.v, in_=zv.v, func=AF.Copy)
                if stage < 5.5:
                    continue
                bn = stat()
                S.do("pool", "tensor_tensor", out=Wc.v, in0=zr.v, in1=pv[2].v, op=ALU.mult)
                S.do("dve", "tensor_tensor", out=Wc.v, in0=Wc.v, in1=k2.v, op=ALU.mult)
                S.do("dve", "tensor_reduce", out=bn.v, in_=hview(Wc.v), axis=AX.X, op=ALU.add)
                if t == 0:
                    dbg("at", at.v, [128, RW])
                    dbg("bt", bt.v, [128, RW])
                    dbg("Kh1", Kh1.v, [128, RW])
                if stage < 6:
                    continue
                for i in range(4):
                    P = psum()
                    Pb = P.v.cast(BF16)
                    for j, src in enumerate((at, rtl, bt, kt)):
                        S.do("pe", "transpose", out=Pb[:, j * 128:(j + 1) * 128], in_=src[:, i * 128:(i + 1) * 128],
                             identity=identb.v)
                    S.do("act", "activation", out=Fh[:, i, :], in_=Pb[:, 0:512], func=AF.Copy)
                if stage < 7:
                    continue
                for h in range(8):
                    i, pb = h // 2, 64 * (h % 2)
                    P = psum()
                    S.do("pe", "matmul", out=P[:, 0:256], lhsT=Fh[pb:pb + 64, i, 256:384], rhs=Fh[pb:pb + 64, i, 0:256],
                         start=True, stop=True)
                    S.do("pe", "matmul", out=P[:, 256:512], lhsT=Fh[pb:pb + 64, i, 384:512], rhs=Fh[pb:pb + 64, i, 0:256],
                         start=True, stop=True)
                    S.do("dve", "tensor_tensor", out=AM[:, slot(h), :], in0=P.v, in1=sm("maskA"), op=ALU.mult)
                for hg in range(2):
                    P = psum()
                    for hh in range(4):
                        h = hh * 2 + hg
                        i, pb = h // 2, 64 * (h % 2)
                        S.do("pe", "matmul", out=P[:, hh * 128:(hh + 1) * 128], lhsT=Fh[pb:pb + 64, i, 0:128],
                             rhs=Fh[pb:pb + 64, i, 256:384], start=True, stop=True)
                    S.do("dve", "tensor_tensor", out=Lp[0][:, hg * 4:hg * 4 + 4, :].rr("p h s -> p (h s)"), in0=P.v,
                         in1=sm("LS4"), op=ALU.mult)
                if stage < 8:
                    continue
                P = psum()
                for h in range(8):
                    sl_ = slot(h)
                    S.do("pe", "matmul", out=P[:, sl_ * 64:(sl_ + 1) * 64], lhsT=AM[:, sl_, 256:384],
                         rhs=Vb[:, h * 64:(h + 1) * 64], start=True, stop=True)
                S.do("act", "activation", out=X[0][:, :, 64:128], in_=hview(P.v), func=AF.Copy)
                atv = at.v.rr("p (hh hg d) -> p hg hh d", hh=4, hg=2)
                for hg in range(2):
                    S.do("pool", "tensor_copy", out=X[0][:, hg * 4:hg * 4 + 4, 0:64], in_=atv[:, hg])
                if stage < 9:
                    continue
                xcur = 0
                for lv in range(6):
                    if lv == 0:
                        Ncur = lambda s_: AM[:, s_, 0:128]
                    else:
                        Ncur = (lambda s_, Nb=Np[lv % 2]: Nb[:, s_, :])
                    Lcur = Lp[lv % 2]
                    for hg in range(2):
                        P = psum()
                        for hh in range(4):
                            s_ = hg * 4 + hh
                            S.do("pe", "matmul", out=P[:, hh * 128:(hh + 1) * 128], lhsT=identb.v,
                                 rhs=X[xcur][:, s_, :], start=True, stop=False)
                            S.do("pe", "matmul", out=P[:, hh * 128:(hh + 1) * 128], lhsT=Ncur(s_),
                                 rhs=X[xcur][:, s_, :], start=False, stop=True)
                        S.do("act", "activation", out=X[1 - xcur][:, hg * 4:hg * 4 + 4, :].rr("p h s -> p (h s)"),
                             in_=P.v, func=AF.Copy)
                    xcur = 1 - xcur
                    if lv == 5:
                        break
                    for hg in range(2):
                        P = psum()
                        for hh in range(4):
                            s_ = hg * 4 + hh
                            S.do("pe", "matmul", out=P[:, hh * 128:(hh + 1) * 128], lhsT=Lcur[:, s_, :],
                                 rhs=Ncur(s_), start=True, stop=True)
                        S.do("dve", "tensor_copy", out=Np[(lv + 1) % 2][:, hg * 4:hg * 4 + 4, :].rr("p h s -> p (h s)"),
                             in_=P.v)
                    if lv < 4:
                        for hg in range(2):
                            P = psum()
                            for hh in range(4):
                                s_ = hg * 4 + hh
                                S.do("pe", "matmul", out=P[:, hh * 128:(hh + 1) * 128], lhsT=Ncur(s_),
                                     rhs=Lcur[:, s_, :], start=True, stop=True)
                            S.do("act", "activation",
                                 out=Lp[(lv + 1) % 2][:, hg * 4:hg * 4 + 4, :].rr("p h s -> p (h s)"),
                                 in_=P.v, func=AF.Copy)
                Xf = X[xcur]
                if t == 0:
                    dbg("Xf", Xf.v, [128, 8, 128])
                if stage < 10:
                    continue
                P = psum()
                for h in range(8):
                    i, pb, sl_ = h // 2, 64 * (h % 2), slot(h)
                    S.do("pe", "matmul", out=P[pb:pb + 64, i * 128:(i + 1) * 128], lhsT=Xf[:, sl_, 0:64],
                         rhs=AM[:, sl_, 128:256], start=True, stop=True)
                S.do("dve", "tensor_tensor", out=RhT.v, in0=P.v.rr("p (i t) -> p i t", i=4), in1=Fh[:, :, 128:256],
                     op=ALU.add)
                P = psum()
                for h in range(8):
                    i, pb, sl_ = h // 2, 64 * (h % 2), slot(h)
                    for c in range(2):
                        S.do("pe", "matmul", out=P[pb:pb + 64, (i * 2 + c) * 64:(i * 2 + c + 1) * 64],
                             lhsT=Xf[:, sl_, 0:64], rhs=Bhc[c][:, h * 64:(h + 1) * 64], start=True, stop=True)
                Pv_ = P.v.rr("p (i c j) -> p i c j", i=4, c=2)
                S.do("act", "activation", out=MpT[0:64, :, :, 0:64], in_=Pv_[0:64], func=AF.Copy)
                S.do("act", "activation", out=MpT[64:128, :, :, 64:128], in_=Pv_[64:128], func=AF.Copy)
                if stage < 11:
                    continue
                hs = [(Hc[(2 * t + q) % 3], Hbd[(2 * t + q) % 3]) for q in range(3)]
                for c in range(2):
                    (hin, _), (hout, hout_bd) = hs[c], hs[c + 1]
                    P = psum()
                    first = True
                    for i in range(4):
                        for h in (2 * i, 2 * i + 1):
                            pb, sl_ = 64 * (h % 2), slot(h)
                            o_ = P[pb:pb + 64, i * 64:(i + 1) * 64]
                            S.do("pe", "matmul", out=o_, lhsT=Bhc[c][:, h * 64:(h + 1) * 64], rhs=Xf[:, sl_, 64:128],
                                 start=(i == 0), stop=False, skip_group_check=True)
                            S.do("pe", "matmul", out=o_, lhsT=Khc[c][:, h * 64:(h + 1) * 64],
                                 rhs=Vb[:, h * 64:(h + 1) * 64], start=False, stop=False, skip_group_check=True)
                        S.do("pe", "matmul", out=P[:, i * 64:(i + 1) * 64], lhsT=MpT[:, i, c, :], rhs=hin[:, i, :],
                             start=False, stop=True, skip_group_check=True)
                    S.do("dve", "tensor_tensor", out=H32.v, in0=H32.v, in1=WC[:, :, c:c + 1].bc([128, 4, 64]),
                         op=ALU.mult)
                    S.do("dve", "tensor_tensor", out=H32.v, in0=H32.v, in1=P[:, 0:256].rr("p (i v) -> p i v", i=4),
                         op=ALU.add)
                    S.do("act", "activation", out=hout.v, in_=H32.v, func=AF.Copy)
                    S.do("act", "activation", out=hout_bd[0:64, :, 0:64], in_=H32[0:64], func=AF.Copy)
                    S.do("act", "activation", out=hout_bd[64:128, :, 64:128], in_=H32[64:128], func=AF.Copy)
                if stage < 12:
                    continue
                P = psum()
                first = True
                for i in range(4):
                    for h in (2 * i, 2 * i + 1):
                        sl_ = slot(h)
                        hc_ = slice(h * 64, (h + 1) * 64)
                        S.do("pe", "matmul", out=P[:, hc_], lhsT=AM[:, sl_, 128:256], rhs=Xf[:, sl_, 64:128],
                             start=first, stop=False, skip_group_check=True)
                        first = False
                        S.do("pe", "matmul", out=P[:, hc_], lhsT=AM[:, sl_, 384:512], rhs=Vb[:, hc_],
                             start=False, stop=False, skip_group_check=True)
                    for c in range(2):
                        S.do("pe", "matmul", out=P[c * 64:(c + 1) * 64, i * 128:(i + 1) * 128],
                             lhsT=RhT[:, i, c * 64:(c + 1) * 64], rhs=hs[c][1][:, i, :],
                             start=False, stop=(c == 1), skip_group_check=True)
                yr = TA[3]
                S.do("act", "activation", out=yr.v, in_=P.v, func=AF.Copy)
                if t == 0:
                    dbg("yraw", yr.v, [128, RW])
                yn = TA[4]
                head_norm(yn.v, yr.v, LN_EPS, TA[9].v)
                S.do("dve", "tensor_tensor", out=yn.v, in0=yn.v, in1=pv[3].v, op=ALU.mult)
                S.do("pool", "tensor_tensor", out=yn.v, in0=yn.v, in1=pv[4].v, op=ALU.add)
                S.do("dve", "tensor_tensor", out=hview(TA[5].v), in0=hview(zv.v), in1=bn[:, :, None].bc([128, 8, 64]),
                     op=ALU.mult)
                S.do("pool", "tensor_tensor", out=yn.v, in0=yn.v, in1=TA[5].v, op=ALU.add)
                yo = yrw[t % 2]
                S.do("dve", "tensor_tensor", out=yo.v, in0=yn.v, in1=gsb.v, op=ALU.mult)
                if t == 0:
                    dbg("yrwkv", yo.v, [128, RW])
                if stage < 12.5:
                    continue
                S.dma("sp", yrs[t * 128:(t + 1) * 128, :], yo.v)
            S.barrier()
            S.emit()

        stW = st0
        wup_v = wup_d.rearrange("(k p) c -> p k c", p=128)
        KA = 6
        WuA = sb(stW, "WuA", [128, KA, 2 * DFF], BF16)
        for k in range(KA):
            for c0 in range(0, 2 * DFF, 1408):
                S.dma("pool", WuA[:, k, c0:c0 + 1408], wup_v[:, k, c0:c0 + 1408])
        with contextlib.ExitStack() as stA:
            Wr = sb(stA, "Wr", [128, KT, 2048], BF16)
            Wo = sb(stA, "Wo", [128, KT, D], BF16)
            gng = sb(stA, "gng", [128, RW])
            DTb = sb(stA, "DTb", [128, 8, 128])
            S.dma("sp", gng.v, pvec_d[5:6, :].partition_broadcast(128))
            S.dma("sp", DTb.v, DT_d.rearrange("p (h n) -> p h n", h=8))
            for k in range(KT):
                S.dma("pool", Wr[:, k, 0:1024], win_v[:, k, 1792:2816])
                S.dma("pool", Wr[:, k, 1024:2048], win_v[:, k, 2816:3840])
                S.dma("pool", Wo[:, k, :], wout_v[:, k, :])
            NA = NormT(stA, "a2", gscA, 0)
            Z = [sb(stA, "Zr%d" % i, [128, RW]) for i in range(4)]
            TA = [sb(stA, "TAr%d" % i, [128, RW]) for i in range(6)]
            TB = [sb(stA, "TBr%d" % i, [128, RW], BF16) for i in range(5)]
            Fh = sb(stA, "Fhr", [128, 4, 384], BF16)
            AM = sb(stA, "AMr", [128, 8, 128], BF16)
            S32 = sb(stA, "S32", [128, 4, 64])
            Sbd = [sb(stA, "Sbd%d" % i, [128, 4, 128], BF16) for i in range(3)]
            rot = [sb(stA, "rot%d" % i, [128, 4, 64]) for i in range(2)]
            ymix = [sb(stA, "ymix%d" % i, [128, D], BF16) for i in range(2)]
            ymT = sb(stA, "ymT", [128, KT, 128], BF16)
            S.do("pool", "memset", ap=S32.v, constant=0.0)
            for b_ in Sbd:
                S.do("pool", "memset", ap=b_.v, constant=0.0)
            for t in range(ntiles if stage >= 13 else 0):
                ym = ymix[t % 2]
                S.dma("sp", ym[:, 0:512], yrs[t * 128:(t + 1) * 128, :])
                rt_ = rot[t % 2]
                S.dma("sp", rt_.v, rot_d[t].rearrange("p (a d) -> p a d", a=4))
                xb, hc = NA.run(t, x_d[t * 128:(t + 1) * 128, :])
                zq, zk2, zv2, zg = Z
                proj_T(zq.v, hc, Wr, 0)
                proj_T(zk2.v, hc, Wr, 512)
                proj_T(zv2.v, hc, Wr, 1024)
                proj_T(zg.v, hc, Wr, 1536)
                qr, kr, qd, kd, Vb2 = TB

                def rotary(dst, src, ci_, si_):
                    t1, t2 = TA[0], TA[1]
                    sv = hview(src)
                    S.do("dve", "tensor_tensor", out=hview(t1.v), in0=sv, in1=rt_[:, ci_:ci_ + 1, :].bc([128, 8, 64]),
                         op=ALU.mult)
                    S.do("dve", "tensor_tensor", out=hview(t2.v)[:, :, 0:32], in0=sv[:, :, 32:64],
                         in1=rt_[:, si_:si_ + 1, 0:32].bc([128, 8, 32]), op=ALU.mult)
                    S.do("dve", "tensor_tensor", out=hview(t2.v)[:, :, 32:64], in0=sv[:, :, 0:32],
                         in1=rt_[:, si_:si_ + 1, 32:64].bc([128, 8, 32]), op=ALU.mult)
                    S.do("dve", "tensor_tensor", out=t1.v, in0=t1.v, in1=t2.v, op=ALU.add)
                    S.do("act", "activation", out=dst.v, in_=t1.v, func=AF.Copy)
                    return t1

                t1 = rotary(qr, zq.v, 0, 1)
                S.do("dve", "tensor_tensor", out=hview(qd.v), in0=hview(t1.v), in1=sm("qdec")[:, :, None].bc([128, 8, 64]),
                     op=ALU.mult)
                t1 = rotary(kr, zk2.v, 2, 3)
                S.do("dve", "tensor_tensor", out=hview(kd.v), in0=hview(t1.v), in1=sm("kdec")[:, :, None].bc([128, 8, 64]),
                     op=ALU.mult)
                S.do("act", "activation", out=Vb2.v, in_=zv2.v, func=AF.Copy)
                for i in range(4):
                    P = psum()
                    Pb = P.v.cast(BF16)
                    for j, src in enumerate((qr, kr, qd)):
                        S.do("pe", "transpose", out=Pb[:, j * 128:(j + 1) * 128], in_=src[:, i * 128:(i + 1) * 128],
                             identity=identb.v)
                    S.do("act", "activation", out=Fh[:, i, :], in_=Pb[:, 0:384], func=AF.Copy)
                for hg in range(2):
                    P = psum()
                    for hh in range(4):
                        h = hh * 2 + hg
                        i, pb = h // 2, 64 * (h % 2)
                        S.do("pe", "matmul", out=P[:, hh * 128:(hh + 1) * 128], lhsT=Fh[pb:pb + 64, i, 128:256],
                             rhs=Fh[pb:pb + 64, i, 0:128], start=True, stop=True)
                    S.do("dve", "tensor_tensor", out=AM[:, hg * 4:hg * 4 + 4, :],
                         in0=P.v.rr("p (h n) -> p h n", h=4), in1=DTb[:, hg * 4:hg * 4 + 4, :], op=ALU.mult)
                ss_ = [Sbd[(2 * t + q) % 3] for q in range(3)]
                for c in range(2):
                    cs = slice(c * 64, (c + 1) * 64)
                    sout = ss_[c + 1]
                    P = psum()
                    for h in range(8):
                        i, pb = h // 2, 64 * (h % 2)
                        S.do("pe", "matmul", out=P[pb:pb + 64, i * 64:(i + 1) * 64], lhsT=kd[cs, h * 64:(h + 1) * 64],
                             rhs=Vb2[cs, h * 64:(h + 1) * 64], start=True, stop=True)
                    S.do("dve", "tensor_tensor", out=S32.v, in0=S32.v, in1=sm("gam")[:, :, None].bc([128, 4, 64]),
                         op=ALU.mult)
                    S.do("dve", "tensor_tensor", out=S32.v, in0=S32.v, in1=P[:, 0:256].rr("p (i v) -> p i v", i=4),
                         op=ALU.add)
                    S.do("act", "activation", out=sout[0:64, :, 0:64], in_=S32[0:64], func=AF.Copy)
                    S.do("act", "activation", out=sout[64:128, :, 64:128], in_=S32[64:128], func=AF.Copy)
                P = psum()
                first = True
                for i in range(4):
                    for h in (2 * i, 2 * i + 1):
                        hc_ = slice(h * 64, (h + 1) * 64)
                        S.do("pe", "matmul", out=P[:, hc_], lhsT=AM[:, (h % 2) * 4 + h // 2, :], rhs=Vb2[:, hc_],
                             start=first, stop=False, skip_group_check=True)
                        first = False
                    for c in range(2):
                        S.do("pe", "matmul", out=P[c * 64:(c + 1) * 64, i * 128:(i + 1) * 128],
                             lhsT=Fh[:, i, 256 + c * 64:256 + (c + 1) * 64], rhs=ss_[c][:, i, :],
                             start=False, stop=(c == 1), skip_group_check=True)
                yq = TA[2]
                S.do("act", "activation", out=yq.v, in_=P.v, func=AF.Copy)
                if t == 0:
                    dbg("yret_raw", yq.v, [128, RW])
                yn2 = TA[3]
                head_norm(yn2.v, yq.v, EPS, TA[4].v)
                S.do("dve", "tensor_tensor", out=yn2.v, in0=yn2.v, in1=gng.v, op=ALU.mult)
                sg_ = TA[5]
                S.do("act", "activation", out=sg_.v, in_=zg.v, func=AF.Tanh, scale=0.5)
                S.do("dve", "scalar_tensor_tensor", out=sg_.v, in0=sg_.v, scalar=1.0, in1=zg.v, op0=ALU.add, op1=ALU.mult)
                S.do("dve", "scalar_tensor_tensor", out=ym[:, 512:1024], in0=yn2.v, scalar=0.5, in1=sg_.v,
                     op0=ALU.mult, op1=ALU.mult)
                if t == 0:
                    dbg("yret", ym[:, 512:1024], [128, RW])
                P = psum()
                Pb = P.v.cast(BF16)
                for k in range(KT):
                    S.do("pe", "transpose", out=Pb[:, k * 128:(k + 1) * 128], in_=ym[:, k * 128:(k + 1) * 128],
                         identity=identb.v)
                S.do("act", "activation", out=ymT.v.rr("p k t -> p (k t)"), in_=Pb[:, 0:1024], func=AF.Copy)
                for c2 in range(2):
                    cs_ = slice(c2 * 512, (c2 + 1) * 512)
                    P = psum()
                    for k in range(KT):
                        S.do("pe", "matmul", out=P.v, lhsT=ymT[:, k, :], rhs=Wo[:, k, cs_],
                             start=(k == 0), stop=(k == KT - 1))
                    S.do("dve", "tensor_tensor", out=TA[c2].v, in0=P.v, in1=gtA[:, cs_], op=ALU.mult)
                    S.do("pool", "tensor_tensor", out=xb[:, cs_], in0=xb[:, cs_], in1=TA[c2].v, op=ALU.add)
                S.dma("sp", x1s[t * 128:(t + 1) * 128, :], xb.v)
                if t == 0:
                    dbg("x1", xb.v, [128, D])
            S.barrier()
            S.emit()

        with contextlib.ExitStack() as stB:
            WuB = sb(stB, "WuB", [128, KT - KA, 2 * DFF], BF16)
            Wd = sb(stB, "Wd", [128, NJ, D], BF16)
            wdn_v = wdn_d.rearrange("(j p) c -> p j c", p=128)
            for k in range(KA, KT):
                for c0 in range(0, 2 * DFF, 1408):
                    S.dma("pool", WuB[:, k - KA, c0:c0 + 1408], wup_v[:, k, c0:c0 + 1408])
            Wu_k = lambda k, c0: (WuA[:, k, c0:c0 + 128] if k < KA else WuB[:, k - KA, c0:c0 + 128])
            fgb = sb(stB, "fgb", [128, D])
            S.dma("sp", fgb.v, fg_d.partition_broadcast(128))
            with contextlib.ExitStack() as stB0:
                gtF = sb(stB0, "gtF", [128, D])
                sc, scB = silu_c(stB0)
                wa = [sb(stB0, "wab%d" % i, [128, KT, 256]) for i in range(2)]
                bbc = [sb(stB0, "bbcb%d" % i, [128, 256]) for i in range(2)]
                for ci in range(20, 24):
                    mod_chunk(ci, wa, bbc, sc, scB, None, gtF)
                wst = [sb(stB0, "wst%d" % i, [128, D]) for i in range(2)]
                for j in range(NJ):
                    w_ = wst[j % 2]
                    S.dma("sp", w_.v, wdn_v[:, j, :])
                    S.do("dve" if j % 2 == 0 else "pool", "tensor_tensor", out=Wd[:, j, :], in0=w_.v, in1=gtF.v,
                         op=ALU.mult)
                S.barrier()
                S.emit()
            cw = sb(stB, "cw", [128, 2 * NJ, 3])
            cb = sb(stB, "cb", [128, 2 * NJ])
            S.dma("sp", cw.v, cwT_d.rearrange("p (j a) -> p j a", a=3))
            S.dma("sp", cb.v, cbT_d)
            xg = [sb(stB, "xg%d" % i, [128, D]) for i in range(2)]
            xr = [sb(stB, "xr%d" % i, [128, D]) for i in range(2)]
            xn2 = sb(stB, "xn2", [128, D])
            h2T = sb(stB, "h2T", [128, KT, 512], BF16)
            actT = sb(stB, "actT", [128, NJ, 512], BF16)
            acc = [sb(stB, "acc%d" % a, [128, 512]) for a in range(2)]
            corr = sb(stB, "corr", [128, 2 * NJ, 2])
            junkb = sb(stB, "junkb", [128, D], BF16)
            junk2 = junkb.v
            tail = sb(stB, "tail", [128, 2 * NJ, 2])
            st2 = [sb(stB, "st2_%d" % i, [128, 2]) for i in range(4)]
            S.do("pool", "memset", ap=tail.v, constant=0.0)
            nb = 0
            for g in range(ngroups):
                for tt in range(4):
                    xb = xg[nb % 2]
                    st_ = st2[nb % 2]
                    nb += 1
                    S.dma("sp", xb.v, x1s[(g * 4 + tt) * 128:(g * 4 + tt + 1) * 128, :])
                    S.do("act", "activation", out=junk2, in_=xb.v, func=AF.Square, accum_out=st_[:, 0:1])
                    S.do("dve", "tensor_scalar", out=st_[:, 0:1], in0=st_[:, 0:1], scalar1=1.0 / D, scalar2=EPS,
                         op0=ALU.mult, op1=ALU.add)
                    rsqrt_small(st_[:, 1:2], st_[:, 0:1])
                    S.do("act", "activation", out=xn2.v, in_=xb.v, func=AF.Copy, scale=st_[:, 1:2])
                    for half in range(2):
                        P = psum()
                        for kk_ in range(4):
                            k = half * 4 + kk_
                            S.do("pe", "transpose", out=P[:, kk_ * 128:(kk_ + 1) * 128],
                                 in_=xn2[:, k * 128:(k + 1) * 128], identity=sm("ident"))
                        for kk_ in range(4):
                            k = half * 4 + kk_
                            S.do("act", "activation", out=h2T[:, k, tt * 128:(tt + 1) * 128],
                                 in_=P[:, kk_ * 128:(kk_ + 1) * 128], func=AF.Identity,
                                 scale=gscF[:, k:k + 1], bias=modT[:, 24 + k:25 + k])
                if g == 0:
                    dbg("h2T", h2T.v, [128, KT, 512])
                S.do("dve", "tensor_tensor", out=corr[:, :, 0:1], in0=tail[:, :, 1:2], in1=cw[:, :, 1:2], op=ALU.mult)
                S.do("dve", "tensor_tensor", out=corr[:, :, 1:2], in0=tail[:, :, 0:1], in1=cw[:, :, 0:1], op=ALU.mult)
                S.do("dve", "tensor_tensor", out=corr[:, :, 0:1], in0=corr[:, :, 0:1], in1=corr[:, :, 1:2], op=ALU.add)
                S.do("dve", "tensor_tensor", out=corr[:, :, 1:2], in0=tail[:, :, 1:2], in1=cw[:, :, 0:1], op=ALU.mult)
                for j in range(NJ):
                    Pj = []
                    for a in range(2):
                        col = a * NJ + j
                        c0 = a * DFF + j * 128
                        P = psum()
                        Pj.append(P)
                        for k in range(KT):
                            S.do("pe", "matmul", out=P.v, lhsT=Wu_k(k, c0), rhs=h2T[:, k, :],
                                 start=(k == 0), stop=(k == KT - 1))
                        ac = acc[a]
                        S.do("act", "activation", out=ac.v, in_=P.v, func=AF.Identity, scale=cw[:, col, 2:3],
                             bias=cb[:, col:col + 1])
                        S.do("dve", "scalar_tensor_tensor", out=ac[:, 1:512], in0=P[:, 0:511], scalar=cw[:, col, 1:2],
                             in1=ac[:, 1:512], op0=ALU.mult, op1=ALU.add)
                        S.do("dve", "scalar_tensor_tensor", out=ac[:, 2:512], in0=P[:, 0:510], scalar=cw[:, col, 0:1],
                             in1=ac[:, 2:512], op0=ALU.mult, op1=ALU.add)
                        S.do("dve", "tensor_tensor", out=ac[:, 0:2], in0=ac[:, 0:2], in1=corr[:, col, :], op=ALU.add)
                        S.do("act", "activation", out=tail[:, col, :], in_=P[:, 510:512], func=AF.Copy)
                    av, ag = acc
                    Pv_, Pg_ = Pj
                    S.do("act", "activation", out=Pg_.v, in_=ag.v, func=AF.Tanh, scale=0.5)
                    S.do("dve", "scalar_tensor_tensor", out=Pv_.v, in0=Pg_.v, scalar=1.0, in1=ag.v,
                         op0=ALU.add, op1=ALU.mult)
                    S.do("dve", "scalar_tensor_tensor", out=actT[:, j, :], in0=av.v, scalar=0.5, in1=Pv_.v,
                         op0=ALU.mult, op1=ALU.mult)
                if g == 0:
                    dbg("actT", actT.v, [128, NJ, 512])
                for tt in range(4):
                    o_ = xr[tt % 2]
                    st_ = st2[2 + tt % 2]
                    S.dma("sp", o_.v, x1s[(g * 4 + tt) * 128:(g * 4 + tt + 1) * 128, :])
                    for c2 in range(2):
                        P = psum()
                        for j in range(NJ):
                            S.do("pe", "matmul", out=P.v, lhsT=actT[:, j, tt * 128:(tt + 1) * 128],
                                 rhs=Wd[:, j, c2 * 512:(c2 + 1) * 512], start=(j == 0), stop=(j == NJ - 1))
                        cs_ = slice(c2 * 512, (c2 + 1) * 512)
                        S.do("dve", "tensor_tensor", out=o_[:, cs_], in0=P.v, in1=o_[:, cs_], op=ALU.add)
                    S.do("act", "activation", out=junk2, in_=o_.v, func=AF.Square, accum_out=st_[:, 0:1])
                    S.do("dve", "tensor_scalar", out=st_[:, 0:1], in0=st_[:, 0:1], scalar1=1.0 / D, scalar2=EPS,
                         op0=ALU.mult, op1=ALU.add)
                    rsqrt_small(st_[:, 1:2], st_[:, 0:1])
                    S.do("dve", "scalar_tensor_tensor", out=o_.v, in0=o_.v, scalar=st_[:, 1:2], in1=fgb.v,
                         op0=ALU.mult, op1=ALU.mult)
                    S.dma("sp", out_d[(g * 4 + tt) * 128:(g * 4 + tt + 1) * 128, :], o_.v)
            S.barrier()
            S.emit()
    return nc, dbg_outs


_CACHE = {}


def make_in_maps(inp):
    f = lambda a: np.ascontiguousarray(np.asarray(a, dtype=np.float32))
    cst = _consts()
    x = f(inp["x"])
    c = f(inp["c"])
    shared = {
        "w_ada": f(inp["w_ada"][0]),
        "b_adaT": f(f(inp["b_ada"][0]).reshape(48, 128).T),
        "b_ada": f(inp["b_ada"][0]).reshape(1, -1),
        "gTa": f(f(inp["attn_norm_g"][0]).reshape(KT, 128).T),
        "gTf": f(f(inp["ffn_norm_g"][0]).reshape(KT, 128).T),
        "fg": f(inp["final_norm_g"]).reshape(1, -1),
        "w_in": f(inp["w_in"][0]),
        "mu": f(inp["rwkv_mu"][0]).reshape(1, -1),
        "w2x": f(np.concatenate([f(inp["rwkv_w2"][0]), f(inp["rwkv_w0"][0])[None, :]], axis=0)),
        "a2x": f(np.concatenate([f(inp["rwkv_a2"][0]), f(inp["rwkv_a0"][0])[None, :]], axis=0)),
        "g2": f(inp["rwkv_g2"][0]),
        "pvec": f(np.stack([f(inp["rwkv_k_k"][0]), f(inp["rwkv_k_a"][0]), f(inp["rwkv_r_k"][0]).reshape(-1),
                            f(inp["rwkv_ln_g"][0]), f(inp["rwkv_ln_b"][0]), f(inp["ret_gn_g"][0])], axis=0)),
        "w_out": f(inp["w_out"][0]),
        "w_up": f(inp["ffn_w_up"][0]),
        "cwT": f(f(inp["ffn_conv_w"][0]).reshape(3, 2 * NJ, 128).transpose(2, 1, 0).reshape(128, -1)),
        "cbT": f(f(inp["ffn_conv_b"][0]).reshape(2 * NJ, 128).T),
        "w_down": f(inp["ffn_w_down"][0]),
        "small": cst["small"], "DT": cst["DT"], "rot": cst["rot"],
    }
    maps = []
    for b in range(NCORES):
        m = dict(shared)
        m["x"] = f(x[b])
        m["cT"] = f(c[b].reshape(KT, 128).T)
        maps.append(m)
    return maps


def kernel(**inputs):
    if "nc" not in _CACHE:
        _CACHE["nc"] = build()[0]
    nc = _CACHE["nc"]
    maps = make_in_maps(inputs)
    res = run_bass_kernel_spmd(nc, maps, core_ids=list(range(NCORES)))
    out = np.stack([np.asarray(r["out"], dtype=np.float32) for r in res.results], axis=0)
    return out
```

```python
import contextlib
import math
import numpy as np
import concourse.bass as bass
import concourse.mybir as mybir
from concourse.bass_utils import run_bass_kernel_spmd

F32 = mybir.dt.float32
BF16 = mybir.dt.bfloat16
AF = mybir.ActivationFunctionType
ALU = mybir.AluOpType
AX = mybir.AxisListType

NCORES = 8
S_LEN = 4096
D = 1024
NT = S_LEN // 128
KT = D // 128
DFF = 2816
NJ = DFF // 128
RW = 512
EPS = 1e-6
LN_EPS = 64e-5
WSC = math.exp(-0.5)


class View:
    __slots__ = ("buf", "ap")

    def __init__(self, buf, ap):
        self.buf = buf
        self.ap = ap

    def __getitem__(self, idx):
        return View(self.buf, self.ap[idx])

    def rr(self, pat, **kw):
        return View(self.buf, self.ap.rearrange(pat, **kw))

    def bc(self, shape):
        return View(self.buf, self.ap.broadcast_to(list(shape)))

    def cast(self, dt):
        return View(self.buf, self.ap.bitcast(dt))


class Buf:
    __slots__ = ("name", "t", "writer", "readers", "dsem", "dcnt", "dram")

    def __init__(self, name, t, dram=False):
        self.name = name
        self.t = t
        self.dram = dram
        self.writer = None
        self.readers = []
        self.dsem = None
        self.dcnt = 0

    def __getitem__(self, idx):
        return View(self, self.t[idx])

    @property
    def v(self):
        return View(self, self.t[:])


class Sched:
    ENGS = ("pe", "act", "dve", "pool", "sp")
    WKEYS = ("out", "accum_out", "ap")

    def __init__(self, nc, same_engine_raw=True):
        self.nc = nc
        self.sem = {}
        self.cnt = {e: 0 for e in self.ENGS}
        self.seen = {e: {} for e in self.ENGS}
        self.same_engine_raw = same_engine_raw
        self.q = {e: [] for e in self.ENGS}
        self.dbufs = []
        self.ninst = 0
        self.nwaits = 0

    def open(self, stack):
        self.stack = stack
        for e in self.ENGS:
            self.sem[e] = stack.enter_context(self.nc.semaphore("s_" + e))

    def _emit_waits(self, e, waits):
        seen = self.seen[e]
        for sem, val in waits:
            k = id(sem)
            if seen.get(k, 0) >= val:
                continue
            seen[k] = val
            self.q[e].append(("wait", sem, val))
            self.nwaits += 1

    def op(self, e, fn, reads, writes):
        waits = []
        for b in reads:
            w = b.writer
            if w is not None and (w[2] != e or self.same_engine_raw):
                waits.append(w[:2])
        for b in writes:
            w = b.writer
            if w is not None and w[2] != e:
                waits.append(w[:2])
            for rd in b.readers:
                if rd[2] != e:
                    waits.append(rd[:2])
        self._emit_waits(e, waits)
        self.cnt[e] += 1
        self.q[e].append(("op", fn, self.sem[e], 1))
        self.ninst += 1
        tok = (self.sem[e], self.cnt[e], e)
        for b in writes:
            b.writer = tok
            b.readers = []
        for b in reads:
            if b in writes:
                continue
            b.readers = [rd for rd in b.readers if rd[2] != e] + [tok]

    def do(self, e, method, **kw):
        reads, writes, real = [], [], {}
        for k, v in kw.items():
            if isinstance(v, View):
                (writes if k in self.WKEYS else reads).append(v.buf)
                real[k] = v.ap
            else:
                real[k] = v
        self.op(e, lambda eng: getattr(eng, method)(**real), reads, writes)

    def dma(self, e, out, in_, **kw):
        reads, writes = [], []
        owner = None
        if isinstance(in_, View):
            reads.append(in_.buf)
            if not in_.buf.dram:
                owner = in_.buf
            in_ = in_.ap
        if isinstance(out, View):
            writes.append(out.buf)
            if not out.buf.dram:
                owner = out.buf
            out = out.ap
        waits = []
        for b in reads:
            if b.writer is not None:
                waits.append(b.writer[:2])
        for b in writes:
            if b.writer is not None:
                waits.append(b.writer[:2])
            for rd in b.readers:
                waits.append(rd[:2])
        self._emit_waits(e, waits)
        if owner.dsem is None:
            owner.dsem = self.stack.enter_context(self.nc.semaphore("d%d_%s" % (len(self.dbufs), owner.name)))
            self.dbufs.append(owner)
        owner.dcnt += 16
        self.q[e].append(("op", (lambda eng, o=out, i=in_, kw=kw: eng.dma_start(out=o, in_=i, **kw)),
                          owner.dsem, 16))
        self.ninst += 1
        tok = (owner.dsem, owner.dcnt, "dma")
        for b in writes:
            b.writer = tok
            b.readers = []
        for b in reads:
            b.readers = b.readers + [tok]

    def barrier(self):
        for e in self.ENGS:
            waits = [(self.sem[o], self.cnt[o]) for o in self.ENGS if o != e and self.cnt[o] > 0]
            waits += [(b.dsem, b.dcnt) for b in self.dbufs]
            self._emit_waits(e, waits)

    def emit(self):
        def replay(q, eng):
            for it in q:
                if it[0] == "wait":
                    eng.wait_ge(it[1], it[2])
                else:
                    it[1](eng).then_inc(it[2], it[3])
        q = self.q
        self.q = {e: [] for e in self.ENGS}
        with self.nc.Block() as block:
            @block.tensor
            def _(eng):
                replay(q["pe"], eng)

            @block.scalar
            def _(eng):
                replay(q["act"], eng)

            @block.vector
            def _(eng):
                replay(q["dve"], eng)

            @block.gpsimd
            def _(eng):
                replay(q["pool"], eng)

            @block.sync
            def _(eng):
                replay(q["sp"], eng)


def _consts():
    idx = np.arange(128)
    ch = idx // 64
    same = ch[:, None] == ch[None, :]
    UI = (same & (idx[:, None] <= idx[None, :])).astype(np.float32)
    US = (same & (idx[:, None] < idx[None, :])).astype(np.float32)
    LS = (same & (idx[:, None] > idx[None, :])).astype(np.float32)
    sel2 = np.stack([(ch == 0), (ch == 1)], axis=1).astype(np.float32)
    maskA = np.concatenate([US, UI, US, UI], axis=1)
    LS4 = np.concatenate([LS] * 4, axis=1)
    ident = np.eye(128, dtype=np.float32)
    H = 8
    lg = np.log1p(-(2.0 ** (-5.0 - np.arange(H, dtype=np.float32)))).astype(np.float32)
    li = (idx % 64).astype(np.float32)
    dist = np.abs(li[:, None] - li[None, :])
    DT = np.zeros((128, H, 128), np.float32)
    for h in range(H):
        DT[:, (h % 2) * 4 + h // 2, :] = np.where(same, np.exp(lg[h] * dist), 0.0)
    qdec = np.exp(lg[None, :] * (li[:, None] + 1.0)).astype(np.float32)
    kdec = np.exp(lg[None, :] * (63.0 - li[:, None])).astype(np.float32)
    g64 = np.exp(lg * 64.0).astype(np.float32)
    gam = np.zeros((128, 4), np.float32)
    for i in range(4):
        gam[0:64, i] = g64[2 * i]
        gam[64:128, i] = g64[2 * i + 1]
    pos = np.arange(S_LEN, dtype=np.float32)
    inv = (10000.0 ** (-np.arange(0, 64, 2, dtype=np.float32) / 64)).astype(np.float32)
    ang = (pos[:, None] * inv[None, :]).astype(np.float32)
    c, s = np.cos(ang).astype(np.float32), np.sin(ang).astype(np.float32)
    CC = np.concatenate([c, c], axis=1)
    SS = np.concatenate([-s, s], axis=1)
    rot = np.stack([CC * 0.125, SS * 0.125, CC, SS], axis=1).astype(np.float32)
    rot = rot.reshape(NT, 128, 256)
    small = np.concatenate([UI, US, LS, ident, maskA, LS4, sel2, qdec, kdec, gam], axis=1)
    return dict(small=np.ascontiguousarray(small), DT=np.ascontiguousarray(DT.reshape(128, 1024)),
                rot=np.ascontiguousarray(rot))


SM_OFF = {}
_o = 0
for _n, _w in (("UI", 128), ("US", 128), ("LS", 128), ("ident", 128), ("maskA", 512), ("LS4", 512),
               ("sel2", 2), ("qdec", 8), ("kdec", 8), ("gam", 4)):
    SM_OFF[_n] = (_o, _o + _w)
    _o += _w
SM_W = _o


def build(debug=False, ntiles=NT, ngroups=NT // 4, stage=99):
    nc = bass.Bass("TRN2", target_bir_lowering=False)

    def din(name, shape):
        return nc.dram_tensor(name, list(shape), F32, kind="ExternalInput").ap()

    x_d = din("x", [S_LEN, D])
    cT_d = din("cT", [128, KT])
    wada_d = din("w_ada", [D, 6 * D])
    badaT_d = din("b_adaT", [128, 48])
    bada_d = din("b_ada", [1, 6 * D])
    gTa_d = din("gTa", [128, KT])
    gTf_d = din("gTf", [128, KT])
    fg_d = din("fg", [1, D])
    win_d = din("w_in", [D, 3840])
    mu_d = din("mu", [1, 1792])
    w2x_d = din("w2x", [65, RW])
    a2x_d = din("a2x", [65, RW])
    g2_d = din("g2", [128, RW])
    pvec_d = din("pvec", [6, RW])
    wout_d = din("w_out", [D, D])
    wup_d = din("w_up", [D, 2 * DFF])
    cwT_d = din("cwT", [128, 2 * NJ * 3])
    cbT_d = din("cbT", [128, 2 * NJ])
    wdn_d = din("w_down", [DFF, D])
    small_d = din("small", [128, SM_W])
    DT_d = din("DT", [128, 1024])
    rot_d = din("rot", [NT, 128, 256])
    out_d = nc.dram_tensor("out", [S_LEN, D], F32, kind="ExternalOutput").ap()
    x1s_t = nc.dram_tensor("x1s", [S_LEN, D], F32, kind="Internal")
    dbg_outs = {}

    with contextlib.ExitStack() as st0:
        S = Sched(nc)
        S.open(st0)

        name_ctr = [0]

        def sb(stack, name, shape, dt=F32):
            name_ctr[0] += 1
            return Buf(name, stack.enter_context(nc.sbuf_tensor("sb%d_%s" % (name_ctr[0], name), list(shape), dt)))

        def dbg(name, view, shape):
            if not debug or name in dbg_outs:
                return
            t = nc.dram_tensor("dbg_" + name, list(shape), F32, kind="ExternalOutput").ap()
            dbg_outs[name] = t
            S.dma("pool", t, view)

        PS = [Buf("ps%d" % i, st0.enter_context(nc.psum_tensor("ps%d" % i, [128, 512], F32))) for i in range(8)]
        ps_rr = [0]

        def psum():
            b = PS[ps_rr[0] % 8]
            ps_rr[0] += 1
            return b

        x1s = Buf("x1s", x1s_t.ap(), dram=True)

        small = sb(st0, "small", [128, SM_W])
        S.dma("sp", small.v, small_d)

        def sm(name):
            a, b = SM_OFF[name]
            return small[:, a:b]

        identb = sb(st0, "identb", [128, 128], BF16)
        S.do("dve", "tensor_copy", out=identb.v, in_=sm("ident"))
        modT = sb(st0, "modT", [128, 48])
        gscA = sb(st0, "gscA", [128, KT])
        gscF = sb(st0, "gscF", [128, KT])
        gtA = sb(st0, "gtA", [128, D])
        mhalf = sb(st0, "mhalf", [128, 8])
        S.do("pool", "memset", ap=mhalf.v, constant=-0.5)

        def rsqrt_small(dst, src):
            n = src.ap.shape[-1]
            S.do("pool", "tensor_tensor", out=dst, in0=src, in1=mhalf[:, 0:n], op=ALU.pow)

        hview = lambda v: v.rr("p (h d) -> p h d", h=8)
        wada_v = wada_d.rearrange("(k p) c -> p k c", p=128)
        win_v = win_d.rearrange("(k p) c -> p k c", p=128)
        wout_v = wout_d.rearrange("(k p) c -> p k c", p=128)

        def silu_c(stack):
            cT = sb(stack, "cT", [128, KT])
            sc = sb(stack, "sc", [128, KT])
            scB = sb(stack, "scB", [128, KT, 128])
            S.dma("sp", cT.v, cT_d)
            S.do("act", "activation", out=sc.v, in_=cT.v, func=AF.Tanh, scale=0.5)
            S.do("dve", "scalar_tensor_tensor", out=sc.v, in0=sc.v, scalar=1.0, in1=cT.v,
                 op0=ALU.add, op1=ALU.mult)
            S.do("dve", "tensor_scalar", out=sc.v, in0=sc.v, scalar1=0.5, scalar2=None, op0=ALU.mult)
            S.do("dve", "tensor_copy", out=scB.v, in_=sc[:, :, None].bc([128, KT, 128]))
            return sc, scB

        def mod_chunk(ci, wa, bbc, sc, scB, badaT, gt_dst):
            w_ = wa[ci % 2]
            S.dma("sp", w_.v, wada_v[:, :, ci * 256:(ci + 1) * 256])
            P = psum()
            if gt_dst is not None:
                b_ = bbc[ci % 2]
                S.dma("sp", b_.v, bada_d[:, ci * 256:(ci + 1) * 256].partition_broadcast(128))
                for k in range(KT):
                    S.do("pe", "matmul", out=P[:, 0:256], lhsT=scB[:, k, :], rhs=w_[:, k, :],
                         start=(k == 0), stop=(k == KT - 1))
                c0 = (ci % 4) * 256
                S.do("dve", "tensor_tensor", out=gt_dst[:, c0:c0 + 256], in0=P[:, 0:256], in1=b_.v, op=ALU.add)
            else:
                for nt_ in range(2):
                    for k in range(KT):
                        S.do("pe", "matmul", out=P[:, nt_:nt_ + 1],
                             lhsT=w_[:, k, nt_ * 128:(nt_ + 1) * 128], rhs=sc[:, k:k + 1],
                             start=(k == 0), stop=(k == KT - 1))
                S.do("dve", "tensor_tensor", out=modT[:, ci * 2:ci * 2 + 2], in0=P[:, 0:2],
                     in1=badaT[:, ci * 2:ci * 2 + 2], op=ALU.add)

        class NormT:
            def __init__(self, stack, tag, gsc, sh_col0):
                self.xt = [sb(stack, tag + "xt%d" % i, [128, D]) for i in range(2)]
                self.xn = sb(stack, tag + "xn", [128, D])
                self.junk = sb(stack, tag + "junk", [128, D], BF16)
                self.hT = [sb(stack, tag + "hT%d" % i, [128, KT, 129], BF16) for i in range(2)]
                self.st = [sb(stack, tag + "nst%d" % i, [128, 2]) for i in range(2)]
                self.gsc, self.sh0 = gsc, sh_col0
                S.do("pool", "memset", ap=self.hT[1][:, :, 128:129], constant=0.0)

            def run(self, t, src_rows):
                xb, hc, hp = self.xt[t % 2], self.hT[t % 2], self.hT[(t + 1) % 2]
                st_ = self.st[t % 2]
                S.dma("sp", xb.v, src_rows)
                S.do("act", "activation", out=self.junk.v, in_=xb.v, func=AF.Square, accum_out=st_[:, 0:1])
                S.do("dve", "tensor_scalar", out=st_[:, 0:1], in0=st_[:, 0:1], scalar1=1.0 / D, scalar2=EPS,
                     op0=ALU.mult, op1=ALU.add)
                rsqrt_small(st_[:, 1:2], st_[:, 0:1])
                S.do("act", "activation", out=self.xn.v, in_=xb.v, func=AF.Copy, scale=st_[:, 1:2])
                S.do("pool", "tensor_copy", out=hc[:, :, 0:1], in_=hp[:, :, 128:129])
                for half in range(2):
                    P = psum()
                    for kk_ in range(4):
                        k = half * 4 + kk_
                        S.do("pe", "transpose", out=P[:, kk_ * 128:(kk_ + 1) * 128],
                             in_=self.xn[:, k * 128:(k + 1) * 128], identity=sm("ident"))
                    for kk_ in range(4):
                        k = half * 4 + kk_
                        S.do("act", "activation", out=hc[:, k, 1:129], in_=P[:, kk_ * 128:(kk_ + 1) * 128],
                             func=AF.Identity, scale=self.gsc[:, k:k + 1],
                             bias=modT[:, self.sh0 + k:self.sh0 + k + 1])
                return xb, hc

        st1 = [sb(st0, "st1_%d" % i, [128, 8]) for i in range(12)]
        st_i = [0]

        def stat():
            b = st1[st_i[0] % len(st1)]
            st_i[0] += 1
            return b

        def head_norm(dst, src, eps, sq):
            s1, s2, mean, var = stat(), stat(), stat(), stat()
            S.do("dve", "tensor_reduce", out=s1.v, in_=hview(src), axis=AX.X, op=ALU.add)
            S.do("act", "activation", out=sq, in_=src, func=AF.Square)
            S.do("dve", "tensor_reduce", out=s2.v, in_=hview(sq), axis=AX.X, op=ALU.add)
            S.do("dve", "tensor_scalar", out=mean.v, in0=s1.v, scalar1=1.0 / 64, scalar2=None, op0=ALU.mult)
            S.do("dve", "tensor_tensor", out=var.v, in0=mean.v, in1=mean.v, op=ALU.mult)
            S.do("dve", "scalar_tensor_tensor", out=var.v, in0=s2.v, scalar=1.0 / 64, in1=var.v,
                 op0=ALU.mult, op1=ALU.subtract)
            S.do("dve", "tensor_scalar", out=var.v, in0=var.v, scalar1=eps, scalar2=None, op0=ALU.add)
            rs = stat()
            rsqrt_small(rs.v, var.v)
            S.do("dve", "tensor_tensor", out=hview(dst), in0=hview(src),
                 in1=mean[:, :, None].bc([128, 8, 64]), op=ALU.subtract)
            S.do("dve", "tensor_tensor", out=hview(dst), in0=hview(dst),
                 in1=rs[:, :, None].bc([128, 8, 64]), op=ALU.mult)

        def proj_T(dst, hc, W, c0, W2=None):
            P = psum()
            n = 2 * KT if W2 is not None else KT
            i = 0
            for k in range(KT):
                S.do("pe", "matmul", out=P.v, lhsT=hc[:, k, 1:129], rhs=W[:, k, c0:c0 + 512],
                     start=(i == 0), stop=(i == n - 1))
                i += 1
            if W2 is not None:
                for k in range(KT):
                    S.do("pe", "matmul", out=P.v, lhsT=hc[:, k, 0:128], rhs=W2[:, k, c0:c0 + 512],
                         start=False, stop=(i == n - 1))
                    i += 1
            S.do("act", "activation", out=dst, in_=P.v, func=AF.Copy)

        yrs = Buf("yrs", nc.dram_tensor("yrs", [S_LEN, RW], BF16, kind="Internal").ap(), dram=True)

        with contextlib.ExitStack() as stA:
            W1 = sb(stA, "W1", [128, KT, 1792], BF16)
            W2 = sb(stA, "W2", [128, KT, 1792], BF16)
            w2x = sb(stA, "w2x", [65, RW])
            a2x = sb(stA, "a2x", [65, RW])
            g2 = sb(stA, "g2", [128, RW])
            pv = [sb(stA, "pv%d" % i, [128, RW]) for i in range(5)]
            S.dma("sp", w2x.v, w2x_d)
            S.dma("sp", a2x.v, a2x_d)
            S.dma("sp", g2.v, g2_d)
            for i in range(5):
                S.dma("sp", pv[i].v, pvec_d[i:i + 1, :].partition_broadcast(128))

            with contextlib.ExitStack() as st00:
                mu_bc = sb(st00, "mu_bc", [128, 1792])
                omm = sb(st00, "omm", [128, 1792])
                S.dma("sp", mu_bc.v, mu_d.partition_broadcast(128))
                S.do("dve", "tensor_scalar", out=omm.v, in0=mu_bc.v, scalar1=-1.0, scalar2=1.0,
                     op0=ALU.mult, op1=ALU.add)
                stg = [sb(st00, "stg%d" % i, [128, 1792]) for i in range(2)]
                for k in range(KT):
                    sg = stg[k % 2]
                    S.dma("sp", sg.v, win_v[:, k, 0:1792])
                    S.do("dve", "tensor_tensor", out=W1[:, k, :], in0=sg.v, in1=omm.v, op=ALU.mult)
                    S.do("pool", "tensor_tensor", out=W2[:, k, :], in0=sg.v, in1=mu_bc.v, op=ALU.mult)
                sc, scB = silu_c(st00)
                badaT = sb(st00, "badaT", [128, 48])
                gTa = sb(st00, "gTa", [128, KT])
                gTf = sb(st00, "gTf", [128, KT])
                S.dma("sp", badaT.v, badaT_d)
                S.dma("sp", gTa.v, gTa_d)
                S.dma("sp", gTf.v, gTf_d)
                wa = [sb(st00, "wa%d" % i, [128, KT, 256]) for i in range(2)]
                bbc = [sb(st00, "bbc%d" % i, [128, 256]) for i in range(2)]
                for ci in range(24):
                    if 8 <= ci < 12:
                        mod_chunk(ci, wa, bbc, sc, scB, badaT, gtA)
                    elif ci >= 20:
                        continue
                    else:
                        mod_chunk(ci, wa, bbc, sc, scB, badaT, None)
                S.do("dve", "scalar_tensor_tensor", out=gscA.v, in0=modT[:, 8:16], scalar=1.0, in1=gTa.v,
                     op0=ALU.add, op1=ALU.mult)
                S.do("dve", "scalar_tensor_tensor", out=gscF.v, in0=modT[:, 32:40], scalar=1.0, in1=gTf.v,
                     op0=ALU.add, op1=ALU.mult)
                dbg("modT", modT.v, [128, 48])
                dbg("gtA", gtA.v, [128, D])
                S.barrier()
                S.emit()

            NA = NormT(stA, "a1", gscA, 0)
            Z = [sb(stA, "Z%d" % i, [128, RW]) for i in range(3)]
            TA = [sb(stA, "TA%d" % i, [128, RW]) for i in range(10)]
            TB = [sb(stA, "TB%d" % i, [128, RW], BF16) for i in range(9)]
            lo_w = sb(stA, "lo_w", [65, 128])
            lo_a = sb(stA, "lo_a", [65, 128])
            lo_g = sb(stA, "lo_g", [128, 128])
            S.do("pool", "memset", ap=lo_w.v, constant=1.0)
            S.do("pool", "memset", ap=lo_a.v, constant=1.0)
            WC = sb(stA, "WC", [128, 4, 2])
            Fh = sb(stA, "Fh", [128, 4, 512], BF16)
            AM = sb(stA, "AM", [128, 8, 512], BF16)
            Lp = [sb(stA, "Lp%d" % i, [128, 8, 128], BF16) for i in range(2)]
            Np = [sb(stA, "Np%d" % i, [128, 8, 128], BF16) for i in range(2)]
            X = [sb(stA, "X%d" % i, [128, 8, 128], BF16) for i in range(2)]
            RhT = sb(stA, "RhT", [128, 4, 128], BF16)
            MpT = sb(stA, "MpT", [128, 4, 2, 128], BF16)
            H32 = sb(stA, "H32", [128, 4, 64])
            Hc = [sb(stA, "Hc%d" % i, [128, 4, 64], BF16) for i in range(3)]
            Hbd = [sb(stA, "Hbd%d" % i, [128, 4, 128], BF16) for i in range(3)]
            slot = lambda h: (h % 2) * 4 + h // 2
            yrw = [sb(stA, "yrw%d" % i, [128, RW], BF16) for i in range(2)]
            S.do("pool", "memset", ap=H32.v, constant=0.0)
            S.do("pool", "memset", ap=Hc[0].v, constant=0.0)
            S.do("pool", "memset", ap=MpT.v, constant=0.0)
            for b_ in Hbd:
                S.do("pool", "memset", ap=b_.v, constant=0.0)

            def proj_F(hc, c0, m):
                P = psum()
                i = 0
                for W_, sl in ((W1, slice(1, 129)), (W2, slice(0, 128))):
                    for k in range(KT):
                        S.do("pe", "matmul", out=P[0:m, 0:128], lhsT=W_[:, k, c0:c0 + m], rhs=hc[:, k, sl],
                             start=(i == 0), stop=(i == 2 * KT - 1))
                        i += 1
                return P[0:m, 0:128]

            for t in range(ntiles):
                xb, hc = NA.run(t, x_d[t * 128:(t + 1) * 128, :])
                if t == 0:
                    dbg("hT", hc[:, :, 1:129], [128, KT, 128])
                if stage < 2:
                    continue
                zr, zk, zv = Z[0], Z[1], Z[2]
                proj_T(zr.v, hc, W1, 0, W2)
                proj_T(zk.v, hc, W1, 512, W2)
                proj_T(zv.v, hc, W1, 1024, W2)
                Pw = proj_F(hc, 1536, 64)
                S.do("act", "activation", out=lo_w[0:64, :], in_=Pw, func=AF.Tanh)
                Pa_ = proj_F(hc, 1600, 64)
                S.do("act", "activation", out=lo_a[0:64, :], in_=Pa_, func=AF.Copy)
                Pg = proj_F(hc, 1664, 128)
                S.do("act", "activation", out=lo_g.v, in_=Pg, func=AF.Tanh, scale=0.5)
                S.do("dve", "tensor_scalar", out=lo_g.v, in0=lo_g.v, scalar1=0.5, scalar2=0.5,
                     op0=ALU.mult, op1=ALU.add)
                if t == 0:
                    dbg("zr", zr.v, [128, RW])
                    dbg("zv", zv.v, [128, RW])
                if stage < 3:
                    continue
                lw, am1, gsb = TA[0], TA[1], TA[2]
                P = psum()
                S.do("pe", "matmul", out=P.v, lhsT=lo_w.v, rhs=w2x.v, start=True, stop=True)
                S.do("act", "activation", out=lw.v, in_=P.v, func=AF.Tanh, scale=0.5)
                S.do("dve", "tensor_scalar", out=lw.v, in0=lw.v, scalar1=1.0, scalar2=-0.5 * WSC,
                     op0=ALU.add, op1=ALU.mult)
                P = psum()
                S.do("pe", "matmul", out=P.v, lhsT=lo_a.v, rhs=a2x.v, start=True, stop=True)
                S.do("act", "activation", out=am1.v, in_=P.v, func=AF.Tanh, scale=0.5)
                S.do("dve", "tensor_scalar", out=am1.v, in0=am1.v, scalar1=0.5, scalar2=-0.5,
                     op0=ALU.mult, op1=ALU.add)
                P = psum()
                S.do("pe", "matmul", out=P.v, lhsT=lo_g.v, rhs=g2.v, start=True, stop=True)
                S.do("act", "activation", out=gsb.v, in_=P.v, func=AF.Copy)
                if t == 0:
                    dbg("lw", lw.v, [128, RW])
                    dbg("am1", am1.v, [128, RW])
                    dbg("gsb", gsb.v, [128, RW])
                if stage < 4:
                    continue
                Wc, Winv, Wprev, Ee = TA[3], TA[4], TA[5], TA[6]
                P = psum()
                S.do("pe", "matmul", out=P.v, lhsT=sm("UI"), rhs=lw.v, start=True, stop=True)
                S.do("act", "activation", out=Wc.v, in_=P.v, func=AF.Exp)
                S.do("act", "activation", out=Winv.v, in_=P.v, func=AF.Exp, scale=-1.0)
                P = psum()
                S.do("pe", "matmul", out=P.v, lhsT=sm("US"), rhs=lw.v, start=True, stop=True)
                S.do("act", "activation", out=Wprev.v, in_=P.v, func=AF.Exp)
                P = psum()
                S.do("pe", "matmul", out=P.v, lhsT=sm("LS"), rhs=lw.v, start=True, stop=True)
                S.do("act", "activation", out=Ee.v, in_=P.v, func=AF.Exp)
                P = psum()
                for i in range(4):
                    S.do("pe", "matmul", out=P[:, 2 * i:2 * i + 2], lhsT=lw[:, i * 128:(i + 1) * 128],
                         rhs=sm("sel2"), start=True, stop=True)
                S.do("act", "activation", out=WC.v, in_=P[:, 0:8].rr("p (i c) -> p i c", i=4), func=AF.Exp)
                if stage < 5:
                    continue
                kk0, k2, kka = TA[7], TA[8], TA[9]
                n2, rn = stat(), stat()
                S.do("dve", "tensor_tensor", out=kk0.v, in0=zk.v, in1=pv[0].v, op=ALU.mult)
                S.do("pool", "tensor_tensor", out=k2.v, in0=kk0.v, in1=kk0.v, op=ALU.mult)
                S.do("dve", "tensor_reduce", out=n2.v, in_=hview(k2.v), axis=AX.X, op=ALU.add)
                if stage < 5.1:
                    continue
                S.do("dve", "tensor_scalar", out=n2.v, in0=n2.v, scalar1=1e-24, scalar2=None, op0=ALU.max)
                rsqrt_small(rn.v, n2.v)
                if stage < 5.2:
                    continue
                S.do("dve", "tensor_tensor", out=hview(kk0.v), in0=hview(kk0.v),
                     in1=rn[:, :, None].bc([128, 8, 64]), op=ALU.mult)
                if stage < 5.3:
                    continue
                S.do("pool", "tensor_tensor", out=k2.v, in0=am1.v, in1=pv[1].v, op=ALU.mult)
                S.do("dve", "scalar_tensor_tensor", out=k2.v, in0=k2.v, scalar=1.0, in1=zk.v,
                     op0=ALU.add, op1=ALU.mult)
                S.do("dve", "scalar_tensor_tensor", out=kka.v, in0=am1.v, scalar=1.0, in1=kk0.v,
                     op0=ALU.add, op1=ALU.mult)
                if stage < 5.4:
                    continue
                at, bt, kt, rtl, Vb, Bh0, Bh1, Kh0, Kh1 = TB
                Bhc, Khc = (Bh0, Bh1), (Kh0, Kh1)
                S.do("dve", "scalar_tensor_tensor", out=at.v, in0=kk0.v, scalar=-1.0, in1=Wprev.v,
                     op0=ALU.mult, op1=ALU.mult)
                S.do("pool", "tensor_tensor", out=bt.v, in0=kka.v, in1=Winv.v, op=ALU.mult)
                S.do("dve", "tensor_tensor", out=kt.v, in0=k2.v, in1=Winv.v, op=ALU.mult)
                S.do("pool", "tensor_tensor", out=rtl.v, in0=zr.v, in1=Wc.v, op=ALU.mult)
                for c in range(2):
                    S.do("dve", "scalar_tensor_tensor", out=Bhc[c].v, in0=kka.v, scalar=sm("sel2")[:, c:c + 1], in1=Ee.v,
                         op0=ALU.mult, op1=ALU.mult)
                    S.do("dve", "scalar_tensor_tensor", out=Khc[c].v, in0=k2.v, scalar=sm("sel2")[:, c:c + 1], in1=Ee.v,
                         op0=ALU.mult, op1=ALU.mult)
                S.do("act", "activation", out=Vb.v, in_=zv.v, func=AF.Copy)
                if stage < 5.5:
                    continue
                bn = stat()
                S.do("pool", "tensor_tensor", out=Wc.v, in0=zr.v, in1=pv[2].v, op=ALU.mult)
                S.do("dve", "tensor_tensor", out=Wc.v, in0=Wc.v, in1=k2.v, op=ALU.mult)
                S.do("dve", "tensor_reduce", out=bn.v, in_=hview(Wc.v), axis=AX.X, op=ALU.add)
                if t == 0:
                    dbg("at", at.v, [128, RW])
                    dbg("bt", bt.v, [128, RW])
                    dbg("Kh1", Kh1.v, [128, RW])
                if stage < 6:
                    continue
                for i in range(4):
                    P = psum()
                    Pb = P.v.cast(BF16)
                    for j, src in enumerate((at, rtl, bt, kt)):
                        S.do("pe", "transpose", out=Pb[:, j * 128:(j + 1) * 128], in_=src[:, i * 128:(i + 1) * 128],
                             identity=identb.v)
                    S.do("act", "activation", out=Fh[:, i, :], in_=Pb[:, 0:512], func=AF.Copy)
                if stage < 7:
                    continue
                for h in range(8):
                    i, pb = h // 2, 64 * (h % 2)
                    P = psum()
                    S.do("pe", "matmul", out=P[:, 0:256], lhsT=Fh[pb:pb + 64, i, 256:384], rhs=Fh[pb:pb + 64, i, 0:256],
                         start=True, stop=True)
                    S.do("pe", "matmul", out=P[:, 256:512], lhsT=Fh[pb:pb + 64, i, 384:512], rhs=Fh[pb:pb + 64, i, 0:256],
                         start=True, stop=True)
                    S.do("dve", "tensor_tensor", out=AM[:, slot(h), :], in0=P.v, in1=sm("maskA"), op=ALU.mult)
                for hg in range(2):
                    P = psum()
                    for hh in range(4):
                        h = hh * 2 + hg
                        i, pb = h // 2, 64 * (h % 2)
                        S.do("pe", "matmul", out=P[:, hh * 128:(hh + 1) * 128], lhsT=Fh[pb:pb + 64, i, 0:128],
                             rhs=Fh[pb:pb + 64, i, 256:384], start=True, stop=True)
                    S.do("dve", "tensor_tensor", out=Lp[0][:, hg * 4:hg * 4 + 4, :].rr("p h s -> p (h s)"), in0=P.v,
                         in1=sm("LS4"), op=ALU.mult)
                if stage < 8:
                    continue
                P = psum()
                for h in range(8):
                    sl_ = slot(h)
                    S.do("pe", "matmul", out=P[:, sl_ * 64:(sl_ + 1) * 64], lhsT=AM[:, sl_, 256:384],
                         rhs=Vb[:, h * 64:(h + 1) * 64], start=True, stop=True)
                S.do("act", "activation", out=X[0][:, :, 64:128], in_=hview(P.v), func=AF.Copy)
                atv = at.v.rr("p (hh hg d) -> p hg hh d", hh=4, hg=2)
                for hg in range(2):
                    S.do("pool", "tensor_copy", out=X[0][:, hg * 4:hg * 4 + 4, 0:64], in_=atv[:, hg])
                if stage < 9:
                    continue
                xcur = 0
                for lv in range(6):
                    if lv == 0:
                        Ncur = lambda s_: AM[:, s_, 0:128]
                    else:
                        Ncur = (lambda s_, Nb=Np[lv % 2]: Nb[:, s_, :])
                    Lcur = Lp[lv % 2]
                    for hg in range(2):
                        P = psum()
                        for hh in range(4):
                            s_ = hg * 4 + hh
                            S.do("pe", "matmul", out=P[:, hh * 128:(hh + 1) * 128], lhsT=identb.v,
                                 rhs=X[xcur][:, s_, :], start=True, stop=False)
                            S.do("pe", "matmul", out=P[:, hh * 128:(hh + 1) * 128], lhsT=Ncur(s_),
                                 rhs=X[xcur][:, s_, :], start=False, stop=True)
                        S.do("act", "activation", out=X[1 - xcur][:, hg * 4:hg * 4 + 4, :].rr("p h s -> p (h s)"),
                             in_=P.v, func=AF.Copy)
                    xcur = 1 - xcur
                    if lv == 5:
                        break
                    for hg in range(2):
                        P = psum()
                        for hh in range(4):
                            s_ = hg * 4 + hh
                            S.do("pe", "matmul", out=P[:, hh * 128:(hh + 1) * 128], lhsT=Lcur[:, s_, :],
                                 rhs=Ncur(s_), start=True, stop=True)
                        S.do("dve", "tensor_copy", out=Np[(lv + 1) % 2][:, hg * 4:hg * 4 + 4, :].rr("p h s -> p (h s)"),
                             in_=P.v)
                    if lv < 4:
                        for hg in range(2):
                            P = psum()
                            for hh in range(4):
                                s_ = hg * 4 + hh
                                S.do("pe", "matmul", out=P[:, hh * 128:(hh + 1) * 128], lhsT=Ncur(s_),
                                     rhs=Lcur[:, s_, :], start=True, stop=True)
                            S.do("act", "activation",
                                 out=Lp[(lv + 1) % 2][:, hg * 4:hg * 4 + 4, :].rr("p h s -> p (h s)"),
                                 in_=P.v, func=AF.Copy)
                Xf = X[xcur]
                if t == 0:
                    dbg("Xf", Xf.v, [128, 8, 128])
                if stage < 10:
                    continue
                P = psum()
                for h in range(8):
                    i, pb, sl_ = h // 2, 64 * (h % 2), slot(h)
                    S.do("pe", "matmul", out=P[pb:pb + 64, i * 128:(i + 1) * 128], lhsT=Xf[:, sl_, 0:64],
                         rhs=AM[:, sl_, 128:256], start=True, stop=True)
                S.do("dve", "tensor_tensor", out=RhT.v, in0=P.v.rr("p (i t) -> p i t", i=4), in1=Fh[:, :, 128:256],
                     op=ALU.add)
                P = psum()
                for h in range(8):
                    i, pb, sl_ = h // 2, 64 * (h % 2), slot(h)
                    for c in range(2):
                        S.do("pe", "matmul", out=P[pb:pb + 64, (i * 2 + c) * 64:(i * 2 + c + 1) * 64],
                             lhsT=Xf[:, sl_, 0:64], rhs=Bhc[c][:, h * 64:(h + 1) * 64], start=True, stop=True)
                Pv_ = P.v.rr("p (i c j) -> p i c j", i=4, c=2)
                S.do("act", "activation", out=MpT[0:64, :, :, 0:64], in_=Pv_[0:64], func=AF.Copy)
                S.do("act", "activation", out=MpT[64:128, :, :, 64:128], in_=Pv_[64:128], func=AF.Copy)
                if stage < 11:
                    continue
                hs = [(Hc[(2 * t + q) % 3], Hbd[(2 * t + q) % 3]) for q in range(3)]
                for c in range(2):
                    (hin, _), (hout, hout_bd) = hs[c], hs[c + 1]
                    P = psum()
                    first = True
                    for i in range(4):
                        for h in (2 * i, 2 * i + 1):
                            pb, sl_ = 64 * (h % 2), slot(h)
                            o_ = P[pb:pb + 64, i * 64:(i + 1) * 64]
                            S.do("pe", "matmul", out=o_, lhsT=Bhc[c][:, h * 64:(h + 1) * 64], rhs=Xf[:, sl_, 64:128],
                                 start=(i == 0), stop=False, skip_group_check=True)
                            S.do("pe", "matmul", out=o_, lhsT=Khc[c][:, h * 64:(h + 1) * 64],
                                 rhs=Vb[:, h * 64:(h + 1) * 64], start=False, stop=False, skip_group_check=True)
                        S.do("pe", "matmul", out=P[:, i * 64:(i + 1) * 64], lhsT=MpT[:, i, c, :], rhs=hin[:, i, :],
                             start=False, stop=True, skip_group_check=True)
                    S.do("dve", "tensor_tensor", out=H32.v, in0=H32.v, in1=WC[:, :, c:c + 1].bc([128, 4, 64]),
                         op=ALU.mult)
                    S.do("dve", "tensor_tensor", out=H32.v, in0=H32.v, in1=P[:, 0:256].rr("p (i v) -> p i v", i=4),
                         op=ALU.add)
                    S.do("act", "activation", out=hout.v, in_=H32.v, func=AF.Copy)
                    S.do("act", "activation", out=hout_bd[0:64, :, 0:64], in_=H32[0:64], func=AF.Copy)
                    S.do("act", "activation", out=hout_bd[64:128, :, 64:128], in_=H32[64:128], func=AF.Copy)
                if stage < 12:
                    continue
                P = psum()
                first = True
                for i in range(4):
                    for h in (2 * i, 2 * i + 1):
                        sl_ = slot(h)
                        hc_ = slice(h * 64, (h + 1) * 64)
                        S.do("pe", "matmul", out=P[:, hc_], lhsT=AM[:, sl_, 128:256], rhs=Xf[:, sl_, 64:128],
                             start=first, stop=False, skip_group_check=True)
                        first = False
                        S.do("pe", "matmul", out=P[:, hc_], lhsT=AM[:, sl_, 384:512], rhs=Vb[:, hc_],
                             start=False, stop=False, skip_group_check=True)
                    for c in range(2):
                        S.do("pe", "matmul", out=P[c * 64:(c + 1) * 64, i * 128:(i + 1) * 128],
                             lhsT=RhT[:, i, c * 64:(c + 1) * 64], rhs=hs[c][1][:, i, :],
                             start=False, stop=(c == 1), skip_group_check=True)
                yr = TA[3]
                S.do("act", "activation", out=yr.v, in_=P.v, func=AF.Copy)
                if t == 0:
                    dbg("yraw", yr.v, [128, RW])
                yn = TA[4]
                head_norm(yn.v, yr.v, LN_EPS, TA[9].v)
                S.do("dve", "tensor_tensor", out=yn.v, in0=yn.v, in1=pv[3].v, op=ALU.mult)
                S.do("pool", "tensor_tensor", out=yn.v, in0=yn.v, in1=pv[4].v, op=ALU.add)
                S.do("dve", "tensor_tensor", out=hview(TA[5].v), in0=hview(zv.v), in1=bn[:, :, None].bc([128, 8, 64]),
                     op=ALU.mult)
                S.do("pool", "tensor_tensor", out=yn.v, in0=yn.v, in1=TA[5].v, op=ALU.add)
                yo = yrw[t % 2]
                S.do("dve", "tensor_tensor", out=yo.v, in0=yn.v, in1=gsb.v, op=ALU.mult)
                if t == 0:
                    dbg("yrwkv", yo.v, [128, RW])
                if stage < 12.5:
                    continue
                S.dma("sp", yrs[t * 128:(t + 1) * 128, :], yo.v)
            S.barrier()
            S.emit()

        with contextlib.ExitStack() as stA:
            Wr = sb(stA, "Wr", [128, KT, 2048], BF16)
            Wo = sb(stA, "Wo", [128, KT, D], BF16)
            gng = sb(stA, "gng", [128, RW])
            DTb = sb(stA, "DTb", [128, 8, 128])
            S.dma("sp", gng.v, pvec_d[5:6, :].partition_broadcast(128))
            S.dma("sp", DTb.v, DT_d.rearrange("p (h n) -> p h n", h=8))
            for k in range(KT):
                S.dma("pool", Wr[:, k, 0:1024], win_v[:, k, 1792:2816])
                S.dma("pool", Wr[:, k, 1024:2048], win_v[:, k, 2816:3840])
                S.dma("pool", Wo[:, k, :], wout_v[:, k, :])
            NA = NormT(stA, "a2", gscA, 0)
            Zs = [[sb(stA, "Zr%d_%d" % (p_, i), [128, RW]) for i in range(4)] for p_ in range(2)]
            TAs = [[sb(stA, "TAr%d_%d" % (p_, i), [128, RW]) for i in range(6)] for p_ in range(2)]
            TBs = [[sb(stA, "TBr%d_%d" % (p_, i), [128, RW], BF16) for i in range(5)] for p_ in range(2)]
            Fhs = [sb(stA, "Fhr%d" % p_, [128, 4, 384], BF16) for p_ in range(2)]
            AMs = [sb(stA, "AMr%d" % p_, [128, 8, 128], BF16) for p_ in range(2)]
            S32 = sb(stA, "S32", [128, 4, 64])
            Sbd = [sb(stA, "Sbd%d" % i, [128, 4, 128], BF16) for i in range(3)]
            rot = [sb(stA, "rot%d" % i, [128, 4, 64]) for i in range(2)]
            ymix = [sb(stA, "ymix%d" % i, [128, D], BF16) for i in range(2)]
            ymTs = [sb(stA, "ymT%d" % p_, [128, KT, 128], BF16) for p_ in range(2)]
            S.do("pool", "memset", ap=S32.v, constant=0.0)
            for b_ in Sbd:
                S.do("pool", "memset", ap=b_.v, constant=0.0)

            def a2_tile(t):
                Z, TA, TB, Fh, AM, ymT = Zs[t % 2], TAs[t % 2], TBs[t % 2], Fhs[t % 2], AMs[t % 2], ymTs[t % 2]
                ym = ymix[t % 2]
                S.dma("sp", ym[:, 0:512], yrs[t * 128:(t + 1) * 128, :])
                rt_ = rot[t % 2]
                S.dma("sp", rt_.v, rot_d[t].rearrange("p (a d) -> p a d", a=4))
                xb, hc = NA.run(t, x_d[t * 128:(t + 1) * 128, :])
                yield
                zq, zk2, zv2, zg = Z
                proj_T(zq.v, hc, Wr, 0)
                yield
                proj_T(zk2.v, hc, Wr, 512)
                yield
                proj_T(zv2.v, hc, Wr, 1024)
                yield
                proj_T(zg.v, hc, Wr, 1536)
                yield
                qr, kr, qd, kd, Vb2 = TB

                def rotary(dst, src, ci_, si_):
                    t1, t2 = TA[0], TA[1]
                    sv = hview(src)
                    S.do("dve", "tensor_tensor", out=hview(t1.v), in0=sv, in1=rt_[:, ci_:ci_ + 1, :].bc([128, 8, 64]),
                         op=ALU.mult)
                    S.do("dve", "tensor_tensor", out=hview(t2.v)[:, :, 0:32], in0=sv[:, :, 32:64],
                         in1=rt_[:, si_:si_ + 1, 0:32].bc([128, 8, 32]), op=ALU.mult)
                    S.do("dve", "tensor_tensor", out=hview(t2.v)[:, :, 32:64], in0=sv[:, :, 0:32],
                         in1=rt_[:, si_:si_ + 1, 32:64].bc([128, 8, 32]), op=ALU.mult)
                    S.do("dve", "tensor_tensor", out=t1.v, in0=t1.v, in1=t2.v, op=ALU.add)
                    S.do("act", "activation", out=dst.v, in_=t1.v, func=AF.Copy)
                    return t1

                t1 = rotary(qr, zq.v, 0, 1)
                S.do("dve", "tensor_tensor", out=hview(qd.v), in0=hview(t1.v), in1=sm("qdec")[:, :, None].bc([128, 8, 64]),
                     op=ALU.mult)
                yield
                t1 = rotary(kr, zk2.v, 2, 3)
                S.do("dve", "tensor_tensor", out=hview(kd.v), in0=hview(t1.v), in1=sm("kdec")[:, :, None].bc([128, 8, 64]),
                     op=ALU.mult)
                S.do("act", "activation", out=Vb2.v, in_=zv2.v, func=AF.Copy)
                yield
                for i in range(4):
                    P = psum()
                    Pb = P.v.cast(BF16)
                    for j, src in enumerate((qr, kr, qd)):
                        S.do("pe", "transpose", out=Pb[:, j * 128:(j + 1) * 128], in_=src[:, i * 128:(i + 1) * 128],
                             identity=identb.v)
                    S.do("act", "activation", out=Fh[:, i, :], in_=Pb[:, 0:384], func=AF.Copy)
                yield "mid"
                for hg in range(2):
                    P = psum()
                    for hh in range(4):
                        h = hh * 2 + hg
                        i, pb = h // 2, 64 * (h % 2)
                        S.do("pe", "matmul", out=P[:, hh * 128:(hh + 1) * 128], lhsT=Fh[pb:pb + 64, i, 128:256],
                             rhs=Fh[pb:pb + 64, i, 0:128], start=True, stop=True)
                    S.do("dve", "tensor_tensor", out=AM[:, hg * 4:hg * 4 + 4, :],
                         in0=P.v.rr("p (h n) -> p h n", h=4), in1=DTb[:, hg * 4:hg * 4 + 4, :], op=ALU.mult)
                yield
                ss_ = [Sbd[(2 * t + q) % 3] for q in range(3)]
                for c in range(2):
                    cs = slice(c * 64, (c + 1) * 64)
                    sout = ss_[c + 1]
                    P = psum()
                    for h in range(8):
                        i, pb = h // 2, 64 * (h % 2)
                        S.do("pe", "matmul", out=P[pb:pb + 64, i * 64:(i + 1) * 64], lhsT=kd[cs, h * 64:(h + 1) * 64],
                             rhs=Vb2[cs, h * 64:(h + 1) * 64], start=True, stop=True)
                    S.do("dve", "tensor_tensor", out=S32.v, in0=S32.v, in1=sm("gam")[:, :, None].bc([128, 4, 64]),
                         op=ALU.mult)
                    S.do("dve", "tensor_tensor", out=S32.v, in0=S32.v, in1=P[:, 0:256].rr("p (i v) -> p i v", i=4),
                         op=ALU.add)
                    S.do("act", "activation", out=sout[0:64, :, 0:64], in_=S32[0:64], func=AF.Copy)
                    S.do("act", "activation", out=sout[64:128, :, 64:128], in_=S32[64:128], func=AF.Copy)
                P = psum()
                first = True
                for i in range(4):
                    for h in (2 * i, 2 * i + 1):
                        hc_ = slice(h * 64, (h + 1) * 64)
                        S.do("pe", "matmul", out=P[:, hc_], lhsT=AM[:, (h % 2) * 4 + h // 2, :], rhs=Vb2[:, hc_],
                             start=first, stop=False, skip_group_check=True)
                        first = False
                    for c in range(2):
                        S.do("pe", "matmul", out=P[c * 64:(c + 1) * 64, i * 128:(i + 1) * 128],
                             lhsT=Fh[:, i, 256 + c * 64:256 + (c + 1) * 64], rhs=ss_[c][:, i, :],
                             start=False, stop=(c == 1), skip_group_check=True)
                yield
                yq = TA[2]
                S.do("act", "activation", out=yq.v, in_=P.v, func=AF.Copy)
                if t == 0:
                    dbg("yret_raw", yq.v, [128, RW])
                yn2 = TA[3]
                head_norm(yn2.v, yq.v, EPS, TA[4].v)
                S.do("dve", "tensor_tensor", out=yn2.v, in0=yn2.v, in1=gng.v, op=ALU.mult)
                yield
                sg_ = TA[5]
                S.do("act", "activation", out=sg_.v, in_=zg.v, func=AF.Tanh, scale=0.5)
                S.do("dve", "scalar_tensor_tensor", out=sg_.v, in0=sg_.v, scalar=1.0, in1=zg.v, op0=ALU.add, op1=ALU.mult)
                S.do("dve", "scalar_tensor_tensor", out=ym[:, 512:1024], in0=yn2.v, scalar=0.5, in1=sg_.v,
                     op0=ALU.mult, op1=ALU.mult)
                if t == 0:
                    dbg("yret", ym[:, 512:1024], [128, RW])
                yield
                P = psum()
                Pb = P.v.cast(BF16)
                for k in range(KT):
                    S.do("pe", "transpose", out=Pb[:, k * 128:(k + 1) * 128], in_=ym[:, k * 128:(k + 1) * 128],
                         identity=identb.v)
                S.do("act", "activation", out=ymT.v.rr("p k t -> p (k t)"), in_=Pb[:, 0:1024], func=AF.Copy)
                yield
                for c2 in range(2):
                    cs_ = slice(c2 * 512, (c2 + 1) * 512)
                    P = psum()
                    for k in range(KT):
                        S.do("pe", "matmul", out=P.v, lhsT=ymT[:, k, :], rhs=Wo[:, k, cs_],
                             start=(k == 0), stop=(k == KT - 1))
                    S.do("dve", "tensor_tensor", out=TA[c2].v, in0=P.v, in1=gtA[:, cs_], op=ALU.mult)
                    S.do("pool", "tensor_tensor", out=xb[:, cs_], in0=xb[:, cs_], in1=TA[c2].v, op=ALU.add)
                S.dma("sp", x1s[t * 128:(t + 1) * 128, :], xb.v)
                if t == 0:
                    dbg("x1", xb.v, [128, D])

            def run_to_mid(g):
                for tok in g:
                    if tok == "mid":
                        return
            nt2 = ntiles if stage >= 13 else 0
            cur = a2_tile(0) if nt2 > 0 else None
            if cur is not None:
                run_to_mid(cur)
            for t in range(nt2):
                nxt = a2_tile(t + 1) if t + 1 < nt2 else None
                cur_done, nxt_mid = False, nxt is None
                while not (cur_done and nxt_mid):
                    if not cur_done:
                        try:
                            next(cur)
                        except StopIteration:
                            cur_done = True
                    if not nxt_mid:
                        try:
                            if next(nxt) == "mid":
                                nxt_mid = True
                        except StopIteration:
                            nxt_mid = True
                cur = nxt
            S.barrier()
            S.emit()

        with contextlib.ExitStack() as stB:
            Wu = sb(stB, "Wu", [128, KT, 2 * DFF], BF16)
            Wd = sb(stB, "Wd", [128, NJ, D], BF16)
            wup_v = wup_d.rearrange("(k p) c -> p k c", p=128)
            wdn_v = wdn_d.rearrange("(j p) c -> p j c", p=128)
            for k in range(KT):
                for c0 in range(0, 2 * DFF, 1408):
                    S.dma("pool", Wu[:, k, c0:c0 + 1408], wup_v[:, k, c0:c0 + 1408])
            Wu_k = lambda k, c0: Wu[:, k, c0:c0 + 128]
            fgb = sb(stB, "fgb", [128, D])
            S.dma("sp", fgb.v, fg_d.partition_broadcast(128))
            with contextlib.ExitStack() as stB0:
                gtF = sb(stB0, "gtF", [128, D])
                sc, scB = silu_c(stB0)
                wa = [sb(stB0, "wab%d" % i, [128, KT, 256]) for i in range(2)]
                bbc = [sb(stB0, "bbcb%d" % i, [128, 256]) for i in range(2)]
                for ci in range(20, 24):
                    mod_chunk(ci, wa, bbc, sc, scB, None, gtF)
                wst = [sb(stB0, "wst%d" % i, [128, D]) for i in range(2)]
                for j in range(NJ):
                    w_ = wst[j % 2]
                    S.dma("sp", w_.v, wdn_v[:, j, :])
                    S.do("dve" if j % 2 == 0 else "pool", "tensor_tensor", out=Wd[:, j, :], in0=w_.v, in1=gtF.v,
                         op=ALU.mult)
                S.barrier()
                S.emit()
            cw = sb(stB, "cw", [128, 2 * NJ, 3])
            cb = sb(stB, "cb", [128, 2 * NJ])
            S.dma("sp", cw.v, cwT_d.rearrange("p (j a) -> p j a", a=3))
            S.dma("sp", cb.v, cbT_d)
            xg = [sb(stB, "xg%d" % i, [128, D]) for i in range(2)]
            xr = [sb(stB, "xr%d" % i, [128, D]) for i in range(2)]
            xn2 = sb(stB, "xn2", [128, D])
            h2T = sb(stB, "h2T", [128, KT, 512], BF16)
            actT = sb(stB, "actT", [128, NJ, 512], BF16)
            acc = [sb(stB, "acc%d" % a, [128, 512]) for a in range(2)]
            corr = sb(stB, "corr", [128, 2 * NJ, 2])
            junkb = sb(stB, "junkb", [128, D], BF16)
            junk2 = junkb.v
            tail = sb(stB, "tail", [128, 2 * NJ, 2])
            st2 = [sb(stB, "st2_%d" % i, [128, 2]) for i in range(4)]
            S.do("pool", "memset", ap=tail.v, constant=0.0)
            nb = 0
            for g in range(ngroups):
                for tt in range(4):
                    xb = xg[nb % 2]
                    st_ = st2[nb % 2]
                    nb += 1
                    S.dma("sp", xb.v, x1s[(g * 4 + tt) * 128:(g * 4 + tt + 1) * 128, :])
                    S.do("act", "activation", out=junk2, in_=xb.v, func=AF.Square, accum_out=st_[:, 0:1])
                    S.do("dve", "tensor_scalar", out=st_[:, 0:1], in0=st_[:, 0:1], scalar1=1.0 / D, scalar2=EPS,
                         op0=ALU.mult, op1=ALU.add)
                    rsqrt_small(st_[:, 1:2], st_[:, 0:1])
                    S.do("act", "activation", out=xn2.v, in_=xb.v, func=AF.Copy, scale=st_[:, 1:2])
                    for half in range(2):
                        P = psum()
                        for kk_ in range(4):
                            k = half * 4 + kk_
                            S.do("pe", "transpose", out=P[:, kk_ * 128:(kk_ + 1) * 128],
                                 in_=xn2[:, k * 128:(k + 1) * 128], identity=sm("ident"))
                        for kk_ in range(4):
                            k = half * 4 + kk_
                            S.do("act", "activation", out=h2T[:, k, tt * 128:(tt + 1) * 128],
                                 in_=P[:, kk_ * 128:(kk_ + 1) * 128], func=AF.Identity,
                                 scale=gscF[:, k:k + 1], bias=modT[:, 24 + k:25 + k])
                if g == 0:
                    dbg("h2T", h2T.v, [128, KT, 512])
                S.do("dve", "tensor_tensor", out=corr[:, :, 0:1], in0=tail[:, :, 1:2], in1=cw[:, :, 1:2], op=ALU.mult)
                S.do("dve", "tensor_tensor", out=corr[:, :, 1:2], in0=tail[:, :, 0:1], in1=cw[:, :, 0:1], op=ALU.mult)
                S.do("dve", "tensor_tensor", out=corr[:, :, 0:1], in0=corr[:, :, 0:1], in1=corr[:, :, 1:2], op=ALU.add)
                S.do("dve", "tensor_tensor", out=corr[:, :, 1:2], in0=tail[:, :, 1:2], in1=cw[:, :, 0:1], op=ALU.mult)
                for j in range(NJ):
                    Pj = []
                    for a in range(2):
                        col = a * NJ + j
                        c0 = a * DFF + j * 128
                        P = psum()
                        Pj.append(P)
                        for k in range(KT):
                            S.do("pe", "matmul", out=P.v, lhsT=Wu_k(k, c0), rhs=h2T[:, k, :],
                                 start=(k == 0), stop=(k == KT - 1))
                        ac = acc[a]
                        S.do("act", "activation", out=ac.v, in_=P.v, func=AF.Identity, scale=cw[:, col, 2:3],
                             bias=cb[:, col:col + 1])
                        S.do("dve", "scalar_tensor_tensor", out=ac[:, 1:512], in0=P[:, 0:511], scalar=cw[:, col, 1:2],
                             in1=ac[:, 1:512], op0=ALU.mult, op1=ALU.add)
                        S.do("dve", "scalar_tensor_tensor", out=ac[:, 2:512], in0=P[:, 0:510], scalar=cw[:, col, 0:1],
                             in1=ac[:, 2:512], op0=ALU.mult, op1=ALU.add)
                        S.do("dve", "tensor_tensor", out=ac[:, 0:2], in0=ac[:, 0:2], in1=corr[:, col, :], op=ALU.add)
                        S.do("act", "activation", out=tail[:, col, :], in_=P[:, 510:512], func=AF.Copy)
                    av, ag = acc
                    Pv_, Pg_ = Pj
                    S.do("act", "activation", out=Pg_.v, in_=ag.v, func=AF.Tanh, scale=0.5)
                    S.do("dve", "scalar_tensor_tensor", out=Pv_.v, in0=Pg_.v, scalar=1.0, in1=ag.v,
                         op0=ALU.add, op1=ALU.mult)
                    S.do("dve", "scalar_tensor_tensor", out=actT[:, j, :], in0=av.v, scalar=0.5, in1=Pv_.v,
                         op0=ALU.mult, op1=ALU.mult)
                if g == 0:
                    dbg("actT", actT.v, [128, NJ, 512])
                for tt in range(4):
                    o_ = xr[tt % 2]
                    st_ = st2[2 + tt % 2]
                    S.dma("sp", o_.v, x1s[(g * 4 + tt) * 128:(g * 4 + tt + 1) * 128, :])
                    for c2 in range(2):
                        P = psum()
                        for j in range(NJ):
                            S.do("pe", "matmul", out=P.v, lhsT=actT[:, j, tt * 128:(tt + 1) * 128],
                                 rhs=Wd[:, j, c2 * 512:(c2 + 1) * 512], start=(j == 0), stop=(j == NJ - 1))
                        cs_ = slice(c2 * 512, (c2 + 1) * 512)
                        S.do("dve", "tensor_tensor", out=o_[:, cs_], in0=P.v, in1=o_[:, cs_], op=ALU.add)
                    S.do("act", "activation", out=junk2, in_=o_.v, func=AF.Square, accum_out=st_[:, 0:1])
                    S.do("dve", "tensor_scalar", out=st_[:, 0:1], in0=st_[:, 0:1], scalar1=1.0 / D, scalar2=EPS,
                         op0=ALU.mult, op1=ALU.add)
                    rsqrt_small(st_[:, 1:2], st_[:, 0:1])
                    S.do("dve", "scalar_tensor_tensor", out=o_.v, in0=o_.v, scalar=st_[:, 1:2], in1=fgb.v,
                         op0=ALU.mult, op1=ALU.mult)
                    S.dma("sp", out_d[(g * 4 + tt) * 128:(g * 4 + tt + 1) * 128, :], o_.v)
            S.barrier()
            S.emit()
    return nc, dbg_outs


_CACHE = {}


def make_in_maps(inp):
    f = lambda a: np.ascontiguousarray(np.asarray(a, dtype=np.float32))
    cst = _consts()
    x = f(inp["x"])
    c = f(inp["c"])
    shared = {
        "w_ada": f(inp["w_ada"][0]),
        "b_adaT": f(f(inp["b_ada"][0]).reshape(48, 128).T),
        "b_ada": f(inp["b_ada"][0]).reshape(1, -1),
        "gTa": f(f(inp["attn_norm_g"][0]).reshape(KT, 128).T),
        "gTf": f(f(inp["ffn_norm_g"][0]).reshape(KT, 128).T),
        "fg": f(inp["final_norm_g"]).reshape(1, -1),
        "w_in": f(inp["w_in"][0]),
        "mu": f(inp["rwkv_mu"][0]).reshape(1, -1),
        "w2x": f(np.concatenate([f(inp["rwkv_w2"][0]), f(inp["rwkv_w0"][0])[None, :]], axis=0)),
        "a2x": f(np.concatenate([f(inp["rwkv_a2"][0]), f(inp["rwkv_a0"][0])[None, :]], axis=0)),
        "g2": f(inp["rwkv_g2"][0]),
        "pvec": f(np.stack([f(inp["rwkv_k_k"][0]), f(inp["rwkv_k_a"][0]), f(inp["rwkv_r_k"][0]).reshape(-1),
                            f(inp["rwkv_ln_g"][0]), f(inp["rwkv_ln_b"][0]), f(inp["ret_gn_g"][0])], axis=0)),
        "w_out": f(inp["w_out"][0]),
        "w_up": f(inp["ffn_w_up"][0]),
        "cwT": f(f(inp["ffn_conv_w"][0]).reshape(3, 2 * NJ, 128).transpose(2, 1, 0).reshape(128, -1)),
        "cbT": f(f(inp["ffn_conv_b"][0]).reshape(2 * NJ, 128).T),
        "w_down": f(inp["ffn_w_down"][0]),
        "small": cst["small"], "DT": cst["DT"], "rot": cst["rot"],
    }
    maps = []
    for b in range(NCORES):
        m = dict(shared)
        m["x"] = f(x[b])
        m["cT"] = f(c[b].reshape(KT, 128).T)
        maps.append(m)
    return maps


def kernel(**inputs):
    if "nc" not in _CACHE:
        _CACHE["nc"] = build()[0]
    nc = _CACHE["nc"]
    maps = make_in_maps(inputs)
    res = run_bass_kernel_spmd(nc, maps, core_ids=list(range(NCORES)))
    out = np.stack([np.asarray(r["out"], dtype=np.float32) for r in res.results], axis=0)
    return out
```

```python
import contextlib
import math
import numpy as np
import concourse.bass as bass
import concourse.mybir as mybir
from concourse.bass_utils import run_bass_kernel_spmd

F32 = mybir.dt.float32
BF16 = mybir.dt.bfloat16
AF = mybir.ActivationFunctionType
ALU = mybir.AluOpType
AX = mybir.AxisListType

NCORES = 8
S_LEN = 4096
D = 1024
NT = S_LEN // 128
KT = D // 128
DFF = 2816
NJ = DFF // 128
RW = 512
EPS = 1e-6
LN_EPS = 64e-5
WSC = math.exp(-0.5)


class View:
    __slots__ = ("buf", "ap")

    def __init__(self, buf, ap):
        self.buf = buf
        self.ap = ap

    def __getitem__(self, idx):
        return View(self.buf, self.ap[idx])

    def rr(self, pat, **kw):
        return View(self.buf, self.ap.rearrange(pat, **kw))

    def bc(self, shape):
        return View(self.buf, self.ap.broadcast_to(list(shape)))

    def cast(self, dt):
        return View(self.buf, self.ap.bitcast(dt))


class Buf:
    __slots__ = ("name", "t", "writer", "readers", "dsem", "dcnt", "dram")

    def __init__(self, name, t, dram=False):
        self.name = name
        self.t = t
        self.dram = dram
        self.writer = None
        self.readers = []
        self.dsem = None
        self.dcnt = 0

    def __getitem__(self, idx):
        return View(self, self.t[idx])

    @property
    def v(self):
        return View(self, self.t[:])


class Sched:
    ENGS = ("pe", "act", "dve", "pool", "sp")
    WKEYS = ("out", "accum_out", "ap")

    def __init__(self, nc, same_engine_raw=True):
        self.nc = nc
        self.sem = {}
        self.cnt = {e: 0 for e in self.ENGS}
        self.seen = {e: {} for e in self.ENGS}
        self.same_engine_raw = same_engine_raw
        self.q = {e: [] for e in self.ENGS}
        self.dbufs = []
        self.ninst = 0
        self.nwaits = 0

    def open(self, stack):
        self.stack = stack
        for e in self.ENGS:
            self.sem[e] = stack.enter_context(self.nc.semaphore("s_" + e))

    def _emit_waits(self, e, waits):
        seen = self.seen[e]
        for sem, val in waits:
            k = id(sem)
            if seen.get(k, 0) >= val:
                continue
            seen[k] = val
            self.q[e].append(("wait", sem, val))
            self.nwaits += 1

    def op(self, e, fn, reads, writes):
        waits = []
        for b in reads:
            w = b.writer
            if w is not None and (w[2] != e or self.same_engine_raw):
                waits.append(w[:2])
        for b in writes:
            w = b.writer
            if w is not None and w[2] != e:
                waits.append(w[:2])
            for rd in b.readers:
                if rd[2] != e:
                    waits.append(rd[:2])
        self._emit_waits(e, waits)
        self.cnt[e] += 1
        self.q[e].append(("op", fn, self.sem[e], 1))
        self.ninst += 1
        tok = (self.sem[e], self.cnt[e], e)
        for b in writes:
            b.writer = tok
            b.readers = []
        for b in reads:
            if b in writes:
                continue
            b.readers = [rd for rd in b.readers if rd[2] != e] + [tok]

    def do(self, e, method, **kw):
        reads, writes, real = [], [], {}
        for k, v in kw.items():
            if isinstance(v, View):
                (writes if k in self.WKEYS else reads).append(v.buf)
                real[k] = v.ap
            else:
                real[k] = v
        self.op(e, lambda eng: getattr(eng, method)(**real), reads, writes)

    def dma(self, e, out, in_, **kw):
        reads, writes = [], []
        owner = None
        if isinstance(in_, View):
            reads.append(in_.buf)
            if not in_.buf.dram:
                owner = in_.buf
            in_ = in_.ap
        if isinstance(out, View):
            writes.append(out.buf)
            if not out.buf.dram:
                owner = out.buf
            out = out.ap
        waits = []
        for b in reads:
            if b.writer is not None:
                waits.append(b.writer[:2])
        for b in writes:
            if b.writer is not None:
                waits.append(b.writer[:2])
            for rd in b.readers:
                waits.append(rd[:2])
        self._emit_waits(e, waits)
        if owner.dsem is None:
            owner.dsem = self.stack.enter_context(self.nc.semaphore("d%d_%s" % (len(self.dbufs), owner.name)))
            self.dbufs.append(owner)
        owner.dcnt += 16
        self.q[e].append(("op", (lambda eng, o=out, i=in_, kw=kw: eng.dma_start(out=o, in_=i, **kw)),
                          owner.dsem, 16))
        self.ninst += 1
        tok = (owner.dsem, owner.dcnt, "dma")
        for b in writes:
            b.writer = tok
            b.readers = []
        for b in reads:
            b.readers = b.readers + [tok]

    def barrier(self):
        for e in self.ENGS:
            waits = [(self.sem[o], self.cnt[o]) for o in self.ENGS if o != e and self.cnt[o] > 0]
            waits += [(b.dsem, b.dcnt) for b in self.dbufs]
            self._emit_waits(e, waits)

    def emit(self):
        def replay(q, eng):
            for it in q:
                if it[0] == "wait":
                    eng.wait_ge(it[1], it[2])
                else:
                    it[1](eng).then_inc(it[2], it[3])
        q = self.q
        self.q = {e: [] for e in self.ENGS}
        with self.nc.Block() as block:
            @block.tensor
            def _(eng):
                replay(q["pe"], eng)

            @block.scalar
            def _(eng):
                replay(q["act"], eng)

            @block.vector
            def _(eng):
                replay(q["dve"], eng)

            @block.gpsimd
            def _(eng):
                replay(q["pool"], eng)

            @block.sync
            def _(eng):
                replay(q["sp"], eng)


def _consts():
    idx = np.arange(128)
    ch = idx // 64
    same = ch[:, None] == ch[None, :]
    UI = (same & (idx[:, None] <= idx[None, :])).astype(np.float32)
    US = (same & (idx[:, None] < idx[None, :])).astype(np.float32)
    LS = (same & (idx[:, None] > idx[None, :])).astype(np.float32)
    sel2 = np.stack([(ch == 0), (ch == 1)], axis=1).astype(np.float32)
    maskA = np.concatenate([US, UI, US, UI], axis=1)
    LS4 = np.concatenate([LS] * 4, axis=1)
    ident = np.eye(128, dtype=np.float32)
    H = 8
    lg = np.log1p(-(2.0 ** (-5.0 - np.arange(H, dtype=np.float32)))).astype(np.float32)
    li = (idx % 64).astype(np.float32)
    dist = np.abs(li[:, None] - li[None, :])
    DT = np.zeros((128, H, 128), np.float32)
    for h in range(H):
        DT[:, (h % 2) * 4 + h // 2, :] = np.where(same, np.exp(lg[h] * dist), 0.0)
    qdec = np.exp(lg[None, :] * (li[:, None] + 1.0)).astype(np.float32)
    kdec = np.exp(lg[None, :] * (63.0 - li[:, None])).astype(np.float32)
    g64 = np.exp(lg * 64.0).astype(np.float32)
    gam = np.zeros((128, 4), np.float32)
    for i in range(4):
        gam[0:64, i] = g64[2 * i]
        gam[64:128, i] = g64[2 * i + 1]
    pos = np.arange(S_LEN, dtype=np.float32)
    inv = (10000.0 ** (-np.arange(0, 64, 2, dtype=np.float32) / 64)).astype(np.float32)
    ang = (pos[:, None] * inv[None, :]).astype(np.float32)
    c, s = np.cos(ang).astype(np.float32), np.sin(ang).astype(np.float32)
    CC = np.concatenate([c, c], axis=1)
    SS = np.concatenate([-s, s], axis=1)
    rot = np.stack([CC * 0.125, SS * 0.125, CC, SS], axis=1).astype(np.float32)
    rot = rot.reshape(NT, 128, 256)
    small = np.concatenate([UI, US, LS, ident, maskA, LS4, sel2, qdec, kdec, gam], axis=1)
    return dict(small=np.ascontiguousarray(small), DT=np.ascontiguousarray(DT.reshape(128, 1024)),
                rot=np.ascontiguousarray(rot))


SM_OFF = {}
_o = 0
for _n, _w in (("UI", 128), ("US", 128), ("LS", 128), ("ident", 128), ("maskA", 512), ("LS4", 512),
               ("sel2", 2), ("qdec", 8), ("kdec", 8), ("gam", 4)):
    SM_OFF[_n] = (_o, _o + _w)
    _o += _w
SM_W = _o


def build(debug=False, ntiles=NT, ngroups=NT // 4, stage=99):
    nc = bass.Bass("TRN2", target_bir_lowering=False)

    def din(name, shape):
        return nc.dram_tensor(name, list(shape), F32, kind="ExternalInput").ap()

    x_d = din("x", [S_LEN, D])
    cT_d = din("cT", [128, KT])
    wada_d = din("w_ada", [D, 6 * D])
    badaT_d = din("b_adaT", [128, 48])
    bada_d = din("b_ada", [1, 6 * D])
    gTa_d = din("gTa", [128, KT])
    gTf_d = din("gTf", [128, KT])
    fg_d = din("fg", [1, D])
    win_d = din("w_in", [D, 3840])
    mu_d = din("mu", [1, 1792])
    w2x_d = din("w2x", [65, RW])
    a2x_d = din("a2x", [65, RW])
    g2_d = din("g2", [128, RW])
    pvec_d = din("pvec", [6, RW])
    wout_d = din("w_out", [D, D])
    wup_d = din("w_up", [D, 2 * DFF])
    cwT_d = din("cwT", [128, 2 * NJ * 3])
    cbT_d = din("cbT", [128, 2 * NJ])
    wdn_d = din("w_down", [DFF, D])
    small_d = din("small", [128, SM_W])
    DT_d = din("DT", [128, 1024])
    rot_d = din("rot", [NT, 128, 256])
    out_d = nc.dram_tensor("out", [S_LEN, D], F32, kind="ExternalOutput").ap()
    x1s_t = nc.dram_tensor("x1s", [S_LEN, D], F32, kind="Internal")
    dbg_outs = {}

    with contextlib.ExitStack() as st0:
        S = Sched(nc)
        S.open(st0)

        name_ctr = [0]

        def sb(stack, name, shape, dt=F32):
            name_ctr[0] += 1
            return Buf(name, stack.enter_context(nc.sbuf_tensor("sb%d_%s" % (name_ctr[0], name), list(shape), dt)))

        def dbg(name, view, shape):
            if not debug or name in dbg_outs:
                return
            t = nc.dram_tensor("dbg_" + name, list(shape), F32, kind="ExternalOutput").ap()
            dbg_outs[name] = t
            S.dma("pool", t, view)

        PS = [Buf("ps%d" % i, st0.enter_context(nc.psum_tensor("ps%d" % i, [128, 512], F32))) for i in range(8)]
        ps_rr = [0]

        def psum():
            b = PS[ps_rr[0] % 8]
            ps_rr[0] += 1
            return b

        x1s = Buf("x1s", x1s_t.ap(), dram=True)

        small = sb(st0, "small", [128, SM_W])
        S.dma("sp", small.v, small_d)

        def sm(name):
            a, b = SM_OFF[name]
            return small[:, a:b]

        identb = sb(st0, "identb", [128, 128], BF16)
        S.do("dve", "tensor_copy", out=identb.v, in_=sm("ident"))
        modT = sb(st0, "modT", [128, 48])
        gscA = sb(st0, "gscA", [128, KT])
        gscF = sb(st0, "gscF", [128, KT])
        gtA = sb(st0, "gtA", [128, D])
        mhalf = sb(st0, "mhalf", [128, 8])
        S.do("pool", "memset", ap=mhalf.v, constant=-0.5)

        def rsqrt_small(dst, src):
            n = src.ap.shape[-1]
            S.do("pool", "tensor_tensor", out=dst, in0=src, in1=mhalf[:, 0:n], op=ALU.pow)

        hview = lambda v: v.rr("p (h d) -> p h d", h=8)
        wada_v = wada_d.rearrange("(k p) c -> p k c", p=128)
        win_v = win_d.rearrange("(k p) c -> p k c", p=128)
        wout_v = wout_d.rearrange("(k p) c -> p k c", p=128)

        def silu_c(stack):
            cT = sb(stack, "cT", [128, KT])
            sc = sb(stack, "sc", [128, KT])
            scB = sb(stack, "scB", [128, KT, 128])
            S.dma("sp", cT.v, cT_d)
            S.do("act", "activation", out=sc.v, in_=cT.v, func=AF.Tanh, scale=0.5)
            S.do("dve", "scalar_tensor_tensor", out=sc.v, in0=sc.v, scalar=1.0, in1=cT.v,
                 op0=ALU.add, op1=ALU.mult)
            S.do("dve", "tensor_scalar", out=sc.v, in0=sc.v, scalar1=0.5, scalar2=None, op0=ALU.mult)
            S.do("dve", "tensor_copy", out=scB.v, in_=sc[:, :, None].bc([128, KT, 128]))
            return sc, scB

        def mod_chunk(ci, wa, bbc, sc, scB, badaT, gt_dst):
            w_ = wa[ci % 2]
            S.dma("sp", w_.v, wada_v[:, :, ci * 256:(ci + 1) * 256])
            P = psum()
            if gt_dst is not None:
                b_ = bbc[ci % 2]
                S.dma("sp", b_.v, bada_d[:, ci * 256:(ci + 1) * 256].partition_broadcast(128))
                for k in range(KT):
                    S.do("pe", "matmul", out=P[:, 0:256], lhsT=scB[:, k, :], rhs=w_[:, k, :],
                         start=(k == 0), stop=(k == KT - 1))
                c0 = (ci % 4) * 256
                S.do("dve", "tensor_tensor", out=gt_dst[:, c0:c0 + 256], in0=P[:, 0:256], in1=b_.v, op=ALU.add)
            else:
                for nt_ in range(2):
                    for k in range(KT):
                        S.do("pe", "matmul", out=P[:, nt_:nt_ + 1],
                             lhsT=w_[:, k, nt_ * 128:(nt_ + 1) * 128], rhs=sc[:, k:k + 1],
                             start=(k == 0), stop=(k == KT - 1))
                S.do("dve", "tensor_tensor", out=modT[:, ci * 2:ci * 2 + 2], in0=P[:, 0:2],
                     in1=badaT[:, ci * 2:ci * 2 + 2], op=ALU.add)

        class NormT:
            def __init__(self, stack, tag, gsc, sh_col0):
                self.xt = [sb(stack, tag + "xt%d" % i, [128, D]) for i in range(2)]
                self.xn = sb(stack, tag + "xn", [128, D])
                self.junk = sb(stack, tag + "junk", [128, D], BF16)
                self.hT = [sb(stack, tag + "hT%d" % i, [128, KT, 129], BF16) for i in range(2)]
                self.st = [sb(stack, tag + "nst%d" % i, [128, 2]) for i in range(2)]
                self.gsc, self.sh0 = gsc, sh_col0
                S.do("pool", "memset", ap=self.hT[1][:, :, 128:129], constant=0.0)

            def run(self, t, src_rows):
                xb, hc, hp = self.xt[t % 2], self.hT[t % 2], self.hT[(t + 1) % 2]
                st_ = self.st[t % 2]
                S.dma("sp", xb.v, src_rows)
                S.do("act", "activation", out=self.junk.v, in_=xb.v, func=AF.Square, accum_out=st_[:, 0:1])
                S.do("dve", "tensor_scalar", out=st_[:, 0:1], in0=st_[:, 0:1], scalar1=1.0 / D, scalar2=EPS,
                     op0=ALU.mult, op1=ALU.add)
                rsqrt_small(st_[:, 1:2], st_[:, 0:1])
                S.do("act", "activation", out=self.xn.v, in_=xb.v, func=AF.Copy, scale=st_[:, 1:2])
                S.do("pool", "tensor_copy", out=hc[:, :, 0:1], in_=hp[:, :, 128:129])
                for half in range(2):
                    P = psum()
                    for kk_ in range(4):
                        k = half * 4 + kk_
                        S.do("pe", "transpose", out=P[:, kk_ * 128:(kk_ + 1) * 128],
                             in_=self.xn[:, k * 128:(k + 1) * 128], identity=sm("ident"))
                    for kk_ in range(4):
                        k = half * 4 + kk_
                        S.do("act", "activation", out=hc[:, k, 1:129], in_=P[:, kk_ * 128:(kk_ + 1) * 128],
                             func=AF.Identity, scale=self.gsc[:, k:k + 1],
                             bias=modT[:, self.sh0 + k:self.sh0 + k + 1])
                return xb, hc

        st1 = [sb(st0, "st1_%d" % i, [128, 8]) for i in range(12)]
        st_i = [0]

        def stat():
            b = st1[st_i[0] % len(st1)]
            st_i[0] += 1
            return b

        def head_norm(dst, src, eps, sq):
            s1, s2, mean, var = stat(), stat(), stat(), stat()
            S.do("dve", "tensor_reduce", out=s1.v, in_=hview(src), axis=AX.X, op=ALU.add)
            S.do("act", "activation", out=sq, in_=src, func=AF.Square)
            S.do("dve", "tensor_reduce", out=s2.v, in_=hview(sq), axis=AX.X, op=ALU.add)
            S.do("dve", "tensor_scalar", out=mean.v, in0=s1.v, scalar1=1.0 / 64, scalar2=None, op0=ALU.mult)
            S.do("dve", "tensor_tensor", out=var.v, in0=mean.v, in1=mean.v, op=ALU.mult)
            S.do("dve", "scalar_tensor_tensor", out=var.v, in0=s2.v, scalar=1.0 / 64, in1=var.v,
                 op0=ALU.mult, op1=ALU.subtract)
            S.do("dve", "tensor_scalar", out=var.v, in0=var.v, scalar1=eps, scalar2=None, op0=ALU.add)
            rs = stat()
            rsqrt_small(rs.v, var.v)
            S.do("dve", "tensor_tensor", out=hview(dst), in0=hview(src),
                 in1=mean[:, :, None].bc([128, 8, 64]), op=ALU.subtract)
            S.do("dve", "tensor_tensor", out=hview(dst), in0=hview(dst),
                 in1=rs[:, :, None].bc([128, 8, 64]), op=ALU.mult)

        def proj_T(dst, hc, W, c0, W2=None):
            P = psum()
            n = 2 * KT if W2 is not None else KT
            i = 0
            for k in range(KT):
                S.do("pe", "matmul", out=P.v, lhsT=hc[:, k, 1:129], rhs=W[:, k, c0:c0 + 512],
                     start=(i == 0), stop=(i == n - 1))
                i += 1
            if W2 is not None:
                for k in range(KT):
                    S.do("pe", "matmul", out=P.v, lhsT=hc[:, k, 0:128], rhs=W2[:, k, c0:c0 + 512],
                         start=False, stop=(i == n - 1))
                    i += 1
            S.do("act", "activation", out=dst, in_=P.v, func=AF.Copy)

        yrs = Buf("yrs", nc.dram_tensor("yrs", [S_LEN, RW], BF16, kind="Internal").ap(), dram=True)

        with contextlib.ExitStack() as stA:
            W1 = sb(stA, "W1", [128, KT, 1792], BF16)
            W2 = sb(stA, "W2", [128, KT, 1792], BF16)
            w2x = sb(stA, "w2x", [65, RW])
            a2x = sb(stA, "a2x", [65, RW])
            g2 = sb(stA, "g2", [128, RW])
            pv = [sb(stA, "pv%d" % i, [128, RW]) for i in range(5)]
            S.dma("sp", w2x.v, w2x_d)
            S.dma("sp", a2x.v, a2x_d)
            S.dma("sp", g2.v, g2_d)
            for i in range(5):
                S.dma("sp", pv[i].v, pvec_d[i:i + 1, :].partition_broadcast(128))

            with contextlib.ExitStack() as st00:
                mu_bc = sb(st00, "mu_bc", [128, 1792])
                omm = sb(st00, "omm", [128, 1792])
                S.dma("sp", mu_bc.v, mu_d.partition_broadcast(128))
                S.do("dve", "tensor_scalar", out=omm.v, in0=mu_bc.v, scalar1=-1.0, scalar2=1.0,
                     op0=ALU.mult, op1=ALU.add)
                stg = [sb(st00, "stg%d" % i, [128, 1792]) for i in range(2)]
                for k in range(KT):
                    sg = stg[k % 2]
                    S.dma("sp", sg.v, win_v[:, k, 0:1792])
                    S.do("dve", "tensor_tensor", out=W1[:, k, :], in0=sg.v, in1=omm.v, op=ALU.mult)
                    S.do("pool", "tensor_tensor", out=W2[:, k, :], in0=sg.v, in1=mu_bc.v, op=ALU.mult)
                sc, scB = silu_c(st00)
                badaT = sb(st00, "badaT", [128, 48])
                gTa = sb(st00, "gTa", [128, KT])
                gTf = sb(st00, "gTf", [128, KT])
                S.dma("sp", badaT.v, badaT_d)
                S.dma("sp", gTa.v, gTa_d)
                S.dma("sp", gTf.v, gTf_d)
                wa = [sb(st00, "wa%d" % i, [128, KT, 256]) for i in range(2)]
                bbc = [sb(st00, "bbc%d" % i, [128, 256]) for i in range(2)]
                for ci in range(24):
                    if 8 <= ci < 12:
                        mod_chunk(ci, wa, bbc, sc, scB, badaT, gtA)
                    elif ci >= 20:
                        continue
                    else:
                        mod_chunk(ci, wa, bbc, sc, scB, badaT, None)
                S.do("dve", "scalar_tensor_tensor", out=gscA.v, in0=modT[:, 8:16], scalar=1.0, in1=gTa.v,
                     op0=ALU.add, op1=ALU.mult)
                S.do("dve", "scalar_tensor_tensor", out=gscF.v, in0=modT[:, 32:40], scalar=1.0, in1=gTf.v,
                     op0=ALU.add, op1=ALU.mult)
                dbg("modT", modT.v, [128, 48])
                dbg("gtA", gtA.v, [128, D])
                S.barrier()
                S.emit()

            NA = NormT(stA, "a1", gscA, 0)
            Zs = [[sb(stA, "Z%d_%d" % (p_, i), [128, RW]) for i in range(3)] for p_ in range(2)]
            TAs = [[sb(stA, "TA%d_%d" % (p_, i), [128, RW]) for i in range(10)] for p_ in range(2)]
            TB = [sb(stA, "TB%d" % i, [128, RW], BF16) for i in range(9)]
            lo_ws = [sb(stA, "lo_w%d" % p_, [65, 128]) for p_ in range(2)]
            lo_as = [sb(stA, "lo_a%d" % p_, [65, 128]) for p_ in range(2)]
            lo_gs = [sb(stA, "lo_g%d" % p_, [128, 128]) for p_ in range(2)]
            for b_ in lo_ws + lo_as:
                S.do("pool", "memset", ap=b_.v, constant=1.0)
            WCs = [sb(stA, "WC%d" % p_, [128, 4, 2]) for p_ in range(2)]
            Fh = sb(stA, "Fh", [128, 4, 512], BF16)
            AM = sb(stA, "AM", [128, 8, 512], BF16)
            Lp = [sb(stA, "Lp%d" % i, [128, 8, 128], BF16) for i in range(2)]
            Np = [sb(stA, "Np%d" % i, [128, 8, 128], BF16) for i in range(2)]
            X = [sb(stA, "X%d" % i, [128, 8, 128], BF16) for i in range(2)]
            RhT = sb(stA, "RhT", [128, 4, 128], BF16)
            MpT = sb(stA, "MpT", [128, 4, 2, 128], BF16)
            H32 = sb(stA, "H32", [128, 4, 64])
            Hc = [sb(stA, "Hc%d" % i, [128, 4, 64], BF16) for i in range(3)]
            Hbd = [sb(stA, "Hbd%d" % i, [128, 4, 128], BF16) for i in range(3)]
            slot = lambda h: (h % 2) * 4 + h // 2
            yrw = [sb(stA, "yrw%d" % i, [128, RW], BF16) for i in range(2)]
            S.do("pool", "memset", ap=H32.v, constant=0.0)
            S.do("pool", "memset", ap=Hc[0].v, constant=0.0)
            S.do("pool", "memset", ap=MpT.v, constant=0.0)
            for b_ in Hbd:
                S.do("pool", "memset", ap=b_.v, constant=0.0)

            def proj_F(hc, c0, m):
                P = psum()
                i = 0
                for W_, sl in ((W1, slice(1, 129)), (W2, slice(0, 128))):
                    for k in range(KT):
                        S.do("pe", "matmul", out=P[0:m, 0:128], lhsT=W_[:, k, c0:c0 + m], rhs=hc[:, k, sl],
                             start=(i == 0), stop=(i == 2 * KT - 1))
                        i += 1
                return P[0:m, 0:128]

            def a1_tile(t):
                Z, TA, WC = Zs[t % 2], TAs[t % 2], WCs[t % 2]
                lo_w, lo_a, lo_g = lo_ws[t % 2], lo_as[t % 2], lo_gs[t % 2]
                xb, hc = NA.run(t, x_d[t * 128:(t + 1) * 128, :])
                if t == 0:
                    dbg("hT", hc[:, :, 1:129], [128, KT, 128])
                yield
                if stage < 2:
                    return
                zr, zk, zv = Z[0], Z[1], Z[2]
                proj_T(zr.v, hc, W1, 0, W2)
                yield
                proj_T(zk.v, hc, W1, 512, W2)
                yield
                proj_T(zv.v, hc, W1, 1024, W2)
                Pw = proj_F(hc, 1536, 64)
                S.do("act", "activation", out=lo_w[0:64, :], in_=Pw, func=AF.Tanh)
                yield
                Pa_ = proj_F(hc, 1600, 64)
                S.do("act", "activation", out=lo_a[0:64, :], in_=Pa_, func=AF.Copy)
                yield
                Pg = proj_F(hc, 1664, 128)
                S.do("act", "activation", out=lo_g.v, in_=Pg, func=AF.Tanh, scale=0.5)
                S.do("dve", "tensor_scalar", out=lo_g.v, in0=lo_g.v, scalar1=0.5, scalar2=0.5,
                     op0=ALU.mult, op1=ALU.add)
                if t == 0:
                    dbg("zr", zr.v, [128, RW])
                    dbg("zv", zv.v, [128, RW])
                yield
                if stage < 3:
                    return
                lw, am1, gsb = TA[0], TA[1], TA[2]
                P = psum()
                S.do("pe", "matmul", out=P.v, lhsT=lo_w.v, rhs=w2x.v, start=True, stop=True)
                S.do("act", "activation", out=lw.v, in_=P.v, func=AF.Tanh, scale=0.5)
                S.do("dve", "tensor_scalar", out=lw.v, in0=lw.v, scalar1=1.0, scalar2=-0.5 * WSC,
                     op0=ALU.add, op1=ALU.mult)
                P = psum()
                S.do("pe", "matmul", out=P.v, lhsT=lo_a.v, rhs=a2x.v, start=True, stop=True)
                S.do("act", "activation", out=am1.v, in_=P.v, func=AF.Tanh, scale=0.5)
                S.do("dve", "tensor_scalar", out=am1.v, in0=am1.v, scalar1=0.5, scalar2=-0.5,
                     op0=ALU.mult, op1=ALU.add)
                P = psum()
                S.do("pe", "matmul", out=P.v, lhsT=lo_g.v, rhs=g2.v, start=True, stop=True)
                S.do("act", "activation", out=gsb.v, in_=P.v, func=AF.Copy)
                if t == 0:
                    dbg("lw", lw.v, [128, RW])
                    dbg("am1", am1.v, [128, RW])
                    dbg("gsb", gsb.v, [128, RW])
                yield
                if stage < 4:
                    return
                Wc, Winv, Wprev, Ee = TA[3], TA[4], TA[5], TA[6]
                P = psum()
                S.do("pe", "matmul", out=P.v, lhsT=sm("UI"), rhs=lw.v, start=True, stop=True)
                S.do("act", "activation", out=Wc.v, in_=P.v, func=AF.Exp)
                S.do("act", "activation", out=Winv.v, in_=P.v, func=AF.Exp, scale=-1.0)
                P = psum()
                S.do("pe", "matmul", out=P.v, lhsT=sm("US"), rhs=lw.v, start=True, stop=True)
                S.do("act", "activation", out=Wprev.v, in_=P.v, func=AF.Exp)
                P = psum()
                S.do("pe", "matmul", out=P.v, lhsT=sm("LS"), rhs=lw.v, start=True, stop=True)
                S.do("act", "activation", out=Ee.v, in_=P.v, func=AF.Exp)
                P = psum()
                for i in range(4):
                    S.do("pe", "matmul", out=P[:, 2 * i:2 * i + 2], lhsT=lw[:, i * 128:(i + 1) * 128],
                         rhs=sm("sel2"), start=True, stop=True)
                S.do("act", "activation", out=WC.v, in_=P[:, 0:8].rr("p (i c) -> p i c", i=4), func=AF.Exp)
                yield "mid"
                if stage < 5:
                    return
                kk0, k2, kka = TA[7], TA[8], TA[9]
                n2, rn = stat(), stat()
                S.do("dve", "tensor_tensor", out=kk0.v, in0=zk.v, in1=pv[0].v, op=ALU.mult)
                S.do("pool", "tensor_tensor", out=k2.v, in0=kk0.v, in1=kk0.v, op=ALU.mult)
                S.do("dve", "tensor_reduce", out=n2.v, in_=hview(k2.v), axis=AX.X, op=ALU.add)
                yield
                if stage < 5.1:
                    return
                S.do("dve", "tensor_scalar", out=n2.v, in0=n2.v, scalar1=1e-24, scalar2=None, op0=ALU.max)
                rsqrt_small(rn.v, n2.v)
                yield
                if stage < 5.2:
                    return
                S.do("dve", "tensor_tensor", out=hview(kk0.v), in0=hview(kk0.v),
                     in1=rn[:, :, None].bc([128, 8, 64]), op=ALU.mult)
                yield
                if stage < 5.3:
                    return
                S.do("pool", "tensor_tensor", out=k2.v, in0=am1.v, in1=pv[1].v, op=ALU.mult)
                S.do("dve", "scalar_tensor_tensor", out=k2.v, in0=k2.v, scalar=1.0, in1=zk.v,
                     op0=ALU.add, op1=ALU.mult)
                S.do("dve", "scalar_tensor_tensor", out=kka.v, in0=am1.v, scalar=1.0, in1=kk0.v,
                     op0=ALU.add, op1=ALU.mult)
                yield
                if stage < 5.4:
                    return
                at, bt, kt, rtl, Vb, Bh0, Bh1, Kh0, Kh1 = TB
                Bhc, Khc = (Bh0, Bh1), (Kh0, Kh1)
                S.do("dve", "scalar_tensor_tensor", out=at.v, in0=kk0.v, scalar=-1.0, in1=Wprev.v,
                     op0=ALU.mult, op1=ALU.mult)
                S.do("pool", "tensor_tensor", out=bt.v, in0=kka.v, in1=Winv.v, op=ALU.mult)
                S.do("dve", "tensor_tensor", out=kt.v, in0=k2.v, in1=Winv.v, op=ALU.mult)
                S.do("pool", "tensor_tensor", out=rtl.v, in0=zr.v, in1=Wc.v, op=ALU.mult)
                for c in range(2):
                    S.do("dve", "scalar_tensor_tensor", out=Bhc[c].v, in0=kka.v, scalar=sm("sel2")[:, c:c + 1], in1=Ee.v,
                         op0=ALU.mult, op1=ALU.mult)
                    S.do("dve", "scalar_tensor_tensor", out=Khc[c].v, in0=k2.v, scalar=sm("sel2")[:, c:c + 1], in1=Ee.v,
                         op0=ALU.mult, op1=ALU.mult)
                S.do("act", "activation", out=Vb.v, in_=zv.v, func=AF.Copy)
                yield
                if stage < 5.5:
                    return
                bn = stat()
                S.do("pool", "tensor_tensor", out=Wc.v, in0=zr.v, in1=pv[2].v, op=ALU.mult)
                S.do("dve", "tensor_tensor", out=Wc.v, in0=Wc.v, in1=k2.v, op=ALU.mult)
                S.do("dve", "tensor_reduce", out=bn.v, in_=hview(Wc.v), axis=AX.X, op=ALU.add)
                if t == 0:
                    dbg("at", at.v, [128, RW])
                    dbg("bt", bt.v, [128, RW])
                    dbg("Kh1", Kh1.v, [128, RW])
                yield
                if stage < 6:
                    return
                for i in range(4):
                    P = psum()
                    Pb = P.v.cast(BF16)
                    for j, src in enumerate((at, rtl, bt, kt)):
                        S.do("pe", "transpose", out=Pb[:, j * 128:(j + 1) * 128], in_=src[:, i * 128:(i + 1) * 128],
                             identity=identb.v)
                    S.do("act", "activation", out=Fh[:, i, :], in_=Pb[:, 0:512], func=AF.Copy)
                yield
                if stage < 7:
                    return
                for h in range(8):
                    i, pb = h // 2, 64 * (h % 2)
                    P = psum()
                    S.do("pe", "matmul", out=P[:, 0:256], lhsT=Fh[pb:pb + 64, i, 256:384], rhs=Fh[pb:pb + 64, i, 0:256],
                         start=True, stop=True)
                    S.do("pe", "matmul", out=P[:, 256:512], lhsT=Fh[pb:pb + 64, i, 384:512], rhs=Fh[pb:pb + 64, i, 0:256],
                         start=True, stop=True)
                    S.do("dve", "tensor_tensor", out=AM[:, slot(h), :], in0=P.v, in1=sm("maskA"), op=ALU.mult)
                for hg in range(2):
                    P = psum()
                    for hh in range(4):
                        h = hh * 2 + hg
                        i, pb = h // 2, 64 * (h % 2)
                        S.do("pe", "matmul", out=P[:, hh * 128:(hh + 1) * 128], lhsT=Fh[pb:pb + 64, i, 0:128],
                             rhs=Fh[pb:pb + 64, i, 256:384], start=True, stop=True)
                    S.do("dve", "tensor_tensor", out=Lp[0][:, hg * 4:hg * 4 + 4, :].rr("p h s -> p (h s)"), in0=P.v,
                         in1=sm("LS4"), op=ALU.mult)
                yield
                if stage < 8:
                    return
                P = psum()
                for h in range(8):
                    sl_ = slot(h)
                    S.do("pe", "matmul", out=P[:, sl_ * 64:(sl_ + 1) * 64], lhsT=AM[:, sl_, 256:384],
                         rhs=Vb[:, h * 64:(h + 1) * 64], start=True, stop=True)
                S.do("act", "activation", out=X[0][:, :, 64:128], in_=hview(P.v), func=AF.Copy)
                atv = at.v.rr("p (hh hg d) -> p hg hh d", hh=4, hg=2)
                for hg in range(2):
                    S.do("pool", "tensor_copy", out=X[0][:, hg * 4:hg * 4 + 4, 0:64], in_=atv[:, hg])
                yield
                if stage < 9:
                    return
                xcur = 0
                for lv in range(6):
                    if lv == 0:
                        Ncur = lambda s_: AM[:, s_, 0:128]
                    else:
                        Ncur = (lambda s_, Nb=Np[lv % 2]: Nb[:, s_, :])
                    Lcur = Lp[lv % 2]
                    for hg in range(2):
                        P = psum()
                        for hh in range(4):
                            s_ = hg * 4 + hh
                            S.do("pe", "matmul", out=P[:, hh * 128:(hh + 1) * 128], lhsT=Ncur(s_),
                                 rhs=X[xcur][:, s_, :], start=True, stop=True)
                        S.do("dve", "tensor_tensor", out=X[1 - xcur][:, hg * 4:hg * 4 + 4, :].rr("p h s -> p (h s)"),
                             in0=P.v, in1=X[xcur][:, hg * 4:hg * 4 + 4, :].rr("p h s -> p (h s)"), op=ALU.add)
                    xcur = 1 - xcur
                    if lv == 5:
                        break
                    for hg in range(2):
                        P = psum()
                        for hh in range(4):
                            s_ = hg * 4 + hh
                            S.do("pe", "matmul", out=P[:, hh * 128:(hh + 1) * 128], lhsT=Lcur[:, s_, :],
                                 rhs=Ncur(s_), start=True, stop=True)
                        S.do("act", "activation", out=Np[(lv + 1) % 2][:, hg * 4:hg * 4 + 4, :].rr("p h s -> p (h s)"),
                             in_=P.v, func=AF.Copy)
                    if lv < 4:
                        for hg in range(2):
                            P = psum()
                            for hh in range(4):
                                s_ = hg * 4 + hh
                                S.do("pe", "matmul", out=P[:, hh * 128:(hh + 1) * 128], lhsT=Ncur(s_),
                                     rhs=Lcur[:, s_, :], start=True, stop=True)
                            S.do("act", "activation",
                                 out=Lp[(lv + 1) % 2][:, hg * 4:hg * 4 + 4, :].rr("p h s -> p (h s)"),
                                 in_=P.v, func=AF.Copy)
                Xf = X[xcur]
                if t == 0:
                    dbg("Xf", Xf.v, [128, 8, 128])
                yield
                if stage < 10:
                    return
                P = psum()
                for h in range(8):
                    i, pb, sl_ = h // 2, 64 * (h % 2), slot(h)
                    S.do("pe", "matmul", out=P[pb:pb + 64, i * 128:(i + 1) * 128], lhsT=Xf[:, sl_, 0:64],
                         rhs=AM[:, sl_, 128:256], start=True, stop=True)
                S.do("dve", "tensor_tensor", out=RhT.v, in0=P.v.rr("p (i t) -> p i t", i=4), in1=Fh[:, :, 128:256],
                     op=ALU.add)
                P = psum()
                for h in range(8):
                    i, pb, sl_ = h // 2, 64 * (h % 2), slot(h)
                    for c in range(2):
                        S.do("pe", "matmul", out=P[pb:pb + 64, (i * 2 + c) * 64:(i * 2 + c + 1) * 64],
                             lhsT=Xf[:, sl_, 0:64], rhs=Bhc[c][:, h * 64:(h + 1) * 64], start=True, stop=True)
                Pv_ = P.v.rr("p (i c j) -> p i c j", i=4, c=2)
                S.do("act", "activation", out=MpT[0:64, :, :, 0:64], in_=Pv_[0:64], func=AF.Copy)
                S.do("act", "activation", out=MpT[64:128, :, :, 64:128], in_=Pv_[64:128], func=AF.Copy)
                yield
                if stage < 11:
                    return
                hs = [(Hc[(2 * t + q) % 3], Hbd[(2 * t + q) % 3]) for q in range(3)]
                for c in range(2):
                    (hin, _), (hout, hout_bd) = hs[c], hs[c + 1]
                    P = psum()
                    first = True
                    for i in range(4):
                        for h in (2 * i, 2 * i + 1):
                            pb, sl_ = 64 * (h % 2), slot(h)
                            o_ = P[pb:pb + 64, i * 64:(i + 1) * 64]
                            S.do("pe", "matmul", out=o_, lhsT=Bhc[c][:, h * 64:(h + 1) * 64], rhs=Xf[:, sl_, 64:128],
                                 start=(i == 0), stop=False, skip_group_check=True)
                            S.do("pe", "matmul", out=o_, lhsT=Khc[c][:, h * 64:(h + 1) * 64],
                                 rhs=Vb[:, h * 64:(h + 1) * 64], start=False, stop=False, skip_group_check=True)
                        S.do("pe", "matmul", out=P[:, i * 64:(i + 1) * 64], lhsT=MpT[:, i, c, :], rhs=hin[:, i, :],
                             start=False, stop=True, skip_group_check=True)
                    S.do("dve", "tensor_tensor", out=H32.v, in0=H32.v, in1=WC[:, :, c:c + 1].bc([128, 4, 64]),
                         op=ALU.mult)
                    S.do("dve", "tensor_tensor", out=H32.v, in0=H32.v, in1=P[:, 0:256].rr("p (i v) -> p i v", i=4),
                         op=ALU.add)
                    S.do("act", "activation", out=hout.v, in_=H32.v, func=AF.Copy)
                    S.do("act", "activation", out=hout_bd[0:64, :, 0:64], in_=H32[0:64], func=AF.Copy)
                    S.do("act", "activation", out=hout_bd[64:128, :, 64:128], in_=H32[64:128], func=AF.Copy)
                yield
                if stage < 12:
                    return
                P = psum()
                first = True
                for i in range(4):
                    for h in (2 * i, 2 * i + 1):
                        sl_ = slot(h)
                        hc_ = slice(h * 64, (h + 1) * 64)
                        S.do("pe", "matmul", out=P[:, hc_], lhsT=AM[:, sl_, 128:256], rhs=Xf[:, sl_, 64:128],
                             start=first, stop=False, skip_group_check=True)
                        first = False
                        S.do("pe", "matmul", out=P[:, hc_], lhsT=AM[:, sl_, 384:512], rhs=Vb[:, hc_],
                             start=False, stop=False, skip_group_check=True)
                    for c in range(2):
                        S.do("pe", "matmul", out=P[c * 64:(c + 1) * 64, i * 128:(i + 1) * 128],
                             lhsT=RhT[:, i, c * 64:(c + 1) * 64], rhs=hs[c][1][:, i, :],
                             start=False, stop=(c == 1), skip_group_check=True)
                yr = TA[3]
                S.do("act", "activation", out=yr.v, in_=P.v, func=AF.Copy)
                if t == 0:
                    dbg("yraw", yr.v, [128, RW])
                yn = TA[4]
                head_norm(yn.v, yr.v, LN_EPS, TA[9].v)
                S.do("dve", "tensor_tensor", out=yn.v, in0=yn.v, in1=pv[3].v, op=ALU.mult)
                S.do("pool", "tensor_tensor", out=yn.v, in0=yn.v, in1=pv[4].v, op=ALU.add)
                S.do("dve", "tensor_tensor", out=hview(TA[5].v), in0=hview(zv.v), in1=bn[:, :, None].bc([128, 8, 64]),
                     op=ALU.mult)
                S.do("pool", "tensor_tensor", out=yn.v, in0=yn.v, in1=TA[5].v, op=ALU.add)
                yo = yrw[t % 2]
                S.do("dve", "tensor_tensor", out=yo.v, in0=yn.v, in1=gsb.v, op=ALU.mult)
                if t == 0:
                    dbg("yrwkv", yo.v, [128, RW])
                yield
                if stage < 12.5:
                    return
                S.dma("sp", yrs[t * 128:(t + 1) * 128, :], yo.v)

            def run_to_mid1(g):
                for tok in g:
                    if tok == "mid":
                        return
            cur = a1_tile(0) if ntiles > 0 else None
            if cur is not None:
                run_to_mid1(cur)
            for t in range(ntiles):
                nxt = a1_tile(t + 1) if t + 1 < ntiles else None
                cur_done, nxt_mid = False, nxt is None
                while not (cur_done and nxt_mid):
                    if not cur_done:
                        try:
                            next(cur)
                        except StopIteration:
                            cur_done = True
                    if not nxt_mid:
                        try:
                            if next(nxt) == "mid":
                                nxt_mid = True
                        except StopIteration:
                            nxt_mid = True
                cur = nxt
            S.barrier()
            S.emit()

        with contextlib.ExitStack() as stA:
            Wr = sb(stA, "Wr", [128, KT, 2048], BF16)
            Wo = sb(stA, "Wo", [128, KT, D], BF16)
            gng = sb(stA, "gng", [128, RW])
            DTb = sb(stA, "DTb", [128, 8, 128])
            S.dma("sp", gng.v, pvec_d[5:6, :].partition_broadcast(128))
            S.dma("sp", DTb.v, DT_d.rearrange("p (h n) -> p h n", h=8))
            for k in range(KT):
                S.dma("pool", Wr[:, k, 0:1024], win_v[:, k, 1792:2816])
                S.dma("pool", Wr[:, k, 1024:2048], win_v[:, k, 2816:3840])
                S.dma("pool", Wo[:, k, :], wout_v[:, k, :])
            NA = NormT(stA, "a2", gscA, 0)
            Zs = [[sb(stA, "Zr%d_%d" % (p_, i), [128, RW]) for i in range(4)] for p_ in range(2)]
            TAs = [[sb(stA, "TAr%d_%d" % (p_, i), [128, RW]) for i in range(6)] for p_ in range(2)]
            TBs = [[sb(stA, "TBr%d_%d" % (p_, i), [128, RW], BF16) for i in range(5)] for p_ in range(2)]
            Fhs = [sb(stA, "Fhr%d" % p_, [128, 4, 384], BF16) for p_ in range(2)]
            AMs = [sb(stA, "AMr%d" % p_, [128, 8, 128], BF16) for p_ in range(2)]
            S32 = sb(stA, "S32", [128, 4, 64])
            Sbd = [sb(stA, "Sbd%d" % i, [128, 4, 128], BF16) for i in range(3)]
            rot = [sb(stA, "rot%d" % i, [128, 4, 64]) for i in range(2)]
            ymix = [sb(stA, "ymix%d" % i, [128, D], BF16) for i in range(2)]
            ymTs = [sb(stA, "ymT%d" % p_, [128, KT, 128], BF16) for p_ in range(2)]
            S.do("pool", "memset", ap=S32.v, constant=0.0)
            for b_ in Sbd:
                S.do("pool", "memset", ap=b_.v, constant=0.0)

            def a2_tile(t):
                Z, TA, TB, Fh, AM, ymT = Zs[t % 2], TAs[t % 2], TBs[t % 2], Fhs[t % 2], AMs[t % 2], ymTs[t % 2]
                ym = ymix[t % 2]
                S.dma("sp", ym[:, 0:512], yrs[t * 128:(t + 1) * 128, :])
                rt_ = rot[t % 2]
                S.dma("sp", rt_.v, rot_d[t].rearrange("p (a d) -> p a d", a=4))
                xb, hc = NA.run(t, x_d[t * 128:(t + 1) * 128, :])
                yield
                zq, zk2, zv2, zg = Z
                proj_T(zq.v, hc, Wr, 0)
                yield
                proj_T(zk2.v, hc, Wr, 512)
                yield
                proj_T(zv2.v, hc, Wr, 1024)
                yield
                proj_T(zg.v, hc, Wr, 1536)
                yield
                qr, kr, qd, kd, Vb2 = TB

                def rotary(dst, src, ci_, si_):
                    t1, t2 = TA[0], TA[1]
                    sv = hview(src)
                    S.do("dve", "tensor_tensor", out=hview(t1.v), in0=sv, in1=rt_[:, ci_:ci_ + 1, :].bc([128, 8, 64]),
                         op=ALU.mult)
                    S.do("dve", "tensor_tensor", out=hview(t2.v)[:, :, 0:32], in0=sv[:, :, 32:64],
                         in1=rt_[:, si_:si_ + 1, 0:32].bc([128, 8, 32]), op=ALU.mult)
                    S.do("dve", "tensor_tensor", out=hview(t2.v)[:, :, 32:64], in0=sv[:, :, 0:32],
                         in1=rt_[:, si_:si_ + 1, 32:64].bc([128, 8, 32]), op=ALU.mult)
                    S.do("dve", "tensor_tensor", out=t1.v, in0=t1.v, in1=t2.v, op=ALU.add)
                    S.do("act", "activation", out=dst.v, in_=t1.v, func=AF.Copy)
                    return t1

                t1 = rotary(qr, zq.v, 0, 1)
                S.do("dve", "tensor_tensor", out=hview(qd.v), in0=hview(t1.v), in1=sm("qdec")[:, :, None].bc([128, 8, 64]),
                     op=ALU.mult)
                yield
                t1 = rotary(kr, zk2.v, 2, 3)
                S.do("dve", "tensor_tensor", out=hview(kd.v), in0=hview(t1.v), in1=sm("kdec")[:, :, None].bc([128, 8, 64]),
                     op=ALU.mult)
                S.do("act", "activation", out=Vb2.v, in_=zv2.v, func=AF.Copy)
                yield
                for i in range(4):
                    P = psum()
                    Pb = P.v.cast(BF16)
                    for j, src in enumerate((qr, kr, qd)):
                        S.do("pe", "transpose", out=Pb[:, j * 128:(j + 1) * 128], in_=src[:, i * 128:(i + 1) * 128],
                             identity=identb.v)
                    S.do("act", "activation", out=Fh[:, i, :], in_=Pb[:, 0:384], func=AF.Copy)
                yield "mid"
                for hg in range(2):
                    P = psum()
                    for hh in range(4):
                        h = hh * 2 + hg
                        i, pb = h // 2, 64 * (h % 2)
                        S.do("pe", "matmul", out=P[:, hh * 128:(hh + 1) * 128], lhsT=Fh[pb:pb + 64, i, 128:256],
                             rhs=Fh[pb:pb + 64, i, 0:128], start=True, stop=True)
                    S.do("dve", "tensor_tensor", out=AM[:, hg * 4:hg * 4 + 4, :],
                         in0=P.v.rr("p (h n) -> p h n", h=4), in1=DTb[:, hg * 4:hg * 4 + 4, :], op=ALU.mult)
                yield
                ss_ = [Sbd[(2 * t + q) % 3] for q in range(3)]
                for c in range(2):
                    cs = slice(c * 64, (c + 1) * 64)
                    sout = ss_[c + 1]
                    P = psum()
                    for h in range(8):
                        i, pb = h // 2, 64 * (h % 2)
                        S.do("pe", "matmul", out=P[pb:pb + 64, i * 64:(i + 1) * 64], lhsT=kd[cs, h * 64:(h + 1) * 64],
                             rhs=Vb2[cs, h * 64:(h + 1) * 64], start=True, stop=True)
                    S.do("dve", "tensor_tensor", out=S32.v, in0=S32.v, in1=sm("gam")[:, :, None].bc([128, 4, 64]),
                         op=ALU.mult)
                    S.do("dve", "tensor_tensor", out=S32.v, in0=S32.v, in1=P[:, 0:256].rr("p (i v) -> p i v", i=4),
                         op=ALU.add)
                    S.do("act", "activation", out=sout[0:64, :, 0:64], in_=S32[0:64], func=AF.Copy)
                    S.do("act", "activation", out=sout[64:128, :, 64:128], in_=S32[64:128], func=AF.Copy)
                P = psum()
                first = True
                for i in range(4):
                    for h in (2 * i, 2 * i + 1):
                        hc_ = slice(h * 64, (h + 1) * 64)
                        S.do("pe", "matmul", out=P[:, hc_], lhsT=AM[:, (h % 2) * 4 + h // 2, :], rhs=Vb2[:, hc_],
                             start=first, stop=False, skip_group_check=True)
                        first = False
                    for c in range(2):
                        S.do("pe", "matmul", out=P[c * 64:(c + 1) * 64, i * 128:(i + 1) * 128],
                             lhsT=Fh[:, i, 256 + c * 64:256 + (c + 1) * 64], rhs=ss_[c][:, i, :],
                             start=False, stop=(c == 1), skip_group_check=True)
                yield
                yq = TA[2]
                S.do("act", "activation", out=yq.v, in_=P.v, func=AF.Copy)
                if t == 0:
                    dbg("yret_raw", yq.v, [128, RW])
                yn2 = TA[3]
                head_norm(yn2.v, yq.v, EPS, TA[4].v)
                S.do("dve", "tensor_tensor", out=yn2.v, in0=yn2.v, in1=gng.v, op=ALU.mult)
                yield
                sg_ = TA[5]
                S.do("act", "activation", out=sg_.v, in_=zg.v, func=AF.Tanh, scale=0.5)
                S.do("dve", "scalar_tensor_tensor", out=sg_.v, in0=sg_.v, scalar=1.0, in1=zg.v, op0=ALU.add, op1=ALU.mult)
                S.do("dve", "scalar_tensor_tensor", out=ym[:, 512:1024], in0=yn2.v, scalar=0.5, in1=sg_.v,
                     op0=ALU.mult, op1=ALU.mult)
                if t == 0:
                    dbg("yret", ym[:, 512:1024], [128, RW])
                yield
                P = psum()
                Pb = P.v.cast(BF16)
                for k in range(KT):
                    S.do("pe", "transpose", out=Pb[:, k * 128:(k + 1) * 128], in_=ym[:, k * 128:(k + 1) * 128],
                         identity=identb.v)
                S.do("act", "activation", out=ymT.v.rr("p k t -> p (k t)"), in_=Pb[:, 0:1024], func=AF.Copy)
                yield
                for c2 in range(2):
                    cs_ = slice(c2 * 512, (c2 + 1) * 512)
                    P = psum()
                    for k in range(KT):
                        S.do("pe", "matmul", out=P.v, lhsT=ymT[:, k, :], rhs=Wo[:, k, cs_],
                             start=(k == 0), stop=(k == KT - 1))
                    S.do("dve", "tensor_tensor", out=TA[c2].v, in0=P.v, in1=gtA[:, cs_], op=ALU.mult)
                    S.do("pool", "tensor_tensor", out=xb[:, cs_], in0=xb[:, cs_], in1=TA[c2].v, op=ALU.add)
                S.dma("sp", x1s[t * 128:(t + 1) * 128, :], xb.v)
                if t == 0:
                    dbg("x1", xb.v, [128, D])

            def run_to_mid(g):
                for tok in g:
                    if tok == "mid":
                        return
            nt2 = ntiles if stage >= 13 else 0
            cur = a2_tile(0) if nt2 > 0 else None
            if cur is not None:
                run_to_mid(cur)
            for t in range(nt2):
                nxt = a2_tile(t + 1) if t + 1 < nt2 else None
                cur_done, nxt_mid = False, nxt is None
                while not (cur_done and nxt_mid):
                    if not cur_done:
                        try:
                            next(cur)
                        except StopIteration:
                            cur_done = True
                    if not nxt_mid:
                        try:
                            if next(nxt) == "mid":
                                nxt_mid = True
                        except StopIteration:
                            nxt_mid = True
                cur = nxt
            S.barrier()
            S.emit()

        with contextlib.ExitStack() as stB:
            Wu = sb(stB, "Wu", [128, KT, 2 * DFF], BF16)
            Wd = sb(stB, "Wd", [128, NJ, D], BF16)
            wup_v = wup_d.rearrange("(k p) c -> p k c", p=128)
            wdn_v = wdn_d.rearrange("(j p) c -> p j c", p=128)
            for k in range(KT):
                for c0 in range(0, 2 * DFF, 1408):
                    S.dma("pool", Wu[:, k, c0:c0 + 1408], wup_v[:, k, c0:c0 + 1408])
            Wu_k = lambda k, c0: Wu[:, k, c0:c0 + 128]
            fgb = sb(stB, "fgb", [128, D])
            S.dma("sp", fgb.v, fg_d.partition_broadcast(128))
            with contextlib.ExitStack() as stB0:
                gtF = sb(stB0, "gtF", [128, D])
                sc, scB = silu_c(stB0)
                wa = [sb(stB0, "wab%d" % i, [128, KT, 256]) for i in range(2)]
                bbc = [sb(stB0, "bbcb%d" % i, [128, 256]) for i in range(2)]
                for ci in range(20, 24):
                    mod_chunk(ci, wa, bbc, sc, scB, None, gtF)
                wst = [sb(stB0, "wst%d" % i, [128, D]) for i in range(2)]
                for j in range(NJ):
                    w_ = wst[j % 2]
                    S.dma("sp", w_.v, wdn_v[:, j, :])
                    S.do("dve" if j % 2 == 0 else "pool", "tensor_tensor", out=Wd[:, j, :], in0=w_.v, in1=gtF.v,
                         op=ALU.mult)
                S.barrier()
                S.emit()
            cw = sb(stB, "cw", [128, 2 * NJ, 3])
            cb = sb(stB, "cb", [128, 2 * NJ])
            S.dma("sp", cw.v, cwT_d.rearrange("p (j a) -> p j a", a=3))
            S.dma("sp", cb.v, cbT_d)
            xg = [sb(stB, "xg%d" % i, [128, D]) for i in range(2)]
            xr = [sb(stB, "xr%d" % i, [128, D]) for i in range(2)]
            xn2 = sb(stB, "xn2", [128, D])
            h2T = sb(stB, "h2T", [128, KT, 512], BF16)
            actT = sb(stB, "actT", [128, NJ, 512], BF16)
            acc = [sb(stB, "acc%d" % a, [128, 512]) for a in range(2)]
            corr = sb(stB, "corr", [128, 2 * NJ, 2])
            junkb = sb(stB, "junkb", [128, D], BF16)
            junk2 = junkb.v
            tail = sb(stB, "tail", [128, 2 * NJ, 2])
            st2 = [sb(stB, "st2_%d" % i, [128, 2]) for i in range(4)]
            S.do("pool", "memset", ap=tail.v, constant=0.0)
            nb = 0
            for g in range(ngroups):
                for tt in range(4):
                    xb = xg[nb % 2]
                    st_ = st2[nb % 2]
                    nb += 1
                    S.dma("sp", xb.v, x1s[(g * 4 + tt) * 128:(g * 4 + tt + 1) * 128, :])
                    S.do("act", "activation", out=junk2, in_=xb.v, func=AF.Square, accum_out=st_[:, 0:1])
                    S.do("dve", "tensor_scalar", out=st_[:, 0:1], in0=st_[:, 0:1], scalar1=1.0 / D, scalar2=EPS,
                         op0=ALU.mult, op1=ALU.add)
                    rsqrt_small(st_[:, 1:2], st_[:, 0:1])
                    S.do("act", "activation", out=xn2.v, in_=xb.v, func=AF.Copy, scale=st_[:, 1:2])
                    for half in range(2):
                        P = psum()
                        for kk_ in range(4):
                            k = half * 4 + kk_
                            S.do("pe", "transpose", out=P[:, kk_ * 128:(kk_ + 1) * 128],
                                 in_=xn2[:, k * 128:(k + 1) * 128], identity=sm("ident"))
                        for kk_ in range(4):
                            k = half * 4 + kk_
                            S.do("act", "activation", out=h2T[:, k, tt * 128:(tt + 1) * 128],
                                 in_=P[:, kk_ * 128:(kk_ + 1) * 128], func=AF.Identity,
                                 scale=gscF[:, k:k + 1], bias=modT[:, 24 + k:25 + k])
                if g == 0:
                    dbg("h2T", h2T.v, [128, KT, 512])
                S.do("dve", "tensor_tensor", out=corr[:, :, 0:1], in0=tail[:, :, 1:2], in1=cw[:, :, 1:2], op=ALU.mult)
                S.do("dve", "tensor_tensor", out=corr[:, :, 1:2], in0=tail[:, :, 0:1], in1=cw[:, :, 0:1], op=ALU.mult)
                S.do("dve", "tensor_tensor", out=corr[:, :, 0:1], in0=corr[:, :, 0:1], in1=corr[:, :, 1:2], op=ALU.add)
                S.do("dve", "tensor_tensor", out=corr[:, :, 1:2], in0=tail[:, :, 1:2], in1=cw[:, :, 0:1], op=ALU.mult)
                for j in range(NJ):
                    Pj = []
                    for a in range(2):
                        col = a * NJ + j
                        c0 = a * DFF + j * 128
                        P = psum()
                        Pj.append(P)
                        for k in range(KT):
                            S.do("pe", "matmul", out=P.v, lhsT=Wu_k(k, c0), rhs=h2T[:, k, :],
                                 start=(k == 0), stop=(k == KT - 1))
                        ac = acc[a]
                        S.do("act", "activation", out=ac.v, in_=P.v, func=AF.Identity, scale=cw[:, col, 2:3],
                             bias=cb[:, col:col + 1])
                        S.do("dve", "scalar_tensor_tensor", out=ac[:, 1:512], in0=P[:, 0:511], scalar=cw[:, col, 1:2],
                             in1=ac[:, 1:512], op0=ALU.mult, op1=ALU.add)
                        S.do("dve", "scalar_tensor_tensor", out=ac[:, 2:512], in0=P[:, 0:510], scalar=cw[:, col, 0:1],
                             in1=ac[:, 2:512], op0=ALU.mult, op1=ALU.add)
                        S.do("dve", "tensor_tensor", out=ac[:, 0:2], in0=ac[:, 0:2], in1=corr[:, col, :], op=ALU.add)
                        S.do("act", "activation", out=tail[:, col, :], in_=P[:, 510:512], func=AF.Copy)
                    av, ag = acc
                    Pv_, Pg_ = Pj
                    S.do("act", "activation", out=Pg_.v, in_=ag.v, func=AF.Tanh, scale=0.5)
                    S.do("dve", "scalar_tensor_tensor", out=Pv_.v, in0=Pg_.v, scalar=1.0, in1=ag.v,
                         op0=ALU.add, op1=ALU.mult)
                    S.do("dve", "scalar_tensor_tensor", out=actT[:, j, :], in0=av.v, scalar=0.5, in1=Pv_.v,
                         op0=ALU.mult, op1=ALU.mult)
                if g == 0:
                    dbg("actT", actT.v, [128, NJ, 512])
                for tt in range(4):
                    o_ = xr[tt % 2]
                    st_ = st2[2 + tt % 2]
                    S.dma("sp", o_.v, x1s[(g * 4 + tt) * 128:(g * 4 + tt + 1) * 128, :])
                    for c2 in range(2):
                        P = psum()
                        for j in range(NJ):
                            S.do("pe", "matmul", out=P.v, lhsT=actT[:, j, tt * 128:(tt + 1) * 128],
                                 rhs=Wd[:, j, c2 * 512:(c2 + 1) * 512], start=(j == 0), stop=(j == NJ - 1))
                        cs_ = slice(c2 * 512, (c2 + 1) * 512)
                        S.do("dve", "tensor_tensor", out=o_[:, cs_], in0=P.v, in1=o_[:, cs_], op=ALU.add)
                    S.do("act", "activation", out=junk2, in_=o_.v, func=AF.Square, accum_out=st_[:, 0:1])
                    S.do("dve", "tensor_scalar", out=st_[:, 0:1], in0=st_[:, 0:1], scalar1=1.0 / D, scalar2=EPS,
                         op0=ALU.mult, op1=ALU.add)
                    rsqrt_small(st_[:, 1:2], st_[:, 0:1])
                    S.do("dve", "scalar_tensor_tensor", out=o_.v, in0=o_.v, scalar=st_[:, 1:2], in1=fgb.v,
                         op0=ALU.mult, op1=ALU.mult)
                    S.dma("sp", out_d[(g * 4 + tt) * 128:(g * 4 + tt + 1) * 128, :], o_.v)
            S.barrier()
            S.emit()
    return nc, dbg_outs


_CACHE = {}


def make_in_maps(inp):
    f = lambda a: np.ascontiguousarray(np.asarray(a, dtype=np.float32))
    cst = _consts()
    x = f(inp["x"])
    c = f(inp["c"])
    shared = {
        "w_ada": f(inp["w_ada"][0]),
        "b_adaT": f(f(inp["b_ada"][0]).reshape(48, 128).T),
        "b_ada": f(inp["b_ada"][0]).reshape(1, -1),
        "gTa": f(f(inp["attn_norm_g"][0]).reshape(KT, 128).T),
        "gTf": f(f(inp["ffn_norm_g"][0]).reshape(KT, 128).T),
        "fg": f(inp["final_norm_g"]).reshape(1, -1),
        "w_in": f(inp["w_in"][0]),
        "mu": f(inp["rwkv_mu"][0]).reshape(1, -1),
        "w2x": f(np.concatenate([f(inp["rwkv_w2"][0]), f(inp["rwkv_w0"][0])[None, :]], axis=0)),
        "a2x": f(np.concatenate([f(inp["rwkv_a2"][0]), f(inp["rwkv_a0"][0])[None, :]], axis=0)),
        "g2": f(inp["rwkv_g2"][0]),
        "pvec": f(np.stack([f(inp["rwkv_k_k"][0]), f(inp["rwkv_k_a"][0]), f(inp["rwkv_r_k"][0]).reshape(-1),
                            f(inp["rwkv_ln_g"][0]), f(inp["rwkv_ln_b"][0]), f(inp["ret_gn_g"][0])], axis=0)),
        "w_out": f(inp["w_out"][0]),
        "w_up": f(inp["ffn_w_up"][0]),
        "cwT": f(f(inp["ffn_conv_w"][0]).reshape(3, 2 * NJ, 128).transpose(2, 1, 0).reshape(128, -1)),
        "cbT": f(f(inp["ffn_conv_b"][0]).reshape(2 * NJ, 128).T),
        "w_down": f(inp["ffn_w_down"][0]),
        "small": cst["small"], "DT": cst["DT"], "rot": cst["rot"],
    }
    maps = []
    for b in range(NCORES):
        m = dict(shared)
        m["x"] = f(x[b])
        m["cT"] = f(c[b].reshape(KT, 128).T)
        maps.append(m)
    return maps


def kernel(**inputs):
    if "nc" not in _CACHE:
        _CACHE["nc"] = build()[0]
    nc = _CACHE["nc"]
    maps = make_in_maps(inputs)
    res = run_bass_kernel_spmd(nc, maps, core_ids=list(range(NCORES)))
    out = np.stack([np.asarray(r["out"], dtype=np.float32) for r in res.results], axis=0)
    return out
```

```python
import contextlib
import math
import numpy as np
import concourse.bass as bass
import concourse.mybir as mybir
from concourse.bass_utils import run_bass_kernel_spmd

F32 = mybir.dt.float32
BF16 = mybir.dt.bfloat16
AF = mybir.ActivationFunctionType
ALU = mybir.AluOpType
AX = mybir.AxisListType

NCORES = 8
S_LEN = 4096
D = 1024
NT = S_LEN // 128
KT = D // 128
DFF = 2816
NJ = DFF // 128
RW = 512
EPS = 1e-6
LN_EPS = 64e-5
WSC = math.exp(-0.5)


class View:
    __slots__ = ("buf", "ap")

    def __init__(self, buf, ap):
        self.buf = buf
        self.ap = ap

    def __getitem__(self, idx):
        return View(self.buf, self.ap[idx])

    def rr(self, pat, **kw):
        return View(self.buf, self.ap.rearrange(pat, **kw))

    def bc(self, shape):
        return View(self.buf, self.ap.broadcast_to(list(shape)))

    def cast(self, dt):
        return View(self.buf, self.ap.bitcast(dt))


class Buf:
    __slots__ = ("name", "t", "writer", "readers", "dsem", "dcnt", "dram")

    def __init__(self, name, t, dram=False):
        self.name = name
        self.t = t
        self.dram = dram
        self.writer = None
        self.readers = []
        self.dsem = None
        self.dcnt = 0

    def __getitem__(self, idx):
        return View(self, self.t[idx])

    @property
    def v(self):
        return View(self, self.t[:])


class Sched:
    ENGS = ("pe", "act", "dve", "pool", "sp")
    WKEYS = ("out", "accum_out", "ap")

    def __init__(self, nc, same_engine_raw=True):
        self.nc = nc
        self.sem = {}
        self.cnt = {e: 0 for e in self.ENGS}
        self.seen = {e: {} for e in self.ENGS}
        self.same_engine_raw = same_engine_raw
        self.q = {e: [] for e in self.ENGS}
        self.dbufs = []
        self.ninst = 0
        self.nwaits = 0

    def open(self, stack):
        self.stack = stack
        for e in self.ENGS:
            self.sem[e] = stack.enter_context(self.nc.semaphore("s_" + e))

    def _emit_waits(self, e, waits):
        seen = self.seen[e]
        for sem, val in waits:
            k = id(sem)
            if seen.get(k, 0) >= val:
                continue
            seen[k] = val
            self.q[e].append(("wait", sem, val))
            self.nwaits += 1

    def op(self, e, fn, reads, writes):
        waits = []
        for b in reads:
            w = b.writer
            if w is not None and (w[2] != e or self.same_engine_raw):
                waits.append(w[:2])
        for b in writes:
            w = b.writer
            if w is not None and w[2] != e:
                waits.append(w[:2])
            for rd in b.readers:
                if rd[2] != e:
                    waits.append(rd[:2])
        self._emit_waits(e, waits)
        self.cnt[e] += 1
        self.q[e].append(("op", fn, self.sem[e], 1))
        self.ninst += 1
        tok = (self.sem[e], self.cnt[e], e)
        for b in writes:
            b.writer = tok
            b.readers = []
        for b in reads:
            if b in writes:
                continue
            b.readers = [rd for rd in b.readers if rd[2] != e] + [tok]

    def do(self, e, method, **kw):
        reads, writes, real = [], [], {}
        for k, v in kw.items():
            if isinstance(v, View):
                (writes if k in self.WKEYS else reads).append(v.buf)
                real[k] = v.ap
            else:
                real[k] = v
        self.op(e, lambda eng: getattr(eng, method)(**real), reads, writes)

    def dma(self, e, out, in_, **kw):
        reads, writes = [], []
        owner = None
        if isinstance(in_, View):
            reads.append(in_.buf)
            if not in_.buf.dram:
                owner = in_.buf
            in_ = in_.ap
        if isinstance(out, View):
            writes.append(out.buf)
            if not out.buf.dram:
                owner = out.buf
            out = out.ap
        waits = []
        for b in reads:
            if b.writer is not None:
                waits.append(b.writer[:2])
        for b in writes:
            if b.writer is not None:
                waits.append(b.writer[:2])
            for rd in b.readers:
                waits.append(rd[:2])
        self._emit_waits(e, waits)
        if owner.dsem is None:
            owner.dsem = self.stack.enter_context(self.nc.semaphore("d%d_%s" % (len(self.dbufs), owner.name)))
            self.dbufs.append(owner)
        owner.dcnt += 16
        self.q[e].append(("op", (lambda eng, o=out, i=in_, kw=kw: eng.dma_start(out=o, in_=i, **kw)),
                          owner.dsem, 16))
        self.ninst += 1
        tok = (owner.dsem, owner.dcnt, "dma")
        for b in writes:
            b.writer = tok
            b.readers = []
        for b in reads:
            b.readers = b.readers + [tok]

    def barrier(self):
        for e in self.ENGS:
            waits = [(self.sem[o], self.cnt[o]) for o in self.ENGS if o != e and self.cnt[o] > 0]
            waits += [(b.dsem, b.dcnt) for b in self.dbufs]
            self._emit_waits(e, waits)

    def emit(self):
        def replay(q, eng):
            for it in q:
                if it[0] == "wait":
                    eng.wait_ge(it[1], it[2])
                else:
                    it[1](eng).then_inc(it[2], it[3])
        q = self.q
        self.q = {e: [] for e in self.ENGS}
        with self.nc.Block() as block:
            @block.tensor
            def _(eng):
                replay(q["pe"], eng)

            @block.scalar
            def _(eng):
                replay(q["act"], eng)

            @block.vector
            def _(eng):
                replay(q["dve"], eng)

            @block.gpsimd
            def _(eng):
                replay(q["pool"], eng)

            @block.sync
            def _(eng):
                replay(q["sp"], eng)


def _consts():
    idx = np.arange(128)
    ch = idx // 64
    same = ch[:, None] == ch[None, :]
    UI = (same & (idx[:, None] <= idx[None, :])).astype(np.float32)
    US = (same & (idx[:, None] < idx[None, :])).astype(np.float32)
    LS = (same & (idx[:, None] > idx[None, :])).astype(np.float32)
    sel2 = np.stack([(ch == 0), (ch == 1)], axis=1).astype(np.float32)
    maskA = np.concatenate([US, UI, US, UI], axis=1)
    LS4 = np.concatenate([LS] * 4, axis=1)
    ident = np.eye(128, dtype=np.float32)
    H = 8
    lg = np.log1p(-(2.0 ** (-5.0 - np.arange(H, dtype=np.float32)))).astype(np.float32)
    li = (idx % 64).astype(np.float32)
    dist = np.abs(li[:, None] - li[None, :])
    DT = np.zeros((128, H, 128), np.float32)
    for h in range(H):
        DT[:, (h % 2) * 4 + h // 2, :] = np.where(same, np.exp(lg[h] * dist), 0.0)
    qdec = np.exp(lg[None, :] * (li[:, None] + 1.0)).astype(np.float32)
    kdec = np.exp(lg[None, :] * (63.0 - li[:, None])).astype(np.float32)
    g64 = np.exp(lg * 64.0).astype(np.float32)
    gam = np.zeros((128, 4), np.float32)
    for i in range(4):
        gam[0:64, i] = g64[2 * i]
        gam[64:128, i] = g64[2 * i + 1]
    pos = np.arange(S_LEN, dtype=np.float32)
    inv = (10000.0 ** (-np.arange(0, 64, 2, dtype=np.float32) / 64)).astype(np.float32)
    ang = (pos[:, None] * inv[None, :]).astype(np.float32)
    c, s = np.cos(ang).astype(np.float32), np.sin(ang).astype(np.float32)
    CC = np.concatenate([c, c], axis=1)
    SS = np.concatenate([-s, s], axis=1)
    rot = np.stack([CC * 0.125, SS * 0.125, CC, SS], axis=1).astype(np.float32)
    rot = rot.reshape(NT, 128, 256)
    small = np.concatenate([UI, US, LS, ident, maskA, LS4, sel2, qdec, kdec, gam], axis=1)
    return dict(small=np.ascontiguousarray(small), DT=np.ascontiguousarray(DT.reshape(128, 1024)),
                rot=np.ascontiguousarray(rot))


SM_OFF = {}
_o = 0
for _n, _w in (("UI", 128), ("US", 128), ("LS", 128), ("ident", 128), ("maskA", 512), ("LS4", 512),
               ("sel2", 2), ("qdec", 8), ("kdec", 8), ("gam", 4)):
    SM_OFF[_n] = (_o, _o + _w)
    _o += _w
SM_W = _o


def build(debug=False, ntiles=NT, ngroups=NT // 4, stage=99):
    nc = bass.Bass("TRN2", target_bir_lowering=False)

    def din(name, shape):
        return nc.dram_tensor(name, list(shape), F32, kind="ExternalInput").ap()

    x_d = din("x", [S_LEN, D])
    cT_d = din("cT", [128, KT])
    wada_d = din("w_ada", [D, 6 * D])
    badaT_d = din("b_adaT", [128, 48])
    bada_d = din("b_ada", [1, 6 * D])
    gTa_d = din("gTa", [128, KT])
    gTf_d = din("gTf", [128, KT])
    fg_d = din("fg", [1, D])
    win_d = din("w_in", [D, 3840])
    mu_d = din("mu", [1, 1792])
    w2x_d = din("w2x", [65, RW])
    a2x_d = din("a2x", [65, RW])
    g2_d = din("g2", [128, RW])
    pvec_d = din("pvec", [6, RW])
    wout_d = din("w_out", [D, D])
    wup_d = din("w_up", [D, 2 * DFF])
    cwT_d = din("cwT", [128, 2 * NJ * 3])
    cbT_d = din("cbT", [128, 2 * NJ])
    wdn_d = din("w_down", [DFF, D])
    small_d = din("small", [128, SM_W])
    DT_d = din("DT", [128, 1024])
    rot_d = din("rot", [NT, 128, 256])
    out_d = nc.dram_tensor("out", [S_LEN, D], F32, kind="ExternalOutput").ap()
    x1s_t = nc.dram_tensor("x1s", [S_LEN, D], F32, kind="Internal")
    dbg_outs = {}

    with contextlib.ExitStack() as st0:
        S = Sched(nc)
        S.open(st0)

        name_ctr = [0]

        def sb(stack, name, shape, dt=F32):
            name_ctr[0] += 1
            return Buf(name, stack.enter_context(nc.sbuf_tensor("sb%d_%s" % (name_ctr[0], name), list(shape), dt)))

        def dbg(name, view, shape):
            if not debug or name in dbg_outs:
                return
            t = nc.dram_tensor("dbg_" + name, list(shape), F32, kind="ExternalOutput").ap()
            dbg_outs[name] = t
            S.dma("pool", t, view)

        PS = [Buf("ps%d" % i, st0.enter_context(nc.psum_tensor("ps%d" % i, [128, 512], F32))) for i in range(8)]
        ps_rr = [0]

        def psum():
            b = PS[ps_rr[0] % 8]
            ps_rr[0] += 1
            return b

        x1s = Buf("x1s", x1s_t.ap(), dram=True)

        small = sb(st0, "small", [128, SM_W])
        S.dma("sp", small.v, small_d)

        def sm(name):
            a, b = SM_OFF[name]
            return small[:, a:b]

        identb = sb(st0, "identb", [128, 128], BF16)
        S.do("dve", "tensor_copy", out=identb.v, in_=sm("ident"))
        modT = sb(st0, "modT", [128, 48])
        gscA = sb(st0, "gscA", [128, KT])
        gscF = sb(st0, "gscF", [128, KT])
        gtA = sb(st0, "gtA", [128, D])
        mhalf = sb(st0, "mhalf", [128, 8])
        S.do("pool", "memset", ap=mhalf.v, constant=-0.5)

        def rsqrt_small(dst, src):
            n = src.ap.shape[-1]
            S.do("pool", "tensor_tensor", out=dst, in0=src, in1=mhalf[:, 0:n], op=ALU.pow)

        hview = lambda v: v.rr("p (h d) -> p h d", h=8)
        wada_v = wada_d.rearrange("(k p) c -> p k c", p=128)
        win_v = win_d.rearrange("(k p) c -> p k c", p=128)
        wout_v = wout_d.rearrange("(k p) c -> p k c", p=128)

        def silu_c(stack):
            cT = sb(stack, "cT", [128, KT])
            sc = sb(stack, "sc", [128, KT])
            scB = sb(stack, "scB", [128, KT, 128])
            S.dma("sp", cT.v, cT_d)
            S.do("act", "activation", out=sc.v, in_=cT.v, func=AF.Tanh, scale=0.5)
            S.do("dve", "scalar_tensor_tensor", out=sc.v, in0=sc.v, scalar=1.0, in1=cT.v,
                 op0=ALU.add, op1=ALU.mult)
            S.do("dve", "tensor_scalar", out=sc.v, in0=sc.v, scalar1=0.5, scalar2=None, op0=ALU.mult)
            S.do("dve", "tensor_copy", out=scB.v, in_=sc[:, :, None].bc([128, KT, 128]))
            return sc, scB

        def mod_chunk(ci, wa, bbc, sc, scB, badaT, gt_dst):
            w_ = wa[ci % 2]
            S.dma("sp", w_.v, wada_v[:, :, ci * 256:(ci + 1) * 256])
            P = psum()
            if gt_dst is not None:
                b_ = bbc[ci % 2]
                S.dma("sp", b_.v, bada_d[:, ci * 256:(ci + 1) * 256].partition_broadcast(128))
                for k in range(KT):
                    S.do("pe", "matmul", out=P[:, 0:256], lhsT=scB[:, k, :], rhs=w_[:, k, :],
                         start=(k == 0), stop=(k == KT - 1))
                c0 = (ci % 4) * 256
                S.do("dve", "tensor_tensor", out=gt_dst[:, c0:c0 + 256], in0=P[:, 0:256], in1=b_.v, op=ALU.add)
            else:
                for nt_ in range(2):
                    for k in range(KT):
                        S.do("pe", "matmul", out=P[:, nt_:nt_ + 1],
                             lhsT=w_[:, k, nt_ * 128:(nt_ + 1) * 128], rhs=sc[:, k:k + 1],
                             start=(k == 0), stop=(k == KT - 1))
                S.do("dve", "tensor_tensor", out=modT[:, ci * 2:ci * 2 + 2], in0=P[:, 0:2],
                     in1=badaT[:, ci * 2:ci * 2 + 2], op=ALU.add)

        class NormT:
            def __init__(self, stack, tag, gsc, sh_col0):
                self.xt = [sb(stack, tag + "xt%d" % i, [128, D]) for i in range(2)]
                self.xn = sb(stack, tag + "xn", [128, D])
                self.junk = sb(stack, tag + "junk", [128, D], BF16)
                self.hT = [sb(stack, tag + "hT%d" % i, [128, KT, 129], BF16) for i in range(2)]
                self.st = [sb(stack, tag + "nst%d" % i, [128, 2]) for i in range(2)]
                self.gsc, self.sh0 = gsc, sh_col0
                S.do("pool", "memset", ap=self.hT[1][:, :, 128:129], constant=0.0)

            def run(self, t, src_rows):
                xb, hc, hp = self.xt[t % 2], self.hT[t % 2], self.hT[(t + 1) % 2]
                st_ = self.st[t % 2]
                S.dma("sp", xb.v, src_rows)
                S.do("act", "activation", out=self.junk.v, in_=xb.v, func=AF.Square, accum_out=st_[:, 0:1])
                S.do("dve", "tensor_scalar", out=st_[:, 0:1], in0=st_[:, 0:1], scalar1=1.0 / D, scalar2=EPS,
                     op0=ALU.mult, op1=ALU.add)
                rsqrt_small(st_[:, 1:2], st_[:, 0:1])
                S.do("act", "activation", out=self.xn.v, in_=xb.v, func=AF.Copy, scale=st_[:, 1:2])
                S.do("pool", "tensor_copy", out=hc[:, :, 0:1], in_=hp[:, :, 128:129])
                for half in range(2):
                    P = psum()
                    for kk_ in range(4):
                        k = half * 4 + kk_
                        S.do("pe", "transpose", out=P[:, kk_ * 128:(kk_ + 1) * 128],
                             in_=self.xn[:, k * 128:(k + 1) * 128], identity=sm("ident"))
                    for kk_ in range(4):
                        k = half * 4 + kk_
                        S.do("act", "activation", out=hc[:, k, 1:129], in_=P[:, kk_ * 128:(kk_ + 1) * 128],
                             func=AF.Identity, scale=self.gsc[:, k:k + 1],
                             bias=modT[:, self.sh0 + k:self.sh0 + k + 1])
                return xb, hc

        st1 = [sb(st0, "st1_%d" % i, [128, 8]) for i in range(12)]
        st_i = [0]

        def stat():
            b = st1[st_i[0] % len(st1)]
            st_i[0] += 1
            return b

        def head_norm(dst, src, eps, sq):
            s1, s2, mean, var = stat(), stat(), stat(), stat()
            S.do("dve", "tensor_reduce", out=s1.v, in_=hview(src), axis=AX.X, op=ALU.add)
            S.do("act", "activation", out=sq, in_=src, func=AF.Square)
            S.do("dve", "tensor_reduce", out=s2.v, in_=hview(sq), axis=AX.X, op=ALU.add)
            S.do("dve", "tensor_scalar", out=mean.v, in0=s1.v, scalar1=1.0 / 64, scalar2=None, op0=ALU.mult)
            S.do("dve", "tensor_tensor", out=var.v, in0=mean.v, in1=mean.v, op=ALU.mult)
            S.do("dve", "scalar_tensor_tensor", out=var.v, in0=s2.v, scalar=1.0 / 64, in1=var.v,
                 op0=ALU.mult, op1=ALU.subtract)
            S.do("dve", "tensor_scalar", out=var.v, in0=var.v, scalar1=eps, scalar2=None, op0=ALU.add)
            rs = stat()
            rsqrt_small(rs.v, var.v)
            S.do("dve", "tensor_tensor", out=hview(dst), in0=hview(src),
                 in1=mean[:, :, None].bc([128, 8, 64]), op=ALU.subtract)
            S.do("dve", "tensor_tensor", out=hview(dst), in0=hview(dst),
                 in1=rs[:, :, None].bc([128, 8, 64]), op=ALU.mult)

        def proj_T(dst, hc, W, c0, W2=None):
            P = psum()
            n = 2 * KT if W2 is not None else KT
            i = 0
            for k in range(KT):
                S.do("pe", "matmul", out=P.v, lhsT=hc[:, k, 1:129], rhs=W[:, k, c0:c0 + 512],
                     start=(i == 0), stop=(i == n - 1))
                i += 1
            if W2 is not None:
                for k in range(KT):
                    S.do("pe", "matmul", out=P.v, lhsT=hc[:, k, 0:128], rhs=W2[:, k, c0:c0 + 512],
                         start=False, stop=(i == n - 1))
                    i += 1
            S.do("act", "activation", out=dst, in_=P.v, func=AF.Copy)

        yrs = Buf("yrs", nc.dram_tensor("yrs", [S_LEN, RW], BF16, kind="Internal").ap(), dram=True)

        with contextlib.ExitStack() as stA:
            W1 = sb(stA, "W1", [128, KT, 1792], BF16)
            W2 = sb(stA, "W2", [128, KT, 1792], BF16)
            w2x = sb(stA, "w2x", [65, RW])
            a2x = sb(stA, "a2x", [65, RW])
            g2 = sb(stA, "g2", [128, RW])
            pv = [sb(stA, "pv%d" % i, [128, RW]) for i in range(5)]
            S.dma("sp", w2x.v, w2x_d)
            S.dma("sp", a2x.v, a2x_d)
            S.dma("sp", g2.v, g2_d)
            for i in range(5):
                S.dma("sp", pv[i].v, pvec_d[i:i + 1, :].partition_broadcast(128))

            with contextlib.ExitStack() as st00:
                mu_bc = sb(st00, "mu_bc", [128, 1792])
                omm = sb(st00, "omm", [128, 1792])
                S.dma("sp", mu_bc.v, mu_d.partition_broadcast(128))
                S.do("dve", "tensor_scalar", out=omm.v, in0=mu_bc.v, scalar1=-1.0, scalar2=1.0,
                     op0=ALU.mult, op1=ALU.add)
                stg = [sb(st00, "stg%d" % i, [128, 1792]) for i in range(2)]
                for k in range(KT):
                    sg = stg[k % 2]
                    S.dma("sp", sg.v, win_v[:, k, 0:1792])
                    S.do("dve", "tensor_tensor", out=W1[:, k, :], in0=sg.v, in1=omm.v, op=ALU.mult)
                    S.do("pool", "tensor_tensor", out=W2[:, k, :], in0=sg.v, in1=mu_bc.v, op=ALU.mult)
                sc, scB = silu_c(st00)
                badaT = sb(st00, "badaT", [128, 48])
                gTa = sb(st00, "gTa", [128, KT])
                gTf = sb(st00, "gTf", [128, KT])
                S.dma("sp", badaT.v, badaT_d)
                S.dma("sp", gTa.v, gTa_d)
                S.dma("sp", gTf.v, gTf_d)
                wa = [sb(st00, "wa%d" % i, [128, KT, 256]) for i in range(2)]
                bbc = [sb(st00, "bbc%d" % i, [128, 256]) for i in range(2)]
                for ci in range(24):
                    if 8 <= ci < 12:
                        mod_chunk(ci, wa, bbc, sc, scB, badaT, gtA)
                    elif ci >= 20:
                        continue
                    else:
                        mod_chunk(ci, wa, bbc, sc, scB, badaT, None)
                S.do("dve", "scalar_tensor_tensor", out=gscA.v, in0=modT[:, 8:16], scalar=1.0, in1=gTa.v,
                     op0=ALU.add, op1=ALU.mult)
                S.do("dve", "scalar_tensor_tensor", out=gscF.v, in0=modT[:, 32:40], scalar=1.0, in1=gTf.v,
                     op0=ALU.add, op1=ALU.mult)
                dbg("modT", modT.v, [128, 48])
                dbg("gtA", gtA.v, [128, D])
                S.barrier()
                S.emit()

            NA = NormT(stA, "a1", gscA, 0)
            Zs = [[sb(stA, "Z%d_%d" % (p_, i), [128, RW]) for i in range(3)] for p_ in range(2)]
            TAs = [[sb(stA, "TA%d_%d" % (p_, i), [128, RW]) for i in range(10)] for p_ in range(2)]
            TB = [sb(stA, "TB%d" % i, [128, RW], BF16) for i in range(9)]
            lo_ws = [sb(stA, "lo_w%d" % p_, [65, 128]) for p_ in range(2)]
            lo_as = [sb(stA, "lo_a%d" % p_, [65, 128]) for p_ in range(2)]
            lo_gs = [sb(stA, "lo_g%d" % p_, [128, 128]) for p_ in range(2)]
            for b_ in lo_ws + lo_as:
                S.do("pool", "memset", ap=b_.v, constant=1.0)
            WCs = [sb(stA, "WC%d" % p_, [128, 4, 2]) for p_ in range(2)]
            Fh = sb(stA, "Fh", [128, 4, 512], BF16)
            AM = sb(stA, "AM", [128, 8, 512], BF16)
            Lp = [sb(stA, "Lp%d" % i, [128, 8, 128], BF16) for i in range(2)]
            Np = [sb(stA, "Np%d" % i, [128, 8, 128], BF16) for i in range(2)]
            X = [sb(stA, "X%d" % i, [128, 8, 128], BF16) for i in range(2)]
            RhT = sb(stA, "RhT", [128, 4, 128], BF16)
            MpT = sb(stA, "MpT", [128, 4, 2, 128], BF16)
            H32 = sb(stA, "H32", [128, 4, 64])
            Hc = [sb(stA, "Hc%d" % i, [128, 4, 64], BF16) for i in range(3)]
            Hbd = [sb(stA, "Hbd%d" % i, [128, 4, 128], BF16) for i in range(3)]
            slot = lambda h: (h % 2) * 4 + h // 2
            yrw = [sb(stA, "yrw%d" % i, [128, RW], BF16) for i in range(2)]
            S.do("pool", "memset", ap=H32.v, constant=0.0)
            S.do("pool", "memset", ap=Hc[0].v, constant=0.0)
            S.do("pool", "memset", ap=MpT.v, constant=0.0)
            for b_ in Hbd:
                S.do("pool", "memset", ap=b_.v, constant=0.0)

            def proj_F(hc, c0, m):
                P = psum()
                i = 0
                for W_, sl in ((W1, slice(1, 129)), (W2, slice(0, 128))):
                    for k in range(KT):
                        S.do("pe", "matmul", out=P[0:m, 0:128], lhsT=W_[:, k, c0:c0 + m], rhs=hc[:, k, sl],
                             start=(i == 0), stop=(i == 2 * KT - 1))
                        i += 1
                return P[0:m, 0:128]

            def a1_tile(t):
                Z, TA, WC = Zs[t % 2], TAs[t % 2], WCs[t % 2]
                lo_w, lo_a, lo_g = lo_ws[t % 2], lo_as[t % 2], lo_gs[t % 2]
                xb, hc = NA.run(t, x_d[t * 128:(t + 1) * 128, :])
                if t == 0:
                    dbg("hT", hc[:, :, 1:129], [128, KT, 128])
                yield
                if stage < 2:
                    return
                zr, zk, zv = Z[0], Z[1], Z[2]
                proj_T(zr.v, hc, W1, 0, W2)
                yield
                proj_T(zk.v, hc, W1, 512, W2)
                yield
                proj_T(zv.v, hc, W1, 1024, W2)
                Pw = proj_F(hc, 1536, 64)
                S.do("act", "activation", out=lo_w[0:64, :], in_=Pw, func=AF.Tanh)
                yield
                Pa_ = proj_F(hc, 1600, 64)
                S.do("act", "activation", out=lo_a[0:64, :], in_=Pa_, func=AF.Copy)
                yield
                Pg = proj_F(hc, 1664, 128)
                S.do("act", "activation", out=lo_g.v, in_=Pg, func=AF.Tanh, scale=0.5)
                S.do("dve", "tensor_scalar", out=lo_g.v, in0=lo_g.v, scalar1=0.5, scalar2=0.5,
                     op0=ALU.mult, op1=ALU.add)
                if t == 0:
                    dbg("zr", zr.v, [128, RW])
                    dbg("zv", zv.v, [128, RW])
                yield
                if stage < 3:
                    return
                lw, am1, gsb = TA[0], TA[1], TA[2]
                P = psum()
                S.do("pe", "matmul", out=P.v, lhsT=lo_w.v, rhs=w2x.v, start=True, stop=True)
                S.do("act", "activation", out=lw.v, in_=P.v, func=AF.Tanh, scale=0.5)
                S.do("dve", "tensor_scalar", out=lw.v, in0=lw.v, scalar1=1.0, scalar2=-0.5 * WSC,
                     op0=ALU.add, op1=ALU.mult)
                P = psum()
                S.do("pe", "matmul", out=P.v, lhsT=lo_a.v, rhs=a2x.v, start=True, stop=True)
                S.do("act", "activation", out=am1.v, in_=P.v, func=AF.Tanh, scale=0.5)
                S.do("dve", "tensor_scalar", out=am1.v, in0=am1.v, scalar1=0.5, scalar2=-0.5,
                     op0=ALU.mult, op1=ALU.add)
                P = psum()
                S.do("pe", "matmul", out=P.v, lhsT=lo_g.v, rhs=g2.v, start=True, stop=True)
                S.do("act", "activation", out=gsb.v, in_=P.v, func=AF.Copy)
                if t == 0:
                    dbg("lw", lw.v, [128, RW])
                    dbg("am1", am1.v, [128, RW])
                    dbg("gsb", gsb.v, [128, RW])
                yield
                if stage < 4:
                    return
                Wc, Winv, Wprev, Ee = TA[3], TA[4], TA[5], TA[6]
                P = psum()
                S.do("pe", "matmul", out=P.v, lhsT=sm("UI"), rhs=lw.v, start=True, stop=True)
                S.do("act", "activation", out=Wc.v, in_=P.v, func=AF.Exp)
                S.do("act", "activation", out=Winv.v, in_=P.v, func=AF.Exp, scale=-1.0)
                P = psum()
                S.do("pe", "matmul", out=P.v, lhsT=sm("US"), rhs=lw.v, start=True, stop=True)
                S.do("act", "activation", out=Wprev.v, in_=P.v, func=AF.Exp)
                P = psum()
                S.do("pe", "matmul", out=P.v, lhsT=sm("LS"), rhs=lw.v, start=True, stop=True)
                S.do("act", "activation", out=Ee.v, in_=P.v, func=AF.Exp)
                P = psum()
                for i in range(4):
                    S.do("pe", "matmul", out=P[:, 2 * i:2 * i + 2], lhsT=lw[:, i * 128:(i + 1) * 128],
                         rhs=sm("sel2"), start=True, stop=True)
                S.do("act", "activation", out=WC.v, in_=P[:, 0:8].rr("p (i c) -> p i c", i=4), func=AF.Exp)
                yield "mid"
                if stage < 5:
                    return
                kk0, k2, kka = TA[7], TA[8], TA[9]
                n2, rn = stat(), stat()
                S.do("dve", "tensor_tensor", out=kk0.v, in0=zk.v, in1=pv[0].v, op=ALU.mult)
                S.do("pool", "tensor_tensor", out=k2.v, in0=kk0.v, in1=kk0.v, op=ALU.mult)
                S.do("dve", "tensor_reduce", out=n2.v, in_=hview(k2.v), axis=AX.X, op=ALU.add)
                yield
                if stage < 5.1:
                    return
                S.do("dve", "tensor_scalar", out=n2.v, in0=n2.v, scalar1=1e-24, scalar2=None, op0=ALU.max)
                rsqrt_small(rn.v, n2.v)
                yield
                if stage < 5.2:
                    return
                S.do("dve", "tensor_tensor", out=hview(kk0.v), in0=hview(kk0.v),
                     in1=rn[:, :, None].bc([128, 8, 64]), op=ALU.mult)
                yield
                if stage < 5.3:
                    return
                S.do("pool", "tensor_tensor", out=k2.v, in0=am1.v, in1=pv[1].v, op=ALU.mult)
                S.do("dve", "scalar_tensor_tensor", out=k2.v, in0=k2.v, scalar=1.0, in1=zk.v,
                     op0=ALU.add, op1=ALU.mult)
                S.do("dve", "scalar_tensor_tensor", out=kka.v, in0=am1.v, scalar=1.0, in1=kk0.v,
                     op0=ALU.add, op1=ALU.mult)
                yield
                if stage < 5.4:
                    return
                at, bt, kt, rtl, Vb, Bh0, Bh1, Kh0, Kh1 = TB
                Bhc, Khc = (Bh0, Bh1), (Kh0, Kh1)
                S.do("dve", "scalar_tensor_tensor", out=at.v, in0=kk0.v, scalar=-1.0, in1=Wprev.v,
                     op0=ALU.mult, op1=ALU.mult)
                S.do("pool", "tensor_tensor", out=bt.v, in0=kka.v, in1=Winv.v, op=ALU.mult)
                S.do("dve", "tensor_tensor", out=kt.v, in0=k2.v, in1=Winv.v, op=ALU.mult)
                S.do("pool", "tensor_tensor", out=rtl.v, in0=zr.v, in1=Wc.v, op=ALU.mult)
                for c in range(2):
                    S.do("dve", "scalar_tensor_tensor", out=Bhc[c].v, in0=kka.v, scalar=sm("sel2")[:, c:c + 1], in1=Ee.v,
                         op0=ALU.mult, op1=ALU.mult)
                    S.do("dve", "scalar_tensor_tensor", out=Khc[c].v, in0=k2.v, scalar=sm("sel2")[:, c:c + 1], in1=Ee.v,
                         op0=ALU.mult, op1=ALU.mult)
                S.do("act", "activation", out=Vb.v, in_=zv.v, func=AF.Copy)
                yield
                if stage < 5.5:
                    return
                bn = stat()
                S.do("pool", "tensor_tensor", out=Wc.v, in0=zr.v, in1=pv[2].v, op=ALU.mult)
                S.do("dve", "tensor_tensor", out=Wc.v, in0=Wc.v, in1=k2.v, op=ALU.mult)
                S.do("dve", "tensor_reduce", out=bn.v, in_=hview(Wc.v), axis=AX.X, op=ALU.add)
                if t == 0:
                    dbg("at", at.v, [128, RW])
                    dbg("bt", bt.v, [128, RW])
                    dbg("Kh1", Kh1.v, [128, RW])
                yield
                if stage < 6:
                    return
                for i in range(4):
                    P = psum()
                    Pb = P.v.cast(BF16)
                    for j, src in enumerate((at, rtl, bt, kt)):
                        S.do("pe", "transpose", out=Pb[:, j * 128:(j + 1) * 128], in_=src[:, i * 128:(i + 1) * 128],
                             identity=identb.v)
                    S.do("act", "activation", out=Fh[:, i, :], in_=Pb[:, 0:512], func=AF.Copy)
                yield
                if stage < 7:
                    return
                for h in range(8):
                    i, pb = h // 2, 64 * (h % 2)
                    P = psum()
                    S.do("pe", "matmul", out=P[:, 0:256], lhsT=Fh[pb:pb + 64, i, 256:384], rhs=Fh[pb:pb + 64, i, 0:256],
                         start=True, stop=True)
                    S.do("pe", "matmul", out=P[:, 256:512], lhsT=Fh[pb:pb + 64, i, 384:512], rhs=Fh[pb:pb + 64, i, 0:256],
                         start=True, stop=True)
                    S.do("dve", "tensor_tensor", out=AM[:, slot(h), :], in0=P.v, in1=sm("maskA"), op=ALU.mult)
                for hg in range(2):
                    P = psum()
                    for hh in range(4):
                        h = hh * 2 + hg
                        i, pb = h // 2, 64 * (h % 2)
                        S.do("pe", "matmul", out=P[:, hh * 128:(hh + 1) * 128], lhsT=Fh[pb:pb + 64, i, 0:128],
                             rhs=Fh[pb:pb + 64, i, 256:384], start=True, stop=True)
                    S.do("dve", "tensor_tensor", out=Lp[0][:, hg * 4:hg * 4 + 4, :].rr("p h s -> p (h s)"), in0=P.v,
                         in1=sm("LS4"), op=ALU.mult)
                yield
                if stage < 8:
                    return
                P = psum()
                for h in range(8):
                    sl_ = slot(h)
                    S.do("pe", "matmul", out=P[:, sl_ * 64:(sl_ + 1) * 64], lhsT=AM[:, sl_, 256:384],
                         rhs=Vb[:, h * 64:(h + 1) * 64], start=True, stop=True)
                S.do("act", "activation", out=X[0][:, :, 64:128], in_=hview(P.v), func=AF.Copy)
                atv = at.v.rr("p (hh hg d) -> p hg hh d", hh=4, hg=2)
                for hg in range(2):
                    S.do("pool", "tensor_copy", out=X[0][:, hg * 4:hg * 4 + 4, 0:64], in_=atv[:, hg])
                yield
                if stage < 9:
                    return
                xcur = 0
                for lv in range(6):
                    if lv == 0:
                        Ncur = lambda s_: AM[:, s_, 0:128]
                    else:
                        Ncur = (lambda s_, Nb=Np[lv % 2]: Nb[:, s_, :])
                    Lcur = Lp[lv % 2]
                    for hg in range(2):
                        P = psum()
                        for hh in range(4):
                            s_ = hg * 4 + hh
                            S.do("pe", "matmul", out=P[:, hh * 128:(hh + 1) * 128], lhsT=Ncur(s_),
                                 rhs=X[xcur][:, s_, :], start=True, stop=True)
                        S.do("dve", "tensor_tensor", out=X[1 - xcur][:, hg * 4:hg * 4 + 4, :].rr("p h s -> p (h s)"),
                             in0=P.v, in1=X[xcur][:, hg * 4:hg * 4 + 4, :].rr("p h s -> p (h s)"), op=ALU.add)
                    xcur = 1 - xcur
                    if lv == 5:
                        break
                    for hg in range(2):
                        P = psum()
                        for hh in range(4):
                            s_ = hg * 4 + hh
                            S.do("pe", "matmul", out=P[:, hh * 128:(hh + 1) * 128], lhsT=Lcur[:, s_, :],
                                 rhs=Ncur(s_), start=True, stop=True)
                        S.do("act", "activation", out=Np[(lv + 1) % 2][:, hg * 4:hg * 4 + 4, :].rr("p h s -> p (h s)"),
                             in_=P.v, func=AF.Copy)
                    if lv < 4:
                        for hg in range(2):
                            P = psum()
                            for hh in range(4):
                                s_ = hg * 4 + hh
                                S.do("pe", "matmul", out=P[:, hh * 128:(hh + 1) * 128], lhsT=Ncur(s_),
                                     rhs=Lcur[:, s_, :], start=True, stop=True)
                            S.do("act", "activation",
                                 out=Lp[(lv + 1) % 2][:, hg * 4:hg * 4 + 4, :].rr("p h s -> p (h s)"),
                                 in_=P.v, func=AF.Copy)
                Xf = X[xcur]
                if t == 0:
                    dbg("Xf", Xf.v, [128, 8, 128])
                yield
                if stage < 10:
                    return
                P = psum()
                for h in range(8):
                    i, pb, sl_ = h // 2, 64 * (h % 2), slot(h)
                    S.do("pe", "matmul", out=P[pb:pb + 64, i * 128:(i + 1) * 128], lhsT=Xf[:, sl_, 0:64],
                         rhs=AM[:, sl_, 128:256], start=True, stop=True)
                S.do("dve", "tensor_tensor", out=RhT.v, in0=P.v.rr("p (i t) -> p i t", i=4), in1=Fh[:, :, 128:256],
                     op=ALU.add)
                P = psum()
                for h in range(8):
                    i, pb, sl_ = h // 2, 64 * (h % 2), slot(h)
                    for c in range(2):
                        S.do("pe", "matmul", out=P[pb:pb + 64, (i * 2 + c) * 64:(i * 2 + c + 1) * 64],
                             lhsT=Xf[:, sl_, 0:64], rhs=Bhc[c][:, h * 64:(h + 1) * 64], start=True, stop=True)
                Pv_ = P.v.rr("p (i c j) -> p i c j", i=4, c=2)
                S.do("act", "activation", out=MpT[0:64, :, :, 0:64], in_=Pv_[0:64], func=AF.Copy)
                S.do("act", "activation", out=MpT[64:128, :, :, 64:128], in_=Pv_[64:128], func=AF.Copy)
                yield
                if stage < 11:
                    return
                hs = [(Hc[(2 * t + q) % 3], Hbd[(2 * t + q) % 3]) for q in range(3)]
                for c in range(2):
                    (hin, _), (hout, hout_bd) = hs[c], hs[c + 1]
                    P = psum()
                    first = True
                    for i in range(4):
                        for h in (2 * i, 2 * i + 1):
                            pb, sl_ = 64 * (h % 2), slot(h)
                            o_ = P[pb:pb + 64, i * 64:(i + 1) * 64]
                            S.do("pe", "matmul", out=o_, lhsT=Bhc[c][:, h * 64:(h + 1) * 64], rhs=Xf[:, sl_, 64:128],
                                 start=(i == 0), stop=False, skip_group_check=True)
                            S.do("pe", "matmul", out=o_, lhsT=Khc[c][:, h * 64:(h + 1) * 64],
                                 rhs=Vb[:, h * 64:(h + 1) * 64], start=False, stop=False, skip_group_check=True)
                        S.do("pe", "matmul", out=P[:, i * 64:(i + 1) * 64], lhsT=MpT[:, i, c, :], rhs=hin[:, i, :],
                             start=False, stop=True, skip_group_check=True)
                    S.do("dve", "tensor_tensor", out=H32.v, in0=H32.v, in1=WC[:, :, c:c + 1].bc([128, 4, 64]),
                         op=ALU.mult)
                    S.do("dve", "tensor_tensor", out=H32.v, in0=H32.v, in1=P[:, 0:256].rr("p (i v) -> p i v", i=4),
                         op=ALU.add)
                    S.do("act", "activation", out=hout.v, in_=H32.v, func=AF.Copy)
                    S.do("act", "activation", out=hout_bd[0:64, :, 0:64], in_=H32[0:64], func=AF.Copy)
                    S.do("act", "activation", out=hout_bd[64:128, :, 64:128], in_=H32[64:128], func=AF.Copy)
                yield
                if stage < 12:
                    return
                P = psum()
                first = True
                for i in range(4):
                    for h in (2 * i, 2 * i + 1):
                        sl_ = slot(h)
                        hc_ = slice(h * 64, (h + 1) * 64)
                        S.do("pe", "matmul", out=P[:, hc_], lhsT=AM[:, sl_, 128:256], rhs=Xf[:, sl_, 64:128],
                             start=first, stop=False, skip_group_check=True)
                        first = False
                        S.do("pe", "matmul", out=P[:, hc_], lhsT=AM[:, sl_, 384:512], rhs=Vb[:, hc_],
                             start=False, stop=False, skip_group_check=True)
                    for c in range(2):
                        S.do("pe", "matmul", out=P[c * 64:(c + 1) * 64, i * 128:(i + 1) * 128],
                             lhsT=RhT[:, i, c * 64:(c + 1) * 64], rhs=hs[c][1][:, i, :],
                             start=False, stop=(c == 1), skip_group_check=True)
                yr = TA[3]
                S.do("act", "activation", out=yr.v, in_=P.v, func=AF.Copy)
                if t == 0:
                    dbg("yraw", yr.v, [128, RW])
                yn = TA[4]
                head_norm(yn.v, yr.v, LN_EPS, TA[9].v)
                S.do("dve", "tensor_tensor", out=yn.v, in0=yn.v, in1=pv[3].v, op=ALU.mult)
                S.do("pool", "tensor_tensor", out=yn.v, in0=yn.v, in1=pv[4].v, op=ALU.add)
                S.do("dve", "tensor_tensor", out=hview(TA[5].v), in0=hview(zv.v), in1=bn[:, :, None].bc([128, 8, 64]),
                     op=ALU.mult)
                S.do("pool", "tensor_tensor", out=yn.v, in0=yn.v, in1=TA[5].v, op=ALU.add)
                yo = yrw[t % 2]
                S.do("dve", "tensor_tensor", out=yo.v, in0=yn.v, in1=gsb.v, op=ALU.mult)
                if t == 0:
                    dbg("yrwkv", yo.v, [128, RW])
                yield
                if stage < 12.5:
                    return
                S.dma("sp", yrs[t * 128:(t + 1) * 128, :], yo.v)

            def run_to_mid1(g):
                for tok in g:
                    if tok == "mid":
                        return
            cur = a1_tile(0) if ntiles > 0 else None
            if cur is not None:
                run_to_mid1(cur)
            for t in range(ntiles):
                nxt = a1_tile(t + 1) if t + 1 < ntiles else None
                cur_done, nxt_mid = False, nxt is None
                while not (cur_done and nxt_mid):
                    if not cur_done:
                        try:
                            next(cur)
                        except StopIteration:
                            cur_done = True
                    if not nxt_mid:
                        try:
                            if next(nxt) == "mid":
                                nxt_mid = True
                        except StopIteration:
                            nxt_mid = True
                cur = nxt
            S.barrier()
            S.emit()

        with contextlib.ExitStack() as stA:
            Wr = sb(stA, "Wr", [128, KT, 2048], BF16)
            Wo = sb(stA, "Wo", [128, KT, D], BF16)
            gng = sb(stA, "gng", [128, RW])
            DTb = sb(stA, "DTb", [128, 8, 128])
            S.dma("sp", gng.v, pvec_d[5:6, :].partition_broadcast(128))
            S.dma("sp", DTb.v, DT_d.rearrange("p (h n) -> p h n", h=8))
            for k in range(KT):
                S.dma("pool", Wr[:, k, 0:1024], win_v[:, k, 1792:2816])
                S.dma("pool", Wr[:, k, 1024:2048], win_v[:, k, 2816:3840])
                S.dma("pool", Wo[:, k, :], wout_v[:, k, :])
            NA = NormT(stA, "a2", gscA, 0)
            Zs = [[sb(stA, "Zr%d_%d" % (p_, i), [128, RW]) for i in range(4)] for p_ in range(2)]
            TAs = [[sb(stA, "TAr%d_%d" % (p_, i), [128, RW]) for i in range(6)] for p_ in range(2)]
            TBs = [[sb(stA, "TBr%d_%d" % (p_, i), [128, RW], BF16) for i in range(5)] for p_ in range(2)]
            Fhs = [sb(stA, "Fhr%d" % p_, [128, 4, 384], BF16) for p_ in range(2)]
            AMs = [sb(stA, "AMr%d" % p_, [128, 8, 128], BF16) for p_ in range(2)]
            S32 = sb(stA, "S32", [128, 4, 64])
            Sbd = [sb(stA, "Sbd%d" % i, [128, 4, 128], BF16) for i in range(3)]
            rot = [sb(stA, "rot%d" % i, [128, 4, 64]) for i in range(2)]
            ymix = [sb(stA, "ymix%d" % i, [128, D], BF16) for i in range(2)]
            ymTs = [sb(stA, "ymT%d" % p_, [128, KT, 128], BF16) for p_ in range(2)]
            S.do("pool", "memset", ap=S32.v, constant=0.0)
            for b_ in Sbd:
                S.do("pool", "memset", ap=b_.v, constant=0.0)

            def a2_tile(t):
                Z, TA, TB, Fh, AM, ymT = Zs[t % 2], TAs[t % 2], TBs[t % 2], Fhs[t % 2], AMs[t % 2], ymTs[t % 2]
                ym = ymix[t % 2]
                S.dma("sp", ym[:, 0:512], yrs[t * 128:(t + 1) * 128, :])
                rt_ = rot[t % 2]
                S.dma("sp", rt_.v, rot_d[t].rearrange("p (a d) -> p a d", a=4))
                xb, hc = NA.run(t, x_d[t * 128:(t + 1) * 128, :])
                yield
                zq, zk2, zv2, zg = Z
                proj_T(zq.v, hc, Wr, 0)
                yield
                proj_T(zk2.v, hc, Wr, 512)
                yield
                proj_T(zv2.v, hc, Wr, 1024)
                yield
                proj_T(zg.v, hc, Wr, 1536)
                yield
                qr, kr, qd, kd, Vb2 = TB

                def rotary(dst, src, ci_, si_):
                    t1, t2 = TA[0], TA[1]
                    sv = hview(src)
                    S.do("dve", "tensor_tensor", out=hview(t1.v), in0=sv, in1=rt_[:, ci_:ci_ + 1, :].bc([128, 8, 64]),
                         op=ALU.mult)
                    S.do("dve", "tensor_tensor", out=hview(t2.v)[:, :, 0:32], in0=sv[:, :, 32:64],
                         in1=rt_[:, si_:si_ + 1, 0:32].bc([128, 8, 32]), op=ALU.mult)
                    S.do("dve", "tensor_tensor", out=hview(t2.v)[:, :, 32:64], in0=sv[:, :, 0:32],
                         in1=rt_[:, si_:si_ + 1, 32:64].bc([128, 8, 32]), op=ALU.mult)
                    S.do("dve", "tensor_tensor", out=t1.v, in0=t1.v, in1=t2.v, op=ALU.add)
                    S.do("act", "activation", out=dst.v, in_=t1.v, func=AF.Copy)
                    return t1

                t1 = rotary(qr, zq.v, 0, 1)
                S.do("dve", "tensor_tensor", out=hview(qd.v), in0=hview(t1.v), in1=sm("qdec")[:, :, None].bc([128, 8, 64]),
                     op=ALU.mult)
                yield
                t1 = rotary(kr, zk2.v, 2, 3)
                S.do("dve", "tensor_tensor", out=hview(kd.v), in0=hview(t1.v), in1=sm("kdec")[:, :, None].bc([128, 8, 64]),
                     op=ALU.mult)
                S.do("act", "activation", out=Vb2.v, in_=zv2.v, func=AF.Copy)
                yield
                for i in range(4):
                    P = psum()
                    Pb = P.v.cast(BF16)
                    for j, src in enumerate((qr, kr, qd)):
                        S.do("pe", "transpose", out=Pb[:, j * 128:(j + 1) * 128], in_=src[:, i * 128:(i + 1) * 128],
                             identity=identb.v)
                    S.do("act", "activation", out=Fh[:, i, :], in_=Pb[:, 0:384], func=AF.Copy)
                yield "mid"
                for hg in range(2):
                    P = psum()
                    for hh in range(4):
                        h = hh * 2 + hg
                        i, pb = h // 2, 64 * (h % 2)
                        S.do("pe", "matmul", out=P[:, hh * 128:(hh + 1) * 128], lhsT=Fh[pb:pb + 64, i, 128:256],
                             rhs=Fh[pb:pb + 64, i, 0:128], start=True, stop=True)
                    S.do("dve", "tensor_tensor", out=AM[:, hg * 4:hg * 4 + 4, :],
                         in0=P.v.rr("p (h n) -> p h n", h=4), in1=DTb[:, hg * 4:hg * 4 + 4, :], op=ALU.mult)
                yield
                ss_ = [Sbd[(2 * t + q) % 3] for q in range(3)]
                for c in range(2):
                    cs = slice(c * 64, (c + 1) * 64)
                    sout = ss_[c + 1]
                    P = psum()
                    for h in range(8):
                        i, pb = h // 2, 64 * (h % 2)
                        S.do("pe", "matmul", out=P[pb:pb + 64, i * 64:(i + 1) * 64], lhsT=kd[cs, h * 64:(h + 1) * 64],
                             rhs=Vb2[cs, h * 64:(h + 1) * 64], start=True, stop=True)
                    S.do("dve", "tensor_tensor", out=S32.v, in0=S32.v, in1=sm("gam")[:, :, None].bc([128, 4, 64]),
                         op=ALU.mult)
                    S.do("dve", "tensor_tensor", out=S32.v, in0=S32.v, in1=P[:, 0:256].rr("p (i v) -> p i v", i=4),
                         op=ALU.add)
                    S.do("act", "activation", out=sout[0:64, :, 0:64], in_=S32[0:64], func=AF.Copy)
                    S.do("act", "activation", out=sout[64:128, :, 64:128], in_=S32[64:128], func=AF.Copy)
                P = psum()
                first = True
                for i in range(4):
                    for h in (2 * i, 2 * i + 1):
                        hc_ = slice(h * 64, (h + 1) * 64)
                        S.do("pe", "matmul", out=P[:, hc_], lhsT=AM[:, (h % 2) * 4 + h // 2, :], rhs=Vb2[:, hc_],
                             start=first, stop=False, skip_group_check=True)
                        first = False
                    for c in range(2):
                        S.do("pe", "matmul", out=P[c * 64:(c + 1) * 64, i * 128:(i + 1) * 128],
                             lhsT=Fh[:, i, 256 + c * 64:256 + (c + 1) * 64], rhs=ss_[c][:, i, :],
                             start=False, stop=(c == 1), skip_group_check=True)
                yield
                yq = TA[2]
                S.do("act", "activation", out=yq.v, in_=P.v, func=AF.Copy)
                if t == 0:
                    dbg("yret_raw", yq.v, [128, RW])
                yn2 = TA[3]
                head_norm(yn2.v, yq.v, EPS, TA[4].v)
                S.do("dve", "tensor_tensor", out=yn2.v, in0=yn2.v, in1=gng.v, op=ALU.mult)
                yield
                sg_ = TA[5]
                S.do("act", "activation", out=sg_.v, in_=zg.v, func=AF.Tanh, scale=0.5)
                S.do("dve", "scalar_tensor_tensor", out=sg_.v, in0=sg_.v, scalar=1.0, in1=zg.v, op0=ALU.add, op1=ALU.mult)
                S.do("dve", "scalar_tensor_tensor", out=ym[:, 512:1024], in0=yn2.v, scalar=0.5, in1=sg_.v,
                     op0=ALU.mult, op1=ALU.mult)
                if t == 0:
                    dbg("yret", ym[:, 512:1024], [128, RW])
                yield
                P = psum()
                Pb = P.v.cast(BF16)
                for k in range(KT):
                    S.do("pe", "transpose", out=Pb[:, k * 128:(k + 1) * 128], in_=ym[:, k * 128:(k + 1) * 128],
                         identity=identb.v)
                S.do("act", "activation", out=ymT.v.rr("p k t -> p (k t)"), in_=Pb[:, 0:1024], func=AF.Copy)
                yield
                for c2 in range(2):
                    cs_ = slice(c2 * 512, (c2 + 1) * 512)
                    P = psum()
                    for k in range(KT):
                        S.do("pe", "matmul", out=P.v, lhsT=ymT[:, k, :], rhs=Wo[:, k, cs_],
                             start=(k == 0), stop=(k == KT - 1))
                    S.do("dve", "tensor_tensor", out=TA[c2].v, in0=P.v, in1=gtA[:, cs_], op=ALU.mult)
                    S.do("pool", "tensor_tensor", out=xb[:, cs_], in0=xb[:, cs_], in1=TA[c2].v, op=ALU.add)
                S.dma("sp", x1s[t * 128:(t + 1) * 128, :], xb.v)
                if t == 0:
                    dbg("x1", xb.v, [128, D])

            def run_to_mid(g):
                for tok in g:
                    if tok == "mid":
                        return
            nt2 = ntiles if stage >= 13 else 0
            cur = a2_tile(0) if nt2 > 0 else None
            if cur is not None:
                run_to_mid(cur)
            for t in range(nt2):
                nxt = a2_tile(t + 1) if t + 1 < nt2 else None
                cur_done, nxt_mid = False, nxt is None
                while not (cur_done and nxt_mid):
                    if not cur_done:
                        try:
                            next(cur)
                        except StopIteration:
                            cur_done = True
                    if not nxt_mid:
                        try:
                            if next(nxt) == "mid":
                                nxt_mid = True
                        except StopIteration:
                            nxt_mid = True
                cur = nxt
            S.barrier()
            S.emit()

        with contextlib.ExitStack() as stB:
            Wu = sb(stB, "Wu", [128, KT, 2 * DFF], BF16)
            Wd = sb(stB, "Wd", [128, NJ, D], BF16)
            wup_v = wup_d.rearrange("(k p) c -> p k c", p=128)
            wdn_v = wdn_d.rearrange("(j p) c -> p j c", p=128)
            for k in range(KT):
                for c0 in range(0, 2 * DFF, 1408):
                    S.dma("pool", Wu[:, k, c0:c0 + 1408], wup_v[:, k, c0:c0 + 1408])
            Wu_k = lambda k, c0: Wu[:, k, c0:c0 + 128]
            fgb = sb(stB, "fgb", [128, D])
            S.dma("sp", fgb.v, fg_d.partition_broadcast(128))
            with contextlib.ExitStack() as stB0:
                gtF = sb(stB0, "gtF", [128, D])
                sc, scB = silu_c(stB0)
                wa = [sb(stB0, "wab%d" % i, [128, KT, 256]) for i in range(2)]
                bbc = [sb(stB0, "bbcb%d" % i, [128, 256]) for i in range(2)]
                for ci in range(20, 24):
                    mod_chunk(ci, wa, bbc, sc, scB, None, gtF)
                wst = [sb(stB0, "wst%d" % i, [128, D]) for i in range(2)]
                for j in range(NJ):
                    w_ = wst[j % 2]
                    S.dma("sp", w_.v, wdn_v[:, j, :])
                    S.do("dve" if j % 2 == 0 else "pool", "tensor_tensor", out=Wd[:, j, :], in0=w_.v, in1=gtF.v,
                         op=ALU.mult)
                S.barrier()
                S.emit()
            cw = sb(stB, "cw", [128, 2 * NJ, 3])
            cb = sb(stB, "cb", [128, 2 * NJ])
            S.dma("sp", cw.v, cwT_d.rearrange("p (j a) -> p j a", a=3))
            S.dma("sp", cb.v, cbT_d)
            xg = [sb(stB, "xg%d" % i, [128, D]) for i in range(2)]
            xr = [sb(stB, "xr%d" % i, [128, D]) for i in range(2)]
            xn2 = sb(stB, "xn2", [128, D])
            h2T = sb(stB, "h2T", [128, KT, 512], BF16)
            actT = sb(stB, "actT", [128, NJ, 512], BF16)
            acc = [sb(stB, "acc%d" % a, [128, 512]) for a in range(2)]
            corr = sb(stB, "corr", [128, 2 * NJ, 2])
            junkb = sb(stB, "junkb", [128, D], BF16)
            junk2 = junkb.v
            tail = sb(stB, "tail", [128, 2 * NJ, 2])
            st2 = [sb(stB, "st2_%d" % i, [128, 2]) for i in range(4)]
            S.do("pool", "memset", ap=tail.v, constant=0.0)
            nb = 0
            for g in range(ngroups):
                for tt in range(4):
                    xb = xg[nb % 2]
                    st_ = st2[nb % 2]
                    nb += 1
                    S.dma("sp", xb.v, x1s[(g * 4 + tt) * 128:(g * 4 + tt + 1) * 128, :])
                    S.do("act", "activation", out=junk2, in_=xb.v, func=AF.Square, accum_out=st_[:, 0:1])
                    S.do("dve", "tensor_scalar", out=st_[:, 0:1], in0=st_[:, 0:1], scalar1=1.0 / D, scalar2=EPS,
                         op0=ALU.mult, op1=ALU.add)
                    rsqrt_small(st_[:, 1:2], st_[:, 0:1])
                    S.do("act", "activation", out=xn2.v, in_=xb.v, func=AF.Copy, scale=st_[:, 1:2])
                    for half in range(2):
                        P = psum()
                        for kk_ in range(4):
                            k = half * 4 + kk_
                            S.do("pe", "transpose", out=P[:, kk_ * 128:(kk_ + 1) * 128],
                                 in_=xn2[:, k * 128:(k + 1) * 128], identity=sm("ident"))
                        for kk_ in range(4):
                            k = half * 4 + kk_
                            S.do("act", "activation", out=h2T[:, k, tt * 128:(tt + 1) * 128],
                                 in_=P[:, kk_ * 128:(kk_ + 1) * 128], func=AF.Identity,
                                 scale=gscF[:, k:k + 1], bias=modT[:, 24 + k:25 + k])
                if g == 0:
                    dbg("h2T", h2T.v, [128, KT, 512])
                S.do("dve", "tensor_tensor", out=corr[:, :, 0:1], in0=tail[:, :, 1:2], in1=cw[:, :, 1:2], op=ALU.mult)
                S.do("dve", "tensor_tensor", out=corr[:, :, 1:2], in0=tail[:, :, 0:1], in1=cw[:, :, 0:1], op=ALU.mult)
                S.do("dve", "tensor_tensor", out=corr[:, :, 0:1], in0=corr[:, :, 0:1], in1=corr[:, :, 1:2], op=ALU.add)
                S.do("dve", "tensor_tensor", out=corr[:, :, 1:2], in0=tail[:, :, 1:2], in1=cw[:, :, 0:1], op=ALU.mult)
                S.do("dve", "tensor_tensor", out=corr.v, in0=corr.v, in1=cb[:, :, None].bc([128, 2 * NJ, 2]), op=ALU.add)
                for j in range(NJ):
                    Pj = []
                    for a in range(2):
                        col = a * NJ + j
                        c0 = a * DFF + j * 128
                        P = psum()
                        Pj.append(P)
                        for k in range(KT):
                            S.do("pe", "matmul", out=P.v, lhsT=Wu_k(k, c0), rhs=h2T[:, k, :],
                                 start=(k == 0), stop=(k == KT - 1))
                        ac = acc[a]
                        S.do("act", "activation", out=ac[:, 2:512], in_=P[:, 2:512], func=AF.Identity,
                             scale=cw[:, col, 2:3], bias=cb[:, col:col + 1])
                        for q in range(2):
                            S.do("act", "activation", out=ac[:, q:q + 1], in_=P[:, q:q + 1], func=AF.Identity,
                                 scale=cw[:, col, 2:3], bias=corr[:, col, q:q + 1])
                        S.do("dve", "scalar_tensor_tensor", out=ac[:, 1:512], in0=P[:, 0:511], scalar=cw[:, col, 1:2],
                             in1=ac[:, 1:512], op0=ALU.mult, op1=ALU.add)
                        S.do("dve", "scalar_tensor_tensor", out=ac[:, 2:512], in0=P[:, 0:510], scalar=cw[:, col, 0:1],
                             in1=ac[:, 2:512], op0=ALU.mult, op1=ALU.add)
                        S.do("act", "activation", out=tail[:, col, :], in_=P[:, 510:512], func=AF.Copy)
                    av, ag = acc
                    Pv_, Pg_ = Pj
                    S.do("act", "activation", out=Pg_.v, in_=ag.v, func=AF.Silu)
                    S.do("dve", "tensor_tensor", out=actT[:, j, :], in0=Pg_.v, in1=av.v, op=ALU.mult)
                if g == 0:
                    dbg("actT", actT.v, [128, NJ, 512])
                for tt in range(4):
                    o_ = xr[tt % 2]
                    st_ = st2[2 + tt % 2]
                    S.dma("sp", o_.v, x1s[(g * 4 + tt) * 128:(g * 4 + tt + 1) * 128, :])
                    for c2 in range(2):
                        P = psum()
                        for j in range(NJ):
                            S.do("pe", "matmul", out=P.v, lhsT=actT[:, j, tt * 128:(tt + 1) * 128],
                                 rhs=Wd[:, j, c2 * 512:(c2 + 1) * 512], start=(j == 0), stop=(j == NJ - 1))
                        cs_ = slice(c2 * 512, (c2 + 1) * 512)
                        S.do("dve", "tensor_tensor", out=o_[:, cs_], in0=P.v, in1=o_[:, cs_], op=ALU.add)
                    S.do("act", "activation", out=junk2, in_=o_.v, func=AF.Square, accum_out=st_[:, 0:1])
                    S.do("dve", "tensor_scalar", out=st_[:, 0:1], in0=st_[:, 0:1], scalar1=1.0 / D, scalar2=EPS,
                         op0=ALU.mult, op1=ALU.add)
                    rsqrt_small(st_[:, 1:2], st_[:, 0:1])
                    S.do("dve", "scalar_tensor_tensor", out=o_.v, in0=o_.v, scalar=st_[:, 1:2], in1=fgb.v,
                         op0=ALU.mult, op1=ALU.mult)
                    S.dma("sp", out_d[(g * 4 + tt) * 128:(g * 4 + tt + 1) * 128, :], o_.v)
            S.barrier()
            S.emit()
    return nc, dbg_outs


_CACHE = {}


def make_in_maps(inp):
    f = lambda a: np.ascontiguousarray(np.asarray(a, dtype=np.float32))
    cst = _consts()
    x = f(inp["x"])
    c = f(inp["c"])
    shared = {
        "w_ada": f(inp["w_ada"][0]),
        "b_adaT": f(f(inp["b_ada"][0]).reshape(48, 128).T),
        "b_ada": f(inp["b_ada"][0]).reshape(1, -1),
        "gTa": f(f(inp["attn_norm_g"][0]).reshape(KT, 128).T),
        "gTf": f(f(inp["ffn_norm_g"][0]).reshape(KT, 128).T),
        "fg": f(inp["final_norm_g"]).reshape(1, -1),
        "w_in": f(inp["w_in"][0]),
        "mu": f(inp["rwkv_mu"][0]).reshape(1, -1),
        "w2x": f(np.concatenate([f(inp["rwkv_w2"][0]), f(inp["rwkv_w0"][0])[None, :]], axis=0)),
        "a2x": f(np.concatenate([f(inp["rwkv_a2"][0]), f(inp["rwkv_a0"][0])[None, :]], axis=0)),
        "g2": f(inp["rwkv_g2"][0]),
        "pvec": f(np.stack([f(inp["rwkv_k_k"][0]), f(inp["rwkv_k_a"][0]), f(inp["rwkv_r_k"][0]).reshape(-1),
                            f(inp["rwkv_ln_g"][0]), f(inp["rwkv_ln_b"][0]), f(inp["ret_gn_g"][0])], axis=0)),
        "w_out": f(inp["w_out"][0]),
        "w_up": f(inp["ffn_w_up"][0]),
        "cwT": f(f(inp["ffn_conv_w"][0]).reshape(3, 2 * NJ, 128).transpose(2, 1, 0).reshape(128, -1)),
        "cbT": f(f(inp["ffn_conv_b"][0]).reshape(2 * NJ, 128).T),
        "w_down": f(inp["ffn_w_down"][0]),
        "small": cst["small"], "DT": cst["DT"], "rot": cst["rot"],
    }
    maps = []
    for b in range(NCORES):
        m = dict(shared)
        m["x"] = f(x[b])
        m["cT"] = f(c[b].reshape(KT, 128).T)
        maps.append(m)
    return maps


def kernel(**inputs):
    if "nc" not in _CACHE:
        _CACHE["nc"] = build()[0]
    nc = _CACHE["nc"]
    maps = make_in_maps(inputs)
    res = run_bass_kernel_spmd(nc, maps, core_ids=list(range(NCORES)))
    out = np.stack([np.asarray(r["out"], dtype=np.float32) for r in res.results], axis=0)
    return out
```
